# Optimizing a Trainium2 kernel written in Bass

```python
import numpy as np
import jax
import jax.numpy as jnp
from jax import lax

D_MODEL = 2048
BATCH = 32
SEQ = 256
DEPTH = 1
DEC_BATCH = 8
DEC_SEQ = 2048
PAST_LEN = 512

HG_HEADS = 8
HG_DK = 128
HG_DV = 128
HG_KEY = HG_HEADS * HG_DK
HG_WIDTH = HG_HEADS * HG_DV
ML_HEADS = 8
ML_DQK = 64
ML_DV = 128
ML_QK = ML_HEADS * ML_DQK
ML_WIDTH = ML_HEADS * ML_DV
D_FF = -(-8 * D_MODEL // (3 * 256)) * 256
CHUNK = 64
EPS = 1e-6
IN_SPLITS = (HG_KEY, HG_KEY, HG_KEY, HG_WIDTH, HG_WIDTH,
             ML_QK, ML_QK, ML_WIDTH, 2 * ML_HEADS, 2 * ML_HEADS, ML_WIDTH,
             D_MODEL, D_MODEL)
IN_WIDTH = sum(IN_SPLITS)

kernel_name = 'hybrid_hgrn2_mlstm_diffusion_step'


def rms_norm(x, g):
    xf = x.astype(jnp.float32)
    y = xf * lax.rsqrt(jnp.mean(xf * xf, axis=-1, keepdims=True) + EPS)
    return (y * g.astype(jnp.float32)).astype(x.dtype)


def to_heads(a, n_heads):
    b, t, _ = a.shape
    return a.reshape(b, t, n_heads, -1).transpose(0, 2, 1, 3).astype(jnp.float32)


def from_heads(a):
    b, h, t, d = a.shape
    return a.transpose(0, 2, 1, 3).reshape(b, t, h * d)


def to_chunks(a):
    return a.reshape(a.shape[:2] + (a.shape[2] // CHUNK, CHUNK) + a.shape[3:])


def flip_t(a):
    return jnp.flip(a, axis=2)


def hgrn2_scan(q, k, v, log_f, s0):
    b_, h_, t_, dv = v.shape
    q, k, v, log_f = to_chunks(q), to_chunks(k), to_chunks(v), to_chunks(log_f)
    b = jnp.cumsum(log_f, axis=3)
    b_ref = b[:, :, :, CHUNK // 2:CHUNK // 2 + 1]
    b_last = b[:, :, :, -1:]
    causal = jnp.tril(jnp.ones((CHUNK, CHUNK), dtype=bool))
    att = jnp.einsum('bhnld,bhnsd->bhnls', q * jnp.exp(b - b_ref), k * jnp.exp(b_ref - b))
    att = jnp.where(causal, att, 0.0)
    o_intra = jnp.einsum('bhnls,bhnsv->bhnlv', att, v)
    ds = jnp.einsum('bhnsd,bhnsv->nbhdv', k * jnp.exp(b_last - b), v)
    decay = jnp.moveaxis(jnp.exp(b_last[:, :, :, 0]), 2, 0)

    def step(s, inp):
        dec, d_s = inp
        return dec[..., None] * s + d_s, s

    s_fin, s_start = lax.scan(step, s0.astype(jnp.float32), (decay, ds))
    o_inter = jnp.einsum('bhnld,nbhdv->bhnlv', q * jnp.exp(b), s_start)
    return (o_intra + o_inter).reshape(b_, h_, t_, dv), s_fin


def mlstm_scan(q, k, v, log_i, log_f, c0, n0, m0):
    b_, h_, t_, dv = v.shape
    q, k, v, log_i, log_f = (to_chunks(q), to_chunks(k), to_chunks(v),
                             to_chunks(log_i), to_chunks(log_f))
    g = jnp.cumsum(log_f, axis=-1)
    g_last = g[..., -1]
    causal = jnp.tril(jnp.ones((CHUNK, CHUNK), dtype=bool))
    log_d = jnp.where(causal, g[..., :, None] - g[..., None, :] + log_i[..., None, :], -jnp.inf)
    w_end = g_last[..., None] - g + log_i
    m_loc = jnp.max(w_end, axis=-1)
    wk = jnp.exp(w_end - m_loc[..., None])[..., None] * k
    c_loc = jnp.einsum('bhnld,bhnlv->nbhdv', wk, v)
    n_loc = jnp.moveaxis(jnp.sum(wk, axis=3), 2, 0)

    def step(carry, inp):
        c, n, m = carry
        a, ml, cl, nl = inp
        m_new = jnp.maximum(a + m, ml)
        s_old = jnp.exp(a + m - m_new)
        s_loc = jnp.exp(ml - m_new)
        c_new = s_old[..., None, None] * c + s_loc[..., None, None] * cl
        n_new = s_old[..., None] * n + s_loc[..., None] * nl
        return (c_new, n_new, m_new), (c, n, m)

    init = (c0.astype(jnp.float32), n0.astype(jnp.float32), m0.astype(jnp.float32))
    (c_fin, n_fin, m_fin), (c_st, n_st, m_st) = lax.scan(
        step, init, (jnp.moveaxis(g_last, 2, 0), jnp.moveaxis(m_loc, 2, 0), c_loc, n_loc))
    log_inter = g + jnp.moveaxis(m_st, 0, 2)[..., None]
    m_t = jnp.maximum(log_inter, jnp.max(log_d, axis=-1))
    s_inter = jnp.exp(log_inter - m_t)
    qk = jnp.einsum('bhnld,bhnsd->bhnls', q, k) * jnp.exp(log_d - m_t[..., None])
    num = (jnp.einsum('bhnls,bhnsv->bhnlv', qk, v)
           + s_inter[..., None] * jnp.einsum('bhnld,nbhdv->bhnlv', q, c_st))
    den = jnp.sum(qk, axis=-1) + s_inter * jnp.einsum('bhnld,nbhd->bhnl', q, n_st)
    h = num / jnp.maximum(jnp.abs(den), jnp.exp(-m_t))[..., None]
    return h.reshape(b_, h_, t_, dv), c_fin, n_fin, m_fin


def mixer(h, st, w_in, hg_lb, hg_norm_g, ml_b_i, ml_b_f, ml_norm_g, w_up_hg, w_up_ml, w_out):
    st_hg, st_c, st_n, st_m = st
    bsz, t_len = h.shape[0], h.shape[1]
    offsets = [int(o) for o in np.cumsum(IN_SPLITS)[:-1]]
    (hq, hf_fw, hf_bw, hi, hgate, mq, mk, mv, mi, mf, mo, ga, gb) = jnp.split(h @ w_in, offsets, axis=-1)

    q = jax.nn.silu(to_heads(hq, HG_HEADS))
    v = to_heads(hi, HG_HEADS)
    hg_out, hg_state = [], []
    for d, fz in enumerate((hf_fw, hf_bw)):
        lb = hg_lb[d].reshape(HG_HEADS, 1, HG_DK)
        z = to_heads(fz, HG_HEADS)
        log_f = jnp.log(lb + (1.0 - lb) * jax.nn.sigmoid(z))
        k = (1.0 - lb) * jax.nn.sigmoid(-z)
        if d == 0:
            o, s = hgrn2_scan(q, k, v, log_f, st_hg[:, d])
        else:
            o, s = hgrn2_scan(flip_t(q), flip_t(k), flip_t(v), flip_t(log_f), st_hg[:, d])
            o = flip_t(o)
        hg_out.append(o)
        hg_state.append(s)
    o = hg_out[0] + hg_out[1]
    o = (o * lax.rsqrt(jnp.mean(o * o, axis=-1, keepdims=True) + EPS)
         * hg_norm_g.astype(jnp.float32) * jax.nn.silu(to_heads(hgate, HG_HEADS)))
    y_hg = from_heads(o).astype(h.dtype) @ w_up_hg

    q = to_heads(mq, ML_HEADS)
    k = to_heads(mk, ML_HEADS) * (ML_DQK ** -0.5)
    v = to_heads(mv, ML_HEADS)
    log_i = (mi + ml_b_i).astype(jnp.float32).reshape(bsz, t_len, 2, ML_HEADS).transpose(2, 0, 3, 1)
    log_f = jax.nn.log_sigmoid((mf + ml_b_f).astype(jnp.float32)).reshape(
        bsz, t_len, 2, ML_HEADS).transpose(2, 0, 3, 1)
    ml_out, ml_c, ml_n, ml_m = [], [], [], []
    for d in range(2):
        if d == 0:
            o, cf, nf, mf_ = mlstm_scan(q, k, v, log_i[d], log_f[d], st_c[:, d], st_n[:, d], st_m[:, d])
        else:
            o, cf, nf, mf_ = mlstm_scan(flip_t(q), flip_t(k), flip_t(v), flip_t(log_i[d]), flip_t(log_f[d]),
                                        st_c[:, d], st_n[:, d], st_m[:, d])
            o = flip_t(o)
        ml_out.append(o)
        ml_c.append(cf)
        ml_n.append(nf)
        ml_m.append(mf_)
    o = ml_out[0] + ml_out[1]
    mu = jnp.mean(o, axis=-1, keepdims=True)
    var = jnp.mean(jnp.square(o - mu), axis=-1, keepdims=True)
    o = ((o - mu) * lax.rsqrt(var + EPS) * ml_norm_g.astype(jnp.float32).reshape(ML_HEADS, 1, ML_DV)
         * jax.nn.sigmoid(to_heads(mo, ML_HEADS)))
    y_ml = from_heads(o).astype(h.dtype) @ w_up_ml

    y = (jax.nn.sigmoid(ga) * y_hg + jax.nn.sigmoid(gb) * y_ml) @ w_out
    new_st = (jnp.stack(hg_state, axis=1), jnp.stack(ml_c, axis=1),
              jnp.stack(ml_n, axis=1), jnp.stack(ml_m, axis=1))
    return y, new_st


def layer(x, cond, st, lp):
    (w_mod, b_mod, g_pre_mix, g_post_mix, g_pre_ffn, g_post_ffn, w_in, hg_lb, hg_norm_g,
     ml_b_i, ml_b_f, ml_norm_g, w_up_hg, w_up_ml, w_out, w_ffn_in, w_ffn_out) = lp
    mod = jax.nn.silu(cond) @ w_mod + b_mod
    sh_a, sc_a, gt_a, sh_f, sc_f, gt_f = jnp.split(mod[:, None, :], 6, axis=-1)
    h = rms_norm(x, g_pre_mix) * (1.0 + sc_a) + sh_a
    y, new_st = mixer(h, st, w_in, hg_lb, hg_norm_g, ml_b_i, ml_b_f, ml_norm_g, w_up_hg, w_up_ml, w_out)
    x = x + gt_a * rms_norm(y, g_post_mix)
    h = rms_norm(x, g_pre_ffn) * (1.0 + sc_f) + sh_f
    a, b = jnp.split(h @ w_ffn_in, 2, axis=-1)
    x = x + gt_f * rms_norm((jax.nn.silu(a) * b) @ w_ffn_out, g_post_ffn)
    return x, new_st


def setup_inputs(seed: int = 0) -> dict:
    key = jax.random.key(seed)
    ks = jax.random.split(key, 26)

    def nrm(k, shape, s):
        return jax.random.normal(k, shape, jnp.float32) * s

    return {
        'x_prompt': nrm(ks[0], (BATCH, SEQ, D_MODEL), 1.0),
        'x_sample': nrm(ks[1], (DEC_BATCH, DEC_SEQ, D_MODEL), 1.0),
        'c': nrm(ks[2], (DEC_BATCH, D_MODEL), 1.0),
        'state_hgrn_s': nrm(ks[3], (DEC_BATCH, DEPTH, 2, HG_HEADS, HG_DK, HG_DV), 0.5),
        'state_mlstm_c': nrm(ks[4], (DEC_BATCH, DEPTH, 2, ML_HEADS, ML_DQK, ML_DV), 0.3),
        'state_mlstm_n': nrm(ks[5], (DEC_BATCH, DEPTH, 2, ML_HEADS, ML_DQK), 0.3),
        'state_mlstm_m': nrm(ks[6], (DEC_BATCH, DEPTH, 2, ML_HEADS), 1.0),
        'c_ctx': nrm(ks[7], (D_MODEL,), 1.0),
        'w_mod': nrm(ks[8], (DEPTH, D_MODEL, 6 * D_MODEL), 0.5 * D_MODEL ** -0.5),
        'b_mod': nrm(ks[9], (DEPTH, 6 * D_MODEL), 0.02),
        'norm_pre_mix': 1.0 + nrm(ks[10], (DEPTH, D_MODEL), 0.05),
        'norm_post_mix': 1.0 + nrm(ks[11], (DEPTH, D_MODEL), 0.05),
        'norm_pre_ffn': 1.0 + nrm(ks[12], (DEPTH, D_MODEL), 0.05),
        'norm_post_ffn': 1.0 + nrm(ks[13], (DEPTH, D_MODEL), 0.05),
        'w_in': nrm(ks[14], (DEPTH, D_MODEL, IN_WIDTH), D_MODEL ** -0.5),
        'hgrn_lb_logits': nrm(ks[15], (2, DEPTH + 1, HG_KEY), 0.1),
        'hgrn_norm_g': 1.0 + nrm(ks[16], (DEPTH, HG_DV), 0.05),
        'mlstm_b_i': nrm(ks[17], (DEPTH, 2 * ML_HEADS), 0.1),
        'mlstm_b_f': 3.0 + 3.0 * jax.random.uniform(ks[18], (DEPTH, 2 * ML_HEADS), jnp.float32),
        'mlstm_norm_g': 1.0 + nrm(ks[19], (DEPTH, ML_WIDTH), 0.05),
        'w_up_hgrn': nrm(ks[20], (DEPTH, HG_WIDTH, D_MODEL), HG_WIDTH ** -0.5),
        'w_up_mlstm': nrm(ks[21], (DEPTH, ML_WIDTH, D_MODEL), ML_WIDTH ** -0.5),
        'w_out': nrm(ks[22], (DEPTH, D_MODEL, D_MODEL), D_MODEL ** -0.5),
        'w_ffn_in': nrm(ks[23], (DEPTH, D_MODEL, 2 * D_FF), D_MODEL ** -0.5),
        'w_ffn_out': nrm(ks[24], (DEPTH, D_FF, D_MODEL), D_FF ** -0.5),
    }


def reference(x_prompt, x_sample, c, state_hgrn_s, state_mlstm_c, state_mlstm_n, state_mlstm_m,
              c_ctx, w_mod, b_mod, norm_pre_mix, norm_post_mix, norm_pre_ffn, norm_post_ffn,
              w_in, hgrn_lb_logits, hgrn_norm_g, mlstm_b_i, mlstm_b_f, mlstm_norm_g,
              w_up_hgrn, w_up_mlstm, w_out, w_ffn_in, w_ffn_out):
    hg_lb = jnp.cumsum(jax.nn.softmax(hgrn_lb_logits.astype(jnp.float32), axis=1), axis=1)
    bsz = x_prompt.shape[0]
    f32 = jnp.float32
    zero_st = (jnp.zeros((bsz, 2, HG_HEADS, HG_DK, HG_DV), f32),
               jnp.zeros((bsz, 2, ML_HEADS, ML_DQK, ML_DV), f32),
               jnp.zeros((bsz, 2, ML_HEADS, ML_DQK), f32),
               jnp.zeros((bsz, 2, ML_HEADS), f32))
    xp, xs = x_prompt, x_sample
    ctx_hg, ctx_c, ctx_n, ctx_m = [], [], [], []
    for l in range(DEPTH):
        lp = (w_mod[l], b_mod[l], norm_pre_mix[l], norm_post_mix[l], norm_pre_ffn[l], norm_post_ffn[l],
              w_in[l], hg_lb[:, l], hgrn_norm_g[l], mlstm_b_i[l], mlstm_b_f[l], mlstm_norm_g[l],
              w_up_hgrn[l], w_up_mlstm[l], w_out[l], w_ffn_in[l], w_ffn_out[l])
        xp, (s_hg, s_c, s_n, s_m) = layer(xp, c_ctx[None, :], zero_st, lp)
        ctx_hg.append(s_hg)
        ctx_c.append(s_c)
        ctx_n.append(s_n)
        ctx_m.append(s_m)
        cache_l = (state_hgrn_s[:, l], state_mlstm_c[:, l], state_mlstm_n[:, l], state_mlstm_m[:, l])
        xs, _ = layer(xs, c, cache_l, lp)
    new_hgrn_s = jnp.stack(ctx_hg, axis=1).astype(x_prompt.dtype)
    new_mlstm_c = jnp.stack(ctx_c, axis=1).astype(x_prompt.dtype)
    new_mlstm_n = jnp.stack(ctx_n, axis=1).astype(x_prompt.dtype)
    new_mlstm_m = jnp.stack(ctx_m, axis=1).astype(x_prompt.dtype)
    return (xp, xs, new_hgrn_s, new_mlstm_c, new_mlstm_n, new_mlstm_m)
```

```python
import contextlib
import numpy as np
import concourse.bass as bass
import concourse.mybir as mybir
from concourse.bass_utils import run_bass_kernel_spmd

F32 = mybir.dt.float32
BF16 = mybir.dt.bfloat16
AF = mybir.ActivationFunctionType
ALU = mybir.AluOpType
AX = mybir.AxisListType

NCORES = 8
NT = 3072
D = 2048
KT = 16
NTT = 24
NTB = 6
NGRP = 12
NCHUNK = 48
DFF = 5632
EPS = 1e-6
NEG = -1.0e30
O_HQ, O_HFF, O_HFB, O_HI, O_HG = 0, 1024, 2048, 3072, 4096
O_MQ, O_MK, O_MV, O_MI, O_MF, O_MO, O_GA, O_GB = 5120, 5632, 6144, 7168, 7184, 7200, 8224, 10272
IN_W = 12320
SEQS = [(0, 4, True, 0), (4, 4, True, 1), (8, 4, True, 2), (12, 4, True, 3), (16, 32, False, 0)]


def bc(ap, dim, n):
    l = [list(x) for x in ap.ap]
    l[dim] = [0, n]
    return bass.AP(ap.tensor, ap.offset, l)


def ins_bc(ap, dim, n):
    l = [list(x) for x in ap.ap]
    l.insert(dim, [0, n])
    return bass.AP(ap.tensor, ap.offset, l)


class Prog:
    ENGS = ("pe", "act", "dve", "pool", "sp")

    def __init__(self, nc, stack, n_dma_sems=80):
        self.nc = nc
        self.eng_sem = {e: stack.enter_context(nc.semaphore("c_" + e)) for e in self.ENGS}
        self.dma_sems = [stack.enter_context(nc.semaphore("d%d" % i)) for i in range(n_dma_sems)]
        self.eng_cnt = {e: 0 for e in self.ENGS}
        self.sem_cnt = [0] * n_dma_sems
        self.pool_keys = {}
        self.discard = False
        self._reset()
        self.total_ops = 0

    def _reset(self):
        self.ops = []
        self.last_w = {}
        self.readers = {}
        self.key_idx = {}
        self.key_cnt = {}
        self.n_sp_keys = 0

    def add(self, eng, fn, reads=(), writes=(), dma_key=None, war=()):
        if self.discard:
            return -1
        idx = len(self.ops)
        deps = set()
        for w in war:
            deps.update(self.readers.get(w, ()))
        for r in reads:
            if r in self.last_w:
                deps.add(self.last_w[r])
        for w in writes:
            if w in self.last_w:
                deps.add(self.last_w[w])
            deps.update(self.readers.get(w, ()))
        deps.discard(idx)
        for r in reads:
            self.readers.setdefault(r, []).append(idx)
        for w in writes:
            self.last_w[w] = idx
            self.readers[w] = []
        latest = {}
        pruned = set()
        for d in deps:
            dop = self.ops[d]
            if dop["dma_key"] is None:
                if dop["eng"] not in latest or latest[dop["eng"]] < d:
                    latest[dop["eng"]] = d
            else:
                pruned.add(d)
        pruned.update(latest.values())
        deps = pruned
        dwait = {}
        for d in deps:
            k = self.ops[d]["dma_key"]
            if k is not None:
                dwait[k] = self.key_cnt[k]
        if dma_key is not None:
            if dma_key not in self.key_idx:
                if eng == "pool":
                    if dma_key not in self.pool_keys:
                        self.pool_keys[dma_key] = len(self.dma_sems) - 1 - len(self.pool_keys)
                        assert len(self.pool_keys) <= 8
                    self.key_idx[dma_key] = self.pool_keys[dma_key]
                else:
                    self.n_sp_keys += 1
                    assert self.n_sp_keys <= len(self.dma_sems) - 8, "too many dma keys"
                    self.key_idx[dma_key] = self.n_sp_keys - 1
                self.key_cnt[dma_key] = self.sem_cnt[self.key_idx[dma_key]]
            self.key_cnt[dma_key] += 16
        self.ops.append(dict(eng=eng, fn=fn, deps=deps, dma_key=dma_key, idx=idx, dwait=dwait))
        return idx

    def pe(self, fn, reads=(), writes=()):
        return self.add("pe", fn, reads, writes)

    def act(self, fn, reads=(), writes=()):
        return self.add("act", fn, reads, writes)

    def dve(self, fn, reads=(), writes=()):
        return self.add("dve", fn, reads, writes)

    def pool(self, fn, reads=(), writes=()):
        return self.add("pool", fn, reads, writes)

    def dma(self, eng, fn, reads=(), writes=(), key=None, war=()):
        assert key is not None
        return self.add(eng, fn, reads, writes, dma_key=key, war=war)

    def flush(self):
        nc = self.nc
        ops = self.ops
        if not ops:
            return
        needed = set()
        for op in ops:
            for d in op["deps"]:
                dop = ops[d]
                if dop["dma_key"] is None and dop["eng"] == "pe" and op["eng"] == "pe" and op["dma_key"] is None:
                    continue
                needed.add(d)
        per_eng = {e: [op for op in ops if op["eng"] == e] for e in self.ENGS}
        for e in self.ENGS:
            comp = [op for op in per_eng[e] if op["dma_key"] is None]
            if comp:
                needed.add(comp[-1]["idx"])
        for op in ops:
            if op["dma_key"] is not None:
                ki = self.key_idx[op["dma_key"]]
                self.sem_cnt[ki] += 16
                op["sig"] = (self.dma_sems[ki], self.sem_cnt[ki], ("k", ki))
            elif op["idx"] in needed:
                self.eng_cnt[op["eng"]] += 1
                op["sig"] = (self.eng_sem[op["eng"]], self.eng_cnt[op["eng"]], ("e", op["eng"]))
            else:
                op["sig"] = None
        final_eng = dict(self.eng_cnt)
        final_keys = [(self.dma_sems[ki], self.sem_cnt[ki]) for ki in range(len(self.dma_sems)) if self.sem_cnt[ki] > 0]
        with nc.Block() as block:
            def make_body(e):
                def body(engine):
                    known = {}
                    for op in per_eng[e]:
                        waits = {}
                        for d in op["deps"]:
                            dop = ops[d]
                            if dop["sig"] is None:
                                continue
                            if dop["dma_key"] is None and dop["eng"] == "pe" and e == "pe" and op["dma_key"] is None:
                                continue
                            sem, val, sk = dop["sig"]
                            if dop["dma_key"] is not None:
                                val = op["dwait"][dop["dma_key"]]
                            if known.get(sk, 0) >= val:
                                continue
                            if sk not in waits or waits[sk][1] < val:
                                waits[sk] = (sem, val)
                        for sk, (sem, val) in waits.items():
                            engine.wait_ge(sem, val)
                            known[sk] = val
                        ins = op["fn"](engine)
                        if op["sig"] is not None:
                            ins.then_inc(op["sig"][0], 16 if op["dma_key"] is not None else 1)
                    for e2 in self.ENGS:
                        if final_eng[e2] > 0:
                            engine.wait_ge(self.eng_sem[e2], final_eng[e2])
                    for sem, val in final_keys:
                        engine.wait_ge(sem, val)
                return body
            block.tensor(make_body("pe"))
            block.scalar(make_body("act"))
            block.vector(make_body("dve"))
            block.gpsimd(make_body("pool"))
            block.sync(make_body("sp"))
        self.total_ops += len(ops)
        self._reset()


class Ring:
    def __init__(self, name, tiles):
        self.name = name
        self.tiles = tiles
        self.i = 0

    def next(self):
        k = self.i % len(self.tiles)
        self.i += 1
        return self.tiles[k], (self.name, k)


def build_nc(debug=False, stop_after=None, part=0):
    nc = bass.Bass("TRN2", target_bir_lowering=False)
    dbg_kind = "ExternalOutput" if debug else "Internal"

    def din(name, shape, dt=F32):
        return nc.dram_tensor(name, shape, dt, kind="ExternalInput").ap()

    def dout(name, shape, dt=F32):
        return nc.dram_tensor(name, shape, dt, kind="ExternalOutput").ap()

    def dscr(name, shape, dt=BF16):
        return nc.dram_tensor(name, shape, dt, kind=dbg_kind).ap()

    x_all = din("x_all", [NT, D])
    cond2 = din("cond2", [32, 128])
    st_s = din("st_s", [2, 8, 128, 128])
    st_c = din("st_c", [2, 8, 64, 128])
    st_n = din("st_n", [2, 8, 64])
    st_m = din("st_m", [1, 16])
    w_mod = din("w_mod", [D, 6 * D])
    b_mod = din("b_mod", [96, 128])
    gvec = din("gvec", [64, 128])
    w_in = din("w_in", [D, IN_W])
    lb_logits = din("lb_logits", [32, 128])
    hg_g = din("hg_g", [1, 128])
    b_if = din("b_if", [1, 32])
    ml_g = din("ml_g", [1, 1024])
    w_up_hg = din("w_up_hg", [1024, D])
    w_up_ml = din("w_up_ml", [1024, D])
    w_out = din("w_out", [D, D])
    w_ffn_in = din("w_ffn_in", [D, 2 * DFF])
    w_ffn_out = din("w_ffn_out", [DFF, D])

    y_all = dout("y_all", [NT, D])
    o_s = dout("o_s", [4, 2, 8, 128, 128])
    o_c = dout("o_c", [4, 2, 8, 64, 128])
    o_n = dout("o_n", [4, 2, 8, 64])
    o_m = dout("o_m", [4, 16])

    QF = dscr("QF", [2, 2, NGRP, 128, 8, 256])
    KF = dscr("KF", [2, 2, NGRP, 128, 8, 256])
    MQ = dscr("MQ", [512, NT])
    MKF = dscr("MKF", [512, NT])
    hkind = {0: dbg_kind, 1: "ExternalOutput", 2: "ExternalInput"}[part]
    SGA = nc.dram_tensor("SGA", [D, NT], BF16, kind=hkind).ap()
    SGB = nc.dram_tensor("SGB", [D, NT], BF16, kind=hkind).ap()
    VV = dscr("VV", [NT, 1024])
    HGATE = dscr("HGATE", [NT, 1024])
    MKT = dscr("MKT", [NT, 512])
    MVV = dscr("MVV", [NT, 1024])
    MOO = dscr("MOO", [NT, 1024])
    GATES = dscr("GATES", [NT, 32], F32)
    OFW = dscr("OFW", [NT, 2048], F32)
    OHG = nc.dram_tensor("OHG", [128, 8, NT], BF16, kind=hkind).ap()
    OML = nc.dram_tensor("OML", [128, 8, NT], BF16, kind=hkind).ap()
    X1 = dscr("X1", [NT, D], F32)

    with contextlib.ExitStack() as outer:
        def sbt(st, name, shape, dt=F32):
            return st.enter_context(nc.sbuf_tensor(name, shape, dt))

        def pst(st, name, shape, dt=F32):
            return st.enter_context(nc.psum_tensor(name, shape, dt))

        P = Prog(nc, outer)

        identf = sbt(outer, "identf", [128, 128])
        identb = sbt(outer, "identb", [128, 128], BF16)
        up01 = sbt(outer, "up01", [64, 64])
        low01 = sbt(outer, "low01", [64, 64])
        upneg = sbt(outer, "upneg", [64, 64])
        lowneg = sbt(outer, "lowneg", [64, 64])
        ones64 = sbt(outer, "ones64", [64, 64])
        onesb = sbt(outer, "onesb", [64, 2], BF16)
        epst = sbt(outer, "epst", [128, 1])
        modv = sbt(outer, "modv", [128, 6, 16, 2])
        lbv = sbt(outer, "lbv", [128, 2, 16])
        dec_all = sbt(outer, "dec_all", [128, 2, 8, NCHUNK])
        hgg_bc = sbt(outer, "hgg_bc", [64, 128])
        mlg_bc = sbt(outer, "mlg_bc", [64, 1024])
        bif_bc = sbt(outer, "bif_bc", [128, 32])

        P.pool(lambda e: e.memset(identf[:], 1.0), writes=["identf"])
        P.pool(lambda e: e.affine_select(out=identf[:], in_=identf[:], pattern=[[-1, 128]], compare_op=ALU.is_equal,
                                        fill=0.0, base=0, channel_multiplier=1), reads=["identf"], writes=["identf"])
        P.dve(lambda e: e.tensor_copy(out=identb[:], in_=identf[:]), reads=["identf"], writes=["identb"])
        for t, nm, val in ((up01, "up01", 1.0), (low01, "low01", 1.0), (upneg, "upneg", 0.0), (lowneg, "lowneg", 0.0),
                           (ones64, "ones64", 1.0), (onesb, "onesb", 1.0), (epst, "epst", EPS)):
            P.pool(lambda e, t=t, val=val: e.memset(t[:], val), writes=[nm])
        P.pool(lambda e: e.affine_select(out=up01[:], in_=up01[:], pattern=[[1, 64]], compare_op=ALU.is_ge, fill=0.0,
                                        base=0, channel_multiplier=-1), reads=["up01"], writes=["up01"])
        P.pool(lambda e: e.affine_select(out=upneg[:], in_=upneg[:], pattern=[[1, 64]], compare_op=ALU.is_ge, fill=NEG,
                                        base=0, channel_multiplier=-1), reads=["upneg"], writes=["upneg"])
        P.pool(lambda e: e.affine_select(out=low01[:], in_=low01[:], pattern=[[-1, 64]], compare_op=ALU.is_ge, fill=0.0,
                                        base=0, channel_multiplier=1), reads=["low01"], writes=["low01"])
        P.pool(lambda e: e.affine_select(out=lowneg[:], in_=lowneg[:], pattern=[[-1, 64]], compare_op=ALU.is_ge, fill=NEG,
                                        base=0, channel_multiplier=1), reads=["lowneg"], writes=["lowneg"])
        P.dma("sp", lambda e: e.dma_start(out=hgg_bc[:], in_=bass.AP(hg_g.tensor, hg_g.offset, [[0, 64], [1, 128]])),
              writes=["hgg_bc"], key="c0")
        P.dma("sp", lambda e: e.dma_start(out=mlg_bc[:], in_=bass.AP(ml_g.tensor, ml_g.offset, [[0, 64], [1, 1024]])),
              writes=["mlg_bc"], key="c0")
        P.dma("sp", lambda e: e.dma_start(out=bif_bc[:], in_=bass.AP(b_if.tensor, b_if.offset, [[0, 128], [1, 32]])),
              writes=["bif_bc"], key="c0")

        with contextlib.ExitStack() as ph:
            rows = sbt(ph, "rows", [128, 256])
            colsT = sbt(ph, "colsT", [128, 256])
            scT = sbt(ph, "scT", [128, 32], BF16)
            modT = sbt(ph, "modT", [128, 96, 2])
            wsl = [sbt(ph, "w0s%d" % i, [128, KT, 512], BF16) for i in range(3)]
            wring = Ring("wsl", wsl)
            tp0 = pst(ph, "tp0", [128, 512])
            modps = pst(ph, "modps", [128, 192])
            rows2 = sbt(ph, "rows2", [128, 128])
            P.dma("sp", lambda e: e.dma_start(out=rows[0:32, 0:128], in_=cond2), writes=["rows"], key="p0a")
            P.dma("sp", lambda e: e.dma_start(out=rows[32:128, 0:128], in_=b_mod), writes=["rows"], key="p0a")
            P.dma("sp", lambda e: e.dma_start(out=rows2[0:64, :], in_=gvec), writes=["rows2"], key="p0b")
            P.dma("sp", lambda e: e.dma_start(out=rows2[64:96, :], in_=lb_logits), writes=["rows2"], key="p0b")
            P.pe(lambda e: e.transpose(tp0[:, 0:128], rows[:, 0:128], identf[:]), reads=["rows", "identf"], writes=["tp0"])
            P.pe(lambda e: e.transpose(tp0[:, 128:224], rows2[0:96, :], identf[0:96, 0:96]), reads=["rows2", "identf"], writes=["tp0"])
            P.dve(lambda e: e.tensor_copy(out=colsT[:, 0:224], in_=tp0[:, 0:224]), reads=["tp0"], writes=["colsT"])
            P.act(lambda e: e.activation(out=scT[:], in_=tp0[:, 0:32], func=AF.Silu), reads=["tp0"], writes=["scT"])
            lg = colsT[:, 192:224].rearrange("p (d l h) -> p d l h", d=2, l=2)
            lb3 = lbv[:, 0, :].rearrange("p (d h) -> p d h", d=2)
            P.dve(lambda e: e.tensor_tensor(out=lb3, in0=lg[:, :, 0, :], in1=lg[:, :, 1, :], op=ALU.subtract),
                  reads=["colsT"], writes=["lbv"])
            P.act(lambda e: e.activation(out=lbv[:, 0, :], in_=lbv[:, 0, :], func=AF.Sigmoid), reads=["lbv"], writes=["lbv"])
            P.dve(lambda e: e.tensor_scalar(out=lbv[:, 1, :], in0=lbv[:, 0, :], scalar1=-1.0, scalar2=1.0, op0=ALU.mult,
                                            op1=ALU.add), reads=["lbv"], writes=["lbv"])
            wm = w_mod.rearrange("(kt p) n -> p kt n", p=128)
            for cb in range(24):
                wt, wres = wring.next()
                P.dma("pool", lambda e, wt=wt, cb=cb: e.dma_start(out=wt[:], in_=wm[:, :, cb * 512:(cb + 1) * 512]),
                      writes=[wres], key=wres)
                for j in range(4):
                    ct = cb * 4 + j
                    for kt in range(KT):
                        rhs = bass.AP(scT[:].tensor, scT[:, kt:kt + 1].offset, [list(scT[:].ap[0]), [16, 2]])
                        P.pe(lambda e, wt=wt, j=j, kt=kt, ct=ct, rhs=rhs: e.matmul(
                            modps[:, ct * 2:ct * 2 + 2], lhsT=wt[:, kt, j * 128:(j + 1) * 128], rhs=rhs,
                            start=(kt == 0), stop=(kt == KT - 1)), reads=[wres, "scT"], writes=["modps"])
            bmT = ins_bc(colsT[:, 32:128], 2, 2)
            P.dve(lambda e: e.tensor_tensor(out=modT[:], in0=modps[:].rearrange("p (c two) -> p c two", two=2), in1=bmT,
                                            op=ALU.add), reads=["modps", "colsT"], writes=["modT"])
            def gv(i):
                return ins_bc(colsT[:, 128 + 16 * i:128 + 16 * (i + 1)], 2, 2)
            def mch(q):
                return modT[:, 16 * q:16 * (q + 1), :]
            P.dve(lambda e: e.scalar_tensor_tensor(out=modv[:, 0], in0=mch(1), scalar=1.0, in1=gv(0), op0=ALU.add,
                                                   op1=ALU.mult), reads=["modT", "colsT"], writes=["modv0"])
            P.dve(lambda e: e.tensor_copy(out=modv[:, 1], in_=mch(0)), reads=["modT"], writes=["modv1"])
            P.dve(lambda e: e.tensor_tensor(out=modv[:, 2], in0=mch(2), in1=gv(1), op=ALU.mult), reads=["modT", "colsT"],
                  writes=["modv2"])
            P.dve(lambda e: e.scalar_tensor_tensor(out=modv[:, 3], in0=mch(4), scalar=1.0, in1=gv(2), op0=ALU.add,
                                                   op1=ALU.mult), reads=["modT", "colsT"], writes=["modv3"])
            P.dve(lambda e: e.tensor_copy(out=modv[:, 4], in_=mch(3)), reads=["modT"], writes=["modv4"])
            P.dve(lambda e: e.tensor_tensor(out=modv[:, 5], in0=mch(5), in1=gv(3), op=ALU.mult), reads=["modT", "colsT"],
                  writes=["modv5"])
            P.flush()
        if stop_after == "0":
            dbg = dout("dbg_modv", [128, 192])
            dbg2 = dout("dbg_lbv", [128, 32])
            P.dma("sp", lambda e: e.dma_start(out=dbg, in_=modv[:].rearrange("p a b c -> p (a b c)")), key="dbg")
            P.dma("sp", lambda e: e.dma_start(out=dbg2, in_=lbv[:].rearrange("p a b -> p (a b)")), key="dbg")
            P.flush()
            return nc

        P.discard = (part == 2)
        with contextlib.ExitStack() as ph:
            hT = sbt(ph, "hT", [128, KT, NT], BF16)
            wsl = [sbt(ph, "was%d" % i, [128, KT, 512], BF16) for i in range(3)]
            wring = Ring("wsl", wsl)
            mmps = Ring("mmps", [pst(ph, "mmps%d" % i, [128, 512]) for i in range(5)])
            win = w_in.rearrange("(kt p) n -> p kt n", p=128)
            with contextlib.ExitStack() as a1:
                xr = Ring("xt", [sbt(a1, "xt%d" % i, [128, D]) for i in range(2)])
                xnr = Ring("xn", [sbt(a1, "xn%d" % i, [128, D]) for i in range(2)])
                junk = sbt(a1, "junk", [128, D], BF16)
                ssr = Ring("ss", [sbt(a1, "ss%d" % i, [128, 2]) for i in range(2)])
                tpr = Ring("tpa", [pst(a1, "tpa%d" % i, [128, 512]) for i in range(2)])
                for tt in range(NTT):
                    c = 0 if tt < 8 else 1
                    xt, xres = xr.next()
                    xn, xnres = xnr.next()
                    ss, ssres = ssr.next()
                    P.dma("sp", lambda e, xt=xt, tt=tt: e.dma_start(out=xt[:], in_=x_all[tt * 128:(tt + 1) * 128, :]),
                          writes=[xres], key=xres)
                    P.act(lambda e, xt=xt, ss=ss: e.activation(out=junk[:], in_=xt[:], func=AF.Square, accum_out=ss[:, 0:1]),
                          reads=[xres], writes=["junk", ssres])
                    P.act(lambda e, ss=ss: e.activation(out=ss[:, 1:2], in_=ss[:, 0:1], func=AF.Ln, scale=1.0 / D, bias=epst[:]),
                          reads=[ssres, "epst"], writes=[ssres])
                    P.act(lambda e, ss=ss: e.activation(out=ss[:, 1:2], in_=ss[:, 1:2], func=AF.Exp, scale=-0.5),
                          reads=[ssres], writes=[ssres])
                    P.dve(lambda e, xt=xt, xn=xn, ss=ss: e.tensor_scalar(out=xn[:], in0=xt[:], scalar1=ss[:, 1:2], scalar2=None,
                                                                         op0=ALU.mult), reads=[xres, ssres], writes=[xnres])

                    for q4 in range(4):
                        tp, tpres = tpr.next()
                        for j in range(4):
                            kt = q4 * 4 + j
                            P.pe(lambda e, tp=tp, j=j, kt=kt, xn=xn: e.transpose(tp[:, j * 128:(j + 1) * 128],
                                                                               xn[:, kt * 128:(kt + 1) * 128], identf[:]),
                                 reads=[xnres, "identf"], writes=[tpres])
                        for j in range(4):
                            kt = q4 * 4 + j
                            dst = hT[:, kt, tt * 128:(tt + 1) * 128]
                            if True:
                                P.act(lambda e, tp=tp, j=j, kt=kt, dst=dst, c=c: e.activation(
                                    out=dst, in_=tp[:, j * 128:(j + 1) * 128], func=AF.Identity,
                                    scale=modv[:, 0, kt, c:c + 1], bias=modv[:, 1, kt, c:c + 1]),
                                    reads=[tpres, "modv0", "modv1"], writes=[("hT", tt, kt)])
                            else:
                                P.dve(lambda e, tp=tp, j=j, kt=kt, dst=dst, c=c: e.tensor_scalar(
                                    out=dst, in0=tp[:, j * 128:(j + 1) * 128], scalar1=modv[:, 0, kt, c:c + 1],
                                    scalar2=modv[:, 1, kt, c:c + 1], op0=ALU.mult, op1=ALU.add),
                                    reads=[tpres, "modv0", "modv1"], writes=[("hT", tt, kt)])
                if stop_after == "A1":
                    dbg = dout("dbg_hT", [128, KT * NT], BF16)
                    for kt_ in range(KT):
                        P.dma("sp", lambda e, kt_=kt_: e.dma_start(out=dbg[:, kt_ * NT:(kt_ + 1) * NT], in_=hT[:, kt_, :]),
                              reads=[("hT", tt, kt_) for tt in range(NTT)], key="dbg")
                    P.flush()
                    return nc

                P.flush()

            def load_w(cols):
                wt, wres = wring.next()
                off = 0
                first = True
                for c0, n in cols:
                    P.dma("pool", lambda e, wt=wt, c0=c0, n=n, off=off: e.dma_start(out=wt[:, :, off:off + n],
                                                                                  in_=win[:, :, c0:c0 + n]),
                          writes=[wres] if first else [], key=wres, war=() if first else ())
                    first = False
                    off += n
                return wt, wres

            def fm_mm(wt, wres, j, tb):
                ps, pres = mmps.next()
                for kt in range(KT):
                    P.pe(lambda e, ps=ps, wt=wt, j=j, kt=kt, tb=tb: e.matmul(
                        ps[:], lhsT=wt[:, kt, j * 128:(j + 1) * 128], rhs=hT[:, kt, tb * 512:(tb + 1) * 512],
                        start=(kt == 0), stop=(kt == KT - 1)),
                        reads=[wres] + [("hT", tb * 4 + i, kt) for i in range(4)], writes=[pres])
                return ps, pres

            def tm_mm(wt, wres, tt, n):
                ps, pres = mmps.next()
                for kt in range(KT):
                    P.pe(lambda e, ps=ps, wt=wt, kt=kt, tt=tt, n=n: e.matmul(
                        ps[:, 0:n], lhsT=hT[:, kt, tt * 128:(tt + 1) * 128], rhs=wt[:, kt, 0:n],
                        start=(kt == 0), stop=(kt == KT - 1)),
                        reads=[wres, ("hT", tt, kt)], writes=[pres])
                return ps, pres

            stg = Ring("stg", [sbt(ph, "stg%d" % i, [128, 512], BF16) for i in range(6)])

            def grp_dst(T, d, kind, h, tb):
                return T[d, kind, 2 * tb:2 * tb + 2, :, h, :].rearrange("g p t -> p g t")

            qTh = sbt(ph, "qTh", [128, NT])
            mask01 = sbt(ph, "mask01", [128, 512])
            P.pool(lambda e: e.memset(mask01[:], 1.0), writes=["mask01"])
            P.pool(lambda e: e.memset(mask01[:].rearrange("p (c j) -> p c j", j=64)[:, :, 0:1], 0.0), reads=["mask01"],
                   writes=["mask01"])
            NTMP = 2
            tmpn = ["sg", "f", "lf", "b", "t", "dm", "e1", "e2"]
            tmps = [{n: sbt(ph, "tmp_%s%d" % (n, i), [128, 512]) for n in tmpn} for i in range(NTMP)]
            smalls = [sbt(ph, "small%d" % i, [128, 3, 8]) for i in range(NTMP)]
            blk = 0
            for h in range(8):
                wt, wres = load_w([(O_HQ + h * 128, 128), (O_HFF + h * 128, 128), (O_HFB + h * 128, 128)])
                for tb in range(NTB):
                    ps, pres = fm_mm(wt, wres, 0, tb)
                    P.act(lambda e, ps=ps, tb=tb: e.activation(out=qTh[:, tb * 512:(tb + 1) * 512], in_=ps[:], func=AF.Silu),
                          reads=[pres], writes=[("qTh", tb)])
                for d in range(2):
                    col = d * 8 + h
                    for tb in range(NTB):
                        T = tmps[blk % NTMP]
                        sm = smalls[blk % NTMP]
                        R = lambda n, b=blk % NTMP: ("tmp", n, b)
                        blk += 1
                        ps, pres = fm_mm(wt, wres, 1 + d, tb)
                        P.act(lambda e, ps=ps, T=T: e.activation(out=T["sg"][:], in_=ps[:], func=AF.Sigmoid),
                              reads=[pres], writes=[R("sg")])
                        P.dve(lambda e, T=T, col=col: e.scalar_tensor_tensor(out=T["f"][:], in0=T["sg"][:], scalar=lbv[:, 1, col:col + 1],
                                                                             in1=bc(lbv[:, 0, col:col + 1], 1, 512), op0=ALU.mult,
                                                                             op1=ALU.add),
                              reads=[R("sg"), "lbv"], writes=[R("f")])
                        P.act(lambda e, T=T: e.activation(out=T["lf"][:], in_=T["f"][:], func=AF.Ln), reads=[R("f")],
                              writes=[R("lf")])
                        P.pool(lambda e, T=T: e.tensor_scalar(out=T["sg"][:], in0=T["f"][:], scalar1=-1.0, scalar2=1.0,
                                                              op0=ALU.mult, op1=ALU.add), reads=[R("f")], writes=[R("sg")])
                        P.dve(lambda e, T=T: e.tensor_tensor_scan(out=T["b"][:], data0=mask01[:], data1=T["lf"][:], initial=0.0,
                                                                  op0=ALU.mult, op1=ALU.add), reads=["mask01", R("lf")],
                              writes=[R("b")])
                        v3 = lambda t: t[:].rearrange("p (c j) -> p c j", j=64)
                        if d == 0:
                            src, rsrc, jref, js1, js2 = T["b"], R("b"), 32, 0, 63
                        else:
                            P.dve(lambda e, T=T: e.tensor_tensor(out=T["t"][:], in0=T["lf"][:], in1=T["b"][:], op=ALU.subtract),
                                  reads=[R("lf"), R("b")], writes=[R("t")])
                            src, rsrc, jref, js1, js2 = T["t"], R("t"), 31, 63, 0
                        P.dve(lambda e, T=T, src=src, jref=jref: e.tensor_tensor(
                            out=v3(T["dm"]), in0=v3(src), in1=bc(v3(src)[:, :, jref:jref + 1], 2, 64), op=ALU.subtract),
                            reads=[rsrc], writes=[R("dm")])
                        P.act(lambda e, T=T: e.activation(out=T["e1"][:], in_=T["dm"][:], func=AF.Exp), reads=[R("dm")],
                              writes=[R("e1")])
                        P.act(lambda e, T=T: e.activation(out=T["e2"][:], in_=T["dm"][:], func=AF.Exp, scale=-1.0),
                              reads=[R("dm")], writes=[R("e2")])
                        P.dve(lambda e, T=T, sm=sm, js1=js1: e.tensor_tensor(out=sm[:, 0, :], in0=v3(T["e2"])[:, :, js1],
                                                                           in1=v3(T["f"])[:, :, js1], op=ALU.mult),
                              reads=[R("e2"), R("f")], writes=[R("sm0")])
                        P.dve(lambda e, T=T, sm=sm, js2=js2: e.tensor_copy(out=sm[:, 1, :], in_=v3(T["e1"])[:, :, js2]),
                              reads=[R("e1")], writes=[R("sm1")])
                        P.dve(lambda e, sm=sm, d=d, h=h, tb=tb: e.tensor_tensor(out=dec_all[:, d, h, tb * 8:(tb + 1) * 8],
                                                                               in0=sm[:, 0, :], in1=sm[:, 1, :], op=ALU.mult),
                              reads=[R("sm0"), R("sm1")], writes=[("dec", d, h, tb)])
                        s1, r1 = stg.next()
                        P.dve(lambda e, T=T, s1=s1, tb=tb: e.tensor_tensor(out=s1[:], in0=qTh[:, tb * 512:(tb + 1) * 512],
                                                                          in1=T["e1"][:], op=ALU.mult),
                              reads=[("qTh", tb), R("e1")], writes=[r1])
                        P.dma("sp", lambda e, s1=s1, d=d, h=h, tb=tb: e.dma_start(
                            out=grp_dst(QF, d, 0, h, tb), in_=s1[:].rearrange("p (g t) -> p g t", g=2)), reads=[r1], key=r1)
                        P.pool(lambda e, T=T, sm=sm, tb=tb: e.tensor_tensor(
                            out=v3(T["lf"]), in0=qTh[:, tb * 512:(tb + 1) * 512].rearrange("p (c j) -> p c j", j=64),
                            in1=bc(sm[:, 0, :].rearrange("p (c o) -> p c o", o=1), 2, 64), op=ALU.mult),
                            reads=[("qTh", tb), R("sm0")], writes=[R("lf")])
                        s2, r2 = stg.next()
                        P.dve(lambda e, T=T, s2=s2: e.tensor_tensor(out=s2[:], in0=T["lf"][:], in1=T["e1"][:], op=ALU.mult),
                              reads=[R("lf"), R("e1")], writes=[r2])
                        P.dma("sp", lambda e, s2=s2, d=d, h=h, tb=tb: e.dma_start(
                            out=grp_dst(QF, d, 1, h, tb), in_=s2[:].rearrange("p (g t) -> p g t", g=2)), reads=[r2], key=r2)
                        s3, r3 = stg.next()
                        P.dve(lambda e, T=T, s3=s3: e.tensor_tensor(out=s3[:], in0=T["sg"][:], in1=T["e2"][:], op=ALU.mult),
                              reads=[R("sg"), R("e2")], writes=[r3])
                        P.dma("sp", lambda e, s3=s3, d=d, h=h, tb=tb: e.dma_start(
                            out=grp_dst(KF, d, 0, h, tb), in_=s3[:].rearrange("p (g t) -> p g t", g=2)), reads=[r3], key=r3)
                        P.pool(lambda e, T=T, sm=sm: e.tensor_tensor(
                            out=v3(T["b"]), in0=v3(T["sg"]), in1=bc(sm[:, 1, :].rearrange("p (c o) -> p c o", o=1), 2, 64),
                            op=ALU.mult), reads=[R("sg"), R("sm1")], writes=[R("b")])
                        s4, r4 = stg.next()
                        P.dve(lambda e, T=T, s4=s4: e.tensor_tensor(out=s4[:], in0=T["b"][:], in1=T["e2"][:], op=ALU.mult),
                              reads=[R("b"), R("e2")], writes=[r4])
                        P.dma("sp", lambda e, s4=s4, d=d, h=h, tb=tb: e.dma_start(
                            out=grp_dst(KF, d, 1, h, tb), in_=s4[:].rearrange("p (g t) -> p g t", g=2)), reads=[r4], key=r4)

            def fm_simple(c0, ntiles, dst, func, scale=1.0):
                for g0 in range(0, ntiles, 4):
                    wt, wres = load_w([(c0 + g0 * 128, 512)])
                    for j in range(4):
                        for tb in range(NTB):
                            ps, pres = fm_mm(wt, wres, j, tb)
                            s, r = stg.next()
                            P.act(lambda e, ps=ps, s=s: e.activation(out=s[:], in_=ps[:], func=func, scale=scale),
                                  reads=[pres], writes=[r])
                            row = (g0 + j) * 128
                            P.dma("sp", lambda e, s=s, row=row, tb=tb: e.dma_start(
                                out=dst[row:row + 128, tb * 512:(tb + 1) * 512], in_=s[:]), reads=[r], key=r)

            fm_simple(O_MQ, 4, MQ, AF.Identity)
            fm_simple(O_MK, 4, MKF, AF.Identity, scale=0.125)

            def tm_simple(c0, ncols, dst, func, scale=1.0):
                for g0 in range(0, ncols, 512):
                    wt, wres = load_w([(c0 + g0, 512)])
                    for tt in range(NTT):
                        ps, pres = tm_mm(wt, wres, tt, 512)
                        s, r = stg.next()
                        if func is None:
                            P.dve(lambda e, ps=ps, s=s: e.tensor_copy(out=s[:], in_=ps[:]), reads=[pres], writes=[r])
                        else:
                            P.act(lambda e, ps=ps, s=s: e.activation(out=s[:], in_=ps[:], func=func, scale=scale),
                                  reads=[pres], writes=[r])
                        P.dma("sp", lambda e, s=s, tt=tt, g0=g0: e.dma_start(
                            out=dst[tt * 128:(tt + 1) * 128, g0:g0 + 512], in_=s[:]), reads=[r], key=r)

            tm_simple(O_HI, 1024, VV, None)
            tm_simple(O_MV, 1024, MVV, None)
            tm_simple(O_MK, 512, MKT, AF.Identity, scale=0.125)
            tm_simple(O_HG, 1024, HGATE, AF.Silu)
            wt, wres = load_w([(O_MI, 32)])
            gst = Ring("gst", [sbt(ph, "gst%d" % i, [128, 32]) for i in range(2)])
            for tt in range(NTT):
                ps, pres = tm_mm(wt, wres, tt, 32)
                g, gr = gst.next()
                P.dve(lambda e, ps=ps, g=g: e.tensor_tensor(out=g[:], in0=ps[:, 0:32], in1=bif_bc[:], op=ALU.add),
                      reads=[pres, "bif_bc"], writes=[gr])
                P.act(lambda e, g=g: e.activation(out=g[:, 16:32], in_=g[:, 16:32], func=AF.Exp, scale=-1.0), reads=[gr],
                      writes=[gr])
                P.act(lambda e, g=g: e.activation(out=g[:, 16:32], in_=g[:, 16:32], func=AF.Ln, bias=1.0, scale=1.0),
                      reads=[gr], writes=[gr])
                P.dve(lambda e, g=g: e.tensor_scalar(out=g[:, 16:32], in0=g[:, 16:32], scalar1=-1.0, scalar2=None,
                                                     op0=ALU.mult), reads=[gr], writes=[gr])
                P.dma("sp", lambda e, g=g, tt=tt: e.dma_start(out=GATES[tt * 128:(tt + 1) * 128, :], in_=g[:]), reads=[gr],
                      key=gr)
            tm_simple(O_MO, 1024, MOO, AF.Sigmoid)
            fm_simple(O_GA, 16, SGA, AF.Sigmoid)
            fm_simple(O_GB, 16, SGB, AF.Sigmoid)
            P.flush()
        if stop_after == "A":
            return nc

        with contextlib.ExitStack() as ph:
            S = sbt(ph, "S", [128, 8, 128]); Sbf = sbt(ph, "Sbf", [128, 8, 128], BF16)
            C = sbt(ph, "C", [64, 8, 128]); Cbf = sbt(ph, "Cbf", [64, 8, 128], BF16)
            nst = sbt(ph, "nst", [64, 8]); nbf = sbt(ph, "nbf", [64, 8], BF16); mst = sbt(ph, "mst", [64, 8])
            nrow = sbt(ph, "nrow", [8, 64])
            NS = 2
            gt = []
            for i in range(NS):
                gt.append(dict(
                    qt=sbt(ph, "g_qt%d" % i, [128, 8, 256], BF16), qh=sbt(ph, "g_qh%d" % i, [128, 8, 256], BF16),
                    kt=sbt(ph, "g_kt%d" % i, [128, 8, 256], BF16), kh=sbt(ph, "g_kh%d" % i, [128, 8, 256], BF16),
                    vv=sbt(ph, "g_vv%d" % i, [64, 4, 1024], BF16), mvv=sbt(ph, "g_mvv%d" % i, [64, 4, 1024], BF16),
                    mkt=sbt(ph, "g_mkt%d" % i, [64, 4, 512], BF16), gat=sbt(ph, "g_gat%d" % i, [64, 4, 32]),
                    mq=sbt(ph, "g_mq%d" % i, [64, 8, 256], BF16), mkf=sbt(ph, "g_mkf%d" % i, [64, 8, 256], BF16)))
            cext = [dict(ofw=sbt(ph, "c_ofw%d" % i, [64, 2048]), hgate=sbt(ph, "c_hg%d" % i, [64, 1024], BF16),
                         moo=sbt(ph, "c_mo%d" % i, [64, 1024], BF16)) for i in range(1)]
            osbr = Ring("osb", [sbt(ph, "osb%d" % i, [64, 2048]) for i in range(2)])
            att_sb = sbt(ph, "att_sb", [64, 8, 64], BF16)
            khat_sb = sbt(ph, "khat_sb", [64, 1024], BF16)
            diag = sbt(ph, "diag", [64, 8, 64])
            masked = sbt(ph, "masked", [64, 8, 64])
            t1 = masked; Et = masked
            t2 = sbt(ph, "t2", [64, 8, 64]); Sint = t2
            PT = sbt(ph, "PT", [64, 8, 64], BF16); qTs = sbt(ph, "qTs", [64, 8, 64], BF16); wk = sbt(ph, "wk", [64, 8, 64], BF16)
            tC = sbt(ph, "tC", [64, 8, 128])
            sm = sbt(ph, "smB", [64, 16, 8])
            gsb = sbt(ph, "gsb", [64, 16])
            osum = sbt(ph, "osum", [64, 2048]); sqb = sbt(ph, "sqb", [64, 1024]); ohn = sbt(ph, "ohn", [64, 1024])
            ohb = sbt(ph, "ohb", [64, 2, 1024], BF16)
            stT = [[sbt(ph, "stT%d_%d" % (b, i), [128, 8, 256], BF16) for i in range(1)] for b in range(2)]
            psA = pst(ph, "psA", [128, 512]); psB = pst(ph, "psB", [128, 512]); psO = pst(ph, "psO", [128, 1024])
            psD = pst(ph, "psD", [128, 1024]); psU = pst(ph, "psU", [128, 512]); psS = pst(ph, "psS", [128, 512])
            psBb = psB[:].bitcast(BF16)
            SMN = ["u", "cmax", "cm", "M", "wexp", "mt", "ex", "aden", "rd", "am", "ml", "mnew", "d12a", "d12b", "so", "sl"]
            smv = {n: sm[:, i, :] for i, n in enumerate(SMN)}
            d12 = sm[:, 12:14, :]
            sosl = sm[:, 14:16, :]
            id64 = identf[0:64, 0:64]
            MQr = MQ.rearrange("(h d) t -> d h t", d=64)
            MKFr = MKF.rearrange("(h d) t -> d h t", d=64)
            gset_i = [0]
            cset_i = [0]
            stT_i = [0]

            def load_group(gi, d):
                k = gset_i[0] % NS
                gset_i[0] += 1
                G = gt[k]
                key = ("gset", k)
                tl = slice(gi * 256, (gi + 1) * 256)
                names = list(G.keys())
                srcs = dict(qt=QF[d, 0, gi], qh=QF[d, 1, gi], kt=KF[d, 0, gi], kh=KF[d, 1, gi],
                            vv=VV[tl, :].rearrange("(c p) n -> p c n", p=64), mvv=MVV[tl, :].rearrange("(c p) n -> p c n", p=64),
                            mkt=MKT[tl, :].rearrange("(c p) n -> p c n", p=64), gat=GATES[tl, :].rearrange("(c p) n -> p c n", p=64),
                            mq=MQr[:, :, tl], mkf=MKFr[:, :, tl])
                allres = [("g", n, k) for n in names]
                for i, n in enumerate(names):
                    P.dma("sp", lambda e, dst=G[n], src=srcs[n]: e.dma_start(out=dst[:], in_=src),
                          writes=[("g", n, k)], key=key, war=allres if i == 0 else ())
                return G, k

            def chunk_step(G, k, ck, d, last_of_pass, seq, bwd_out):
                ci = ck % 4
                cs = slice(ci * 64, (ci + 1) * 64)
                gr = lambda n: ("g", n, k)
                attmask = up01 if d == 0 else low01
                tri = up01 if d == 0 else low01
                cummask = lowneg if d == 0 else upneg
                emask = upneg if d == 0 else lowneg
                rn = lambda n: "up01" if n is up01 else "low01" if n is low01 else "upneg" if n is upneg else "lowneg"
                osb, osr = osbr.next()
                for h in range(8):
                    P.pe(lambda e, h=h: e.matmul(psA[0:64, h * 64:(h + 1) * 64], lhsT=G["kt"][:, h, cs], rhs=G["qt"][:, h, cs],
                                                 start=True, stop=True), reads=[gr("kt"), gr("qt")], writes=["psA"])
                P.dve(lambda e: e.tensor_tensor(out=att_sb[:], in0=psA[0:64, :].rearrange("p (h l) -> p h l", h=8),
                                                in1=ins_bc(attmask[:], 1, 8), op=ALU.mult), reads=["psA", rn(attmask)],
                      writes=["att_sb"])
                for h in range(8):
                    P.pe(lambda e, h=h: e.transpose(psBb[0:64, h * 128:(h + 1) * 128], G["kh"][:, h, cs], identb[:]),
                         reads=[gr("kh"), "identb"], writes=["psB"])
                P.act(lambda e: e.copy(out=khat_sb[:], in_=psBb[0:64, :]), reads=["psB"], writes=["khat_sb"])
                for h in range(8):
                    hs = slice(h * 128, (h + 1) * 128)
                    P.pe(lambda e, h=h, hs=hs: e.matmul(psO[0:64, hs], lhsT=att_sb[:, h, :], rhs=G["vv"][:, ci, hs], start=True,
                                                        stop=False), reads=["att_sb", gr("vv")], writes=["psO"])
                    P.pe(lambda e, h=h, hs=hs: e.matmul(psO[0:64, hs], lhsT=G["qh"][:, h, cs], rhs=Sbf[:, h, :], start=False,
                                                        stop=True), reads=[gr("qh"), "Sbf"], writes=["psO"])
                P.act(lambda e: e.copy(out=osb[:, 0:1024], in_=psO[0:64, :]), reads=["psO"], writes=[(osr, 0)])
                for h in range(8):
                    hs = slice(h * 128, (h + 1) * 128)
                    P.pe(lambda e, hs=hs: e.matmul(psD[:, hs], lhsT=khat_sb[:, hs], rhs=G["vv"][:, ci, hs], start=True, stop=True),
                         reads=["khat_sb", gr("vv")], writes=["psD"])
                decv = bc(dec_all[:, d, :, ck:ck + 1], 2, 128)
                P.dve(lambda e: e.tensor_tensor(out=S[:], in0=S[:], in1=decv, op=ALU.mult), reads=["S"] + [("dec", d, h, ck // 8) for h in range(8)],
                      writes=["S"])
                P.dve(lambda e: e.tensor_tensor(out=S[:], in0=S[:], in1=psD[:].rearrange("p (h v) -> p h v", h=8), op=ALU.add),
                      reads=["S", "psD"], writes=["S"])
                P.pool(lambda e: e.tensor_copy(out=Sbf[:], in_=S[:]), reads=["S"], writes=["Sbf"])
                li = G["gat"][:, ci, d * 8:(d + 1) * 8]
                lfm = G["gat"][:, ci, 16 + d * 8:16 + (d + 1) * 8]
                P.pe(lambda e: e.matmul(psS[0:64, 0:8], lhsT=tri[:], rhs=lfm, start=True, stop=True), reads=[rn(tri), gr("gat")],
                     writes=["psS"])
                P.pe(lambda e: e.matmul(psS[0:64, 8:16], lhsT=ones64[:], rhs=lfm, start=True, stop=True), reads=["ones64", gr("gat")],
                     writes=["psS"])
                P.dve(lambda e: e.tensor_tensor(out=smv["u"], in0=li, in1=psS[0:64, 0:8], op=ALU.subtract), reads=[gr("gat"), "psS"],
                      writes=["u"])
                P.dve(lambda e: e.tensor_copy(out=gsb[:], in_=psS[0:64, 0:16]), reads=["psS"], writes=["gsb"])
                v1 = lambda a: a.rearrange("p (h o) -> p h o", o=1)
                P.pool(lambda e: e.tensor_tensor(out=diag[:], in0=ins_bc(id64, 1, 8), in1=bc(v1(smv["u"]), 2, 64), op=ALU.mult),
                       reads=["identf", "u"], writes=["diag"])
                P.pe(lambda e: e.matmul(psU[0:64, :], lhsT=ones64[:], rhs=diag[:].rearrange("p h s -> p (h s)"), start=True, stop=True),
                     reads=["ones64", "diag"], writes=["psU"])
                psU3 = psU[0:64, :].rearrange("p (h s) -> p h s", h=8)
                P.dve(lambda e: e.tensor_reduce(out=smv["cmax"], in_=psU3, axis=AX.X, op=ALU.max), reads=["psU"], writes=["cmax"])
                P.dve(lambda e: e.tensor_tensor(out=masked[:], in0=psU3, in1=ins_bc(cummask[:], 1, 8), op=ALU.add),
                      reads=["psU", rn(cummask)], writes=["masked"])
                P.dve(lambda e: e.tensor_reduce(out=smv["cm"], in_=masked[:], axis=AX.X, op=ALU.max), reads=["masked"], writes=["cm"])
                P.dve(lambda e: e.tensor_tensor(out=smv["M"], in0=smv["cm"], in1=mst[:], op=ALU.max), reads=["cm", "mst"], writes=["M"])
                P.pool(lambda e: e.tensor_tensor(out=diag[:], in0=ins_bc(id64, 1, 8), in1=bc(v1(smv["M"]), 2, 64), op=ALU.mult),
                       reads=["identf", "M", "diag"], writes=["diag"])
                P.pe(lambda e: e.matmul(psU[0:64, :], lhsT=ones64[:], rhs=diag[:].rearrange("p h s -> p (h s)"), start=True, stop=True),
                     reads=["ones64", "diag"], writes=["psU"])
                P.dve(lambda e: e.tensor_tensor(out=t1[:], in0=bc(v1(smv["u"]), 2, 64), in1=psU3, op=ALU.subtract), reads=["u", "psU", "masked"],
                      writes=["masked"])
                P.dve(lambda e: e.tensor_tensor(out=t1[:], in0=t1[:], in1=ins_bc(emask[:], 1, 8), op=ALU.add), reads=["masked", rn(emask)],
                      writes=["masked"])
                P.act(lambda e: e.activation(out=Et[:], in_=t1[:], func=AF.Exp), reads=["masked"], writes=["masked"])
                P.dve(lambda e: e.tensor_tensor(out=t2[:], in0=bc(v1(mst[:]), 2, 64), in1=psU3, op=ALU.subtract), reads=["mst", "psU"],
                      writes=["t2"])
                P.act(lambda e: e.activation(out=Sint[:], in_=t2[:], func=AF.Exp), reads=["t2"], writes=["t2"])
                for h in range(8):
                    P.pe(lambda e, h=h: e.matmul(psA[0:64, h * 64:(h + 1) * 64], lhsT=G["mkf"][:, h, cs], rhs=G["mq"][:, h, cs],
                                                 start=True, stop=True), reads=[gr("mkf"), gr("mq")], writes=["psA"])
                P.dve(lambda e: e.tensor_tensor(out=PT[:], in0=psA[0:64, :].rearrange("p (h l) -> p h l", h=8), in1=Et[:], op=ALU.mult),
                      reads=["psA", "masked"], writes=["PT"])
                P.dve(lambda e: e.tensor_tensor(out=qTs[:], in0=G["mq"][:, :, cs], in1=Sint[:], op=ALU.mult), reads=[gr("mq"), "t2"],
                      writes=["qTs"])
                P.dve(lambda e: e.tensor_tensor(out=smv["wexp"], in0=smv["u"], in1=smv["cmax"], op=ALU.subtract), reads=["u", "cmax"],
                      writes=["wexp"])
                P.act(lambda e: e.activation(out=smv["wexp"], in_=smv["wexp"], func=AF.Exp), reads=["wexp"], writes=["wexp"])
                P.dve(lambda e: e.tensor_tensor(out=wk[:], in0=G["mkt"][:, ci, :].rearrange("p (h x) -> p h x", h=8),
                                                in1=bc(v1(smv["wexp"]), 2, 64), op=ALU.mult), reads=[gr("mkt"), "wexp"], writes=["wk"])
                for h in range(8):
                    hs = slice(h * 128, (h + 1) * 128)
                    P.pe(lambda e, h=h, hs=hs: e.matmul(psO[0:64, hs], lhsT=PT[:, h, :], rhs=G["mvv"][:, ci, hs], start=True, stop=False),
                         reads=["PT", gr("mvv")], writes=["psO"])
                    P.pe(lambda e, h=h, hs=hs: e.matmul(psO[0:64, hs], lhsT=qTs[:, h, :], rhs=Cbf[:, h, :], start=False, stop=True),
                         reads=["qTs", "Cbf"], writes=["psO"])
                    P.pe(lambda e, h=h: e.matmul(psS[0:64, 16 + h:17 + h], lhsT=PT[:, h, :], rhs=onesb[:, 0:1], start=True, stop=False),
                         reads=["PT", "onesb"], writes=["psS"])
                    P.pe(lambda e, h=h: e.matmul(psS[0:64, 16 + h:17 + h], lhsT=qTs[:, h, :], rhs=nbf[:, h:h + 1], start=False, stop=True),
                         reads=["qTs", "nbf"], writes=["psS"])
                for h in range(8):
                    hs = slice(h * 128, (h + 1) * 128)
                    P.pe(lambda e, h=h, hs=hs: e.matmul(psD[0:64, hs], lhsT=wk[:, h, :], rhs=G["mvv"][:, ci, hs], start=True, stop=True),
                         reads=["wk", gr("mvv")], writes=["psD"])
                    P.pe(lambda e, h=h: e.matmul(psS[0:64, 24 + h:25 + h], lhsT=wk[:, h, :], rhs=onesb[:, 0:1], start=True, stop=True),
                         reads=["wk", "onesb"], writes=["psS"])
                P.dve(lambda e: e.tensor_tensor(out=smv["mt"], in0=gsb[:, 0:8], in1=smv["M"], op=ALU.add), reads=["gsb", "M"], writes=["mt"])
                P.act(lambda e: e.activation(out=smv["ex"], in_=smv["mt"], func=AF.Exp, scale=-1.0), reads=["mt"], writes=["ex"])
                P.act(lambda e: e.activation(out=smv["aden"], in_=psS[0:64, 16:24], func=AF.Abs), reads=["psS"], writes=["aden"])
                P.dve(lambda e: e.tensor_tensor(out=smv["rd"], in0=smv["aden"], in1=smv["ex"], op=ALU.max), reads=["aden", "ex"], writes=["rd"])
                P.dve(lambda e: e.reciprocal(out=smv["rd"], in_=smv["rd"]), reads=["rd"], writes=["rd"])
                P.dve(lambda e: e.tensor_tensor(out=osb[:, 1024:2048].rearrange("p (h v) -> p h v", h=8),
                                                in0=psO[0:64, :].rearrange("p (h v) -> p h v", h=8), in1=bc(v1(smv["rd"]), 2, 128),
                                                op=ALU.mult), reads=["psO", "rd"], writes=[(osr, 1)])
                P.dve(lambda e: e.tensor_tensor(out=smv["am"], in0=gsb[:, 8:16], in1=mst[:], op=ALU.add), reads=["gsb", "mst"], writes=["am"])
                P.dve(lambda e: e.tensor_tensor(out=smv["ml"], in0=gsb[:, 8:16], in1=smv["cmax"], op=ALU.add), reads=["gsb", "cmax"],
                      writes=["ml"])
                P.dve(lambda e: e.tensor_tensor(out=smv["mnew"], in0=smv["am"], in1=smv["ml"], op=ALU.max), reads=["am", "ml"],
                      writes=["mnew"])
                P.dve(lambda e: e.tensor_tensor(out=smv["d12a"], in0=smv["am"], in1=smv["mnew"], op=ALU.subtract), reads=["am", "mnew"],
                      writes=["d12a"])
                P.dve(lambda e: e.tensor_tensor(out=smv["d12b"], in0=smv["ml"], in1=smv["mnew"], op=ALU.subtract), reads=["ml", "mnew"],
                      writes=["d12b"])
                P.act(lambda e: e.activation(out=sosl, in_=d12, func=AF.Exp), reads=["d12a", "d12b"], writes=["sosl"])
                P.dve(lambda e: e.tensor_copy(out=mst[:], in_=smv["mnew"]), reads=["mnew"], writes=["mst"])
                P.dve(lambda e: e.tensor_tensor(out=tC[:], in0=psD[0:64, :].rearrange("p (h v) -> p h v", h=8),
                                                in1=bc(v1(smv["sl"]), 2, 128), op=ALU.mult), reads=["psD", "sosl"], writes=["tC"])
                P.dve(lambda e: e.tensor_tensor(out=C[:], in0=C[:], in1=bc(v1(smv["so"]), 2, 128), op=ALU.mult), reads=["C", "sosl"],
                      writes=["C"])
                P.dve(lambda e: e.tensor_tensor(out=C[:], in0=C[:], in1=tC[:], op=ALU.add), reads=["C", "tC"], writes=["C"])
                P.pool(lambda e: e.tensor_copy(out=Cbf[:], in_=C[:]), reads=["C"], writes=["Cbf"])
                P.dve(lambda e: e.tensor_tensor(out=nst[:], in0=nst[:], in1=smv["so"], op=ALU.mult), reads=["nst", "sosl"], writes=["nst"])
                P.dve(lambda e: e.tensor_tensor(out=smv["aden"], in0=psS[0:64, 24:32], in1=smv["sl"], op=ALU.mult),
                      reads=["psS", "sosl", "aden"], writes=["aden"])
                P.dve(lambda e: e.tensor_tensor(out=nst[:], in0=nst[:], in1=smv["aden"], op=ALU.add), reads=["nst", "aden"], writes=["nst"])
                P.pool(lambda e: e.tensor_copy(out=nbf[:], in_=nst[:]), reads=["nst"], writes=["nbf"])
                tok0 = ck * 64
                if not bwd_out:
                    P.dma("sp", lambda e: e.dma_start(out=OFW[tok0:tok0 + 64, :], in_=osb[:]), reads=[(osr, 0), (osr, 1)],
                          writes=[("OFW", ck)], key=osr)
                    return
                X = cext[0]
                xk = ("cext", 0)
                cset_i[0] += 1
                P.dma("sp", lambda e: e.dma_start(out=X["ofw"][:], in_=OFW[tok0:tok0 + 64, :]), reads=[("OFW", ck)], writes=[(xk, "ofw")],
                      key=xk, war=[(xk, "hgate"), (xk, "moo")])
                P.dma("sp", lambda e: e.dma_start(out=X["hgate"][:], in_=HGATE[tok0:tok0 + 64, :]), writes=[(xk, "hgate")], key=xk)
                P.dma("sp", lambda e: e.dma_start(out=X["moo"][:], in_=MOO[tok0:tok0 + 64, :]), writes=[(xk, "moo")], key=xk)
                P.dve(lambda e: e.tensor_tensor(out=osum[:], in0=osb[:], in1=X["ofw"][:], op=ALU.add), reads=[(osr, 0), (osr, 1), (xk, "ofw")],
                      writes=["osum"])
                o3 = lambda a: a.rearrange("p (h v) -> p h v", h=8)
                P.pool(lambda e: e.tensor_tensor(out=sqb[:], in0=osum[:, 0:1024], in1=osum[:, 0:1024], op=ALU.mult), reads=["osum"],
                       writes=["sqb"])
                P.dve(lambda e: e.tensor_reduce(out=smv["am"], in_=o3(sqb[:]), axis=AX.X, op=ALU.add), reads=["sqb"], writes=["am"])
                P.act(lambda e: e.activation(out=smv["am"], in_=smv["am"], func=AF.Ln, scale=1.0 / 128, bias=epst[0:64, :]), reads=["am", "epst"],
                      writes=["am"])
                P.act(lambda e: e.activation(out=smv["am"], in_=smv["am"], func=AF.Exp, scale=-0.5), reads=["am"], writes=["am"])
                P.dve(lambda e: e.tensor_tensor(out=o3(ohn[:]), in0=o3(osum[:, 0:1024]), in1=bc(v1(smv["am"]), 2, 128), op=ALU.mult),
                      reads=["osum", "am"], writes=["ohn"])
                P.dve(lambda e: e.tensor_tensor(out=o3(ohn[:]), in0=o3(ohn[:]), in1=ins_bc(hgg_bc[:], 1, 8), op=ALU.mult),
                      reads=["ohn", "hgg_bc"], writes=["ohn"])
                P.dve(lambda e: e.tensor_tensor(out=ohb[:, 0, :], in0=ohn[:], in1=X["hgate"][:], op=ALU.mult), reads=["ohn", (xk, "hgate")],
                      writes=["ohb0"])
                P.dve(lambda e: e.tensor_reduce(out=smv["ml"], in_=o3(osum[:, 1024:2048]), axis=AX.X, op=ALU.add), reads=["osum"],
                      writes=["ml"])
                P.dve(lambda e: e.tensor_scalar(out=smv["ml"], in0=smv["ml"], scalar1=1.0 / 128, scalar2=None, op0=ALU.mult), reads=["ml"],
                      writes=["ml"])
                P.dve(lambda e: e.tensor_tensor(out=o3(ohn[:]), in0=o3(osum[:, 1024:2048]), in1=bc(v1(smv["ml"]), 2, 128), op=ALU.subtract),
                      reads=["osum", "ml", "ohn"], writes=["ohn"])
                P.pool(lambda e: e.tensor_tensor(out=sqb[:], in0=ohn[:], in1=ohn[:], op=ALU.mult), reads=["ohn", "sqb"], writes=["sqb"])
                P.dve(lambda e: e.tensor_reduce(out=smv["mnew"], in_=o3(sqb[:]), axis=AX.X, op=ALU.add), reads=["sqb"], writes=["mnew"])
                P.act(lambda e: e.activation(out=smv["mnew"], in_=smv["mnew"], func=AF.Ln, scale=1.0 / 128, bias=epst[0:64, :]),
                      reads=["mnew", "epst"], writes=["mnew"])
                P.act(lambda e: e.activation(out=smv["mnew"], in_=smv["mnew"], func=AF.Exp, scale=-0.5), reads=["mnew"], writes=["mnew"])
                P.dve(lambda e: e.tensor_tensor(out=o3(ohn[:]), in0=o3(ohn[:]), in1=bc(v1(smv["mnew"]), 2, 128), op=ALU.mult),
                      reads=["ohn", "mnew"], writes=["ohn"])
                P.dve(lambda e: e.tensor_tensor(out=ohn[:], in0=ohn[:], in1=mlg_bc[:], op=ALU.mult), reads=["ohn", "mlg_bc"], writes=["ohn"])
                P.dve(lambda e: e.tensor_tensor(out=ohb[:, 1, :], in0=ohn[:], in1=X["moo"][:], op=ALU.mult), reads=["ohn", (xk, "moo")],
                      writes=["ohb1"])
                si = 0
                for b, dst in ((0, OHG), (1, OML)):
                    psT = psBb[:, 0:512].rearrange("p (h t) -> p h t", h=8)
                    for h in range(8):
                        P.pe(lambda e, h=h, b=b: e.transpose(psBb[:, h * 64:(h + 1) * 64], ohb[:, b, h * 128:(h + 1) * 128], identb[0:64, 0:64]),
                             reads=["ohb%d" % b, "identb"], writes=["psB"])
                    P.act(lambda e, b=b, psT=psT: e.copy(out=stT[b][si][:, :, cs], in_=psT), reads=["psB"], writes=[("stT", b, si)])
                    if last_of_pass:
                        g = ck // 4
                        P.dma("sp", lambda e, b=b, dst=dst, g=g: e.dma_start(out=dst[:, :, g * 256:(g + 1) * 256], in_=stT[b][si][:]),
                              reads=[("stT", b, si)], key=("stT", b, si))
                if last_of_pass:
                    stT_i[0] += 1

            for (c0, nchk, is_prompt, li_) in SEQS:
                for d in range(2):
                    if is_prompt:
                        P.pool(lambda e: e.memset(S[:], 0.0), writes=["S"])
                        P.pool(lambda e: e.memset(Sbf[:], 0.0), writes=["Sbf"])
                        P.pool(lambda e: e.memset(C[:], 0.0), writes=["C"])
                        P.pool(lambda e: e.memset(Cbf[:], 0.0), writes=["Cbf"])
                        P.pool(lambda e: e.memset(nst[:], 0.0), writes=["nst"])
                        P.pool(lambda e: e.memset(nbf[:], 0.0), writes=["nbf"])
                        P.pool(lambda e: e.memset(mst[:], 0.0), writes=["mst"])
                    else:
                        P.dma("sp", lambda e, d=d: e.dma_start(out=S[:], in_=st_s[d].rearrange("h k v -> k h v")), writes=["S"], key="st0")
                        P.dma("sp", lambda e, d=d: e.dma_start(out=C[:], in_=st_c[d].rearrange("h k v -> k h v")), writes=["C"], key="st0")
                        P.dma("sp", lambda e, d=d: e.dma_start(out=nrow[:], in_=st_n[d]), writes=["nrow"], key="st0")
                        P.dma("sp", lambda e, d=d: e.dma_start(out=mst[:], in_=bass.AP(st_m.tensor, st_m.offset + d * 8, [[0, 64], [1, 8]])),
                              writes=["mst"], key="st0")
                        P.pe(lambda e: e.transpose(psS[0:64, 32:40], nrow[:], identf[0:8, 0:8]), reads=["nrow", "identf"], writes=["psS"])
                        P.dve(lambda e: e.tensor_copy(out=nst[:], in_=psS[0:64, 32:40]), reads=["psS"], writes=["nst"])
                        P.pool(lambda e: e.tensor_copy(out=Sbf[:], in_=S[:]), reads=["S"], writes=["Sbf"])
                        P.pool(lambda e: e.tensor_copy(out=Cbf[:], in_=C[:]), reads=["C"], writes=["Cbf"])
                        P.pool(lambda e: e.tensor_copy(out=nbf[:], in_=nst[:]), reads=["nst"], writes=["nbf"])
                    order = list(range(c0, c0 + nchk)) if d == 0 else list(range(c0 + nchk - 1, c0 - 1, -1))
                    G = None
                    cur_g = None
                    for i, ck in enumerate(order):
                        if ck // 4 != cur_g:
                            cur_g = ck // 4
                            G, k = load_group(cur_g, d)
                        last_in_grp = (i == len(order) - 1) or (order[i + 1] // 4 != cur_g)
                        chunk_step(G, k, ck, d, last_in_grp, li_, d == 1)
                    if is_prompt:
                        b = li_
                        P.dma("sp", lambda e, b=b, d=d: e.dma_start(out=o_s[b, d].rearrange("h k v -> k h v"), in_=S[:]), reads=["S"],
                              key="fin")
                        P.dma("sp", lambda e, b=b, d=d: e.dma_start(out=o_c[b, d].rearrange("h k v -> k h v"), in_=C[:]), reads=["C"],
                              key="fin")
                        P.pe(lambda e: e.transpose(psS[0:8, 64:128], nst[:], id64), reads=["nst", "identf"], writes=["psS"])
                        P.dve(lambda e: e.tensor_copy(out=nrow[:], in_=psS[0:8, 64:128]), reads=["psS"], writes=["nrow"])
                        P.dma("sp", lambda e, b=b, d=d: e.dma_start(out=o_n[b, d], in_=nrow[:]), reads=["nrow"], key="fin")
                        P.dma("sp", lambda e, b=b, d=d: e.dma_start(out=o_m[b:b + 1, d * 8:(d + 1) * 8], in_=mst[0:1, :]), reads=["mst"],
                              key="fin")
            P.flush()
        if stop_after == "B" or part == 1:
            return nc
        P.discard = False

        with contextlib.ExitStack() as ph:
            actA = sbt(ph, "actA", [128, 16, 512], BF16)
            big = sbt(ph, "big", [128, 44, 512], BF16)
            h2T = sbt(ph, "h2T", [128, 16, 512], BF16)
            hbr = Ring("hb", [sbt(ph, "hb%d" % i, [128, D], BF16) for i in range(2)])
            wsl = [sbt(ph, "wcs%d" % i, [128, KT, 512], BF16) for i in range(2)]
            wring = Ring("wsl", wsl)
            sgr = Ring("sg", [sbt(ph, "sgt%d" % i, [128, 2, 512], BF16) for i in range(2)])
            xr = Ring("xc", [sbt(ph, "xc%d" % i, [128, D]) for i in range(2)])
            yblk = sbt(ph, "yblk", [128, 4, D])
            Gbc = sbt(ph, "Gbc", [128, D])
            tmpb = Ring("tmpb", [sbt(ph, "tmpb%d" % i, [128, 128]) for i in range(2)])
            junkc = sbt(ph, "junkc", [128, 512], BF16)
            junk2 = sbt(ph, "junk2", [128, D], BF16)
            m12 = Ring("m12", [sbt(ph, "m12_%d" % i, [128, 2, 512]) for i in range(2)])
            ssp = sbt(ph, "ssp", [128, 4, 4])
            ssv = sbt(ph, "ssv", [128, 4, 4])
            mmr = Ring("mmc", [pst(ph, "mmc%d" % i, [128, 512]) for i in range(3)])
            tpcr = Ring("tpc", [pst(ph, "tpc%d" % i, [128, 512]) for i in range(1)])
            big4 = pst(ph, "big4", [128, 4, 512])
            wuh = w_up_hg.rearrange("(kt p) n -> p kt n", p=128)
            wum = w_up_ml.rearrange("(kt p) n -> p kt n", p=128)
            wo = w_out.rearrange("(kt p) n -> p kt n", p=128)
            wfi = w_ffn_in.rearrange("(kt p) n -> p kt n", p=128)
            wfo = w_ffn_out.rearrange("(kt p) n -> p kt n", p=128)

            def load_slot(pieces):
                wt, wres = wring.next()
                first = True
                for (src, k0, nk, c0, ncol) in pieces:
                    P.dma("pool", lambda e, wt=wt, src=src, k0=k0, nk=nk, c0=c0, ncol=ncol: e.dma_start(
                        out=wt[:, k0:k0 + nk, c0:c0 + ncol], in_=src), writes=[wres] if first else [], key=wres)
                    first = False
                return wt, wres

            def make_gbc(q, c):
                for q4 in range(4):
                    ps, pres = mmr.next()
                    for j in range(4):
                        kt = q4 * 4 + j
                        tb_, tres = tmpb.next()
                        P.dve(lambda e, tb_=tb_, kt=kt: e.tensor_copy(out=tb_[:], in_=bc(modv[:, q, kt, c:c + 1], 1, 128)),
                              reads=["modv%d" % q], writes=[tres])
                        P.pe(lambda e, ps=ps, tb_=tb_, j=j: e.matmul(ps[:, j * 128:(j + 1) * 128], lhsT=tb_[:], rhs=identf[:], start=True,
                                                                    stop=True), reads=[tres, "identf"], writes=[pres])
                    P.act(lambda e, ps=ps, q4=q4: e.copy(out=Gbc[:, q4 * 512:(q4 + 1) * 512], in_=ps[:]), reads=[pres], writes=["Gbc"])

            def rstd_from(ssap, outap, res_in, res_out):
                P.act(lambda e: e.activation(out=outap, in_=ssap, func=AF.Ln, scale=1.0 / D, bias=epst[:]), reads=[res_in, "epst"],
                      writes=[res_out])
                P.act(lambda e: e.activation(out=outap, in_=outap, func=AF.Exp, scale=-0.5), reads=[res_out], writes=[res_out])

            _cstop = 9
            _ntb = NTB
            for tb in range(_ntb):
                c = 0 if tb < 2 else 1
                tsl = slice(tb * 512, (tb + 1) * 512)
                P.dma("sp", lambda e, tsl=tsl: e.dma_start(out=actA[:, 0:8, :], in_=OHG[:, :, tsl]),
                      writes=[("actA", kt) for kt in range(8)], key="actA")
                P.dma("sp", lambda e, tsl=tsl: e.dma_start(out=actA[:, 8:16, :], in_=OML[:, :, tsl]),
                      writes=[("actA", kt) for kt in range(8, 16)], key="actA")
                for jb in range(4):
                    wt, wres = load_slot([(wuh[:, :, jb * 512:(jb + 1) * 512], 0, 8, 0, 512),
                                          (wum[:, :, jb * 512:(jb + 1) * 512], 8, 8, 0, 512)])
                    for jj in range(4):
                        j = jb * 4 + jj
                        ps1, pr1 = mmr.next()
                        ps2, pr2 = mmr.next()
                        for kt in range(8):
                            P.pe(lambda e, ps1=ps1, wt=wt, kt=kt, jj=jj: e.matmul(ps1[:], lhsT=wt[:, kt, jj * 128:(jj + 1) * 128],
                                                                                rhs=actA[:, kt, :], start=(kt == 0), stop=(kt == 7)),
                                 reads=[wres, ("actA", kt)], writes=[pr1])
                        for kt in range(8, 16):
                            P.pe(lambda e, ps2=ps2, wt=wt, kt=kt, jj=jj: e.matmul(ps2[:], lhsT=wt[:, kt, jj * 128:(jj + 1) * 128],
                                                                                rhs=actA[:, kt, :], start=(kt == 8), stop=(kt == 15)),
                                 reads=[wres, ("actA", kt)], writes=[pr2])
                        sg, sgres = sgr.next()
                        P.dma("sp", lambda e, sg=sg, j=j, tsl=tsl: e.dma_start(out=sg[:, 0, :], in_=SGA[j * 128:(j + 1) * 128, tsl]),
                              writes=[sgres], key=sgres)
                        P.dma("sp", lambda e, sg=sg, j=j, tsl=tsl: e.dma_start(out=sg[:, 1, :], in_=SGB[j * 128:(j + 1) * 128, tsl]),
                              key=sgres)
                        mm, mres = m12.next()
                        P.dve(lambda e, mm=mm, ps1=ps1, sg=sg: e.tensor_tensor(out=mm[:, 0, :], in0=ps1[:], in1=sg[:, 0, :], op=ALU.mult),
                              reads=[pr1, sgres], writes=[(mres, 0)])
                        P.dve(lambda e, mm=mm, ps2=ps2, sg=sg: e.tensor_tensor(out=mm[:, 1, :], in0=ps2[:], in1=sg[:, 1, :], op=ALU.mult),
                              reads=[pr2, sgres], writes=[(mres, 1)])
                        P.pool(lambda e, mm=mm, j=j: e.tensor_tensor(out=big[:, j, :], in0=mm[:, 0, :], in1=mm[:, 1, :], op=ALU.add),
                               reads=[(mres, 0), (mres, 1)], writes=[("big", j)])
                if _cstop < 1.2:
                    continue
                make_gbc(2, c)
                if _cstop < 1.5:
                    continue
                for cb in range(4):
                    wt, wres = load_slot([(wo[:, :, cb * 512:(cb + 1) * 512], 0, 16, 0, 512)])
                    for tt in range(4):
                        ps, pres = mmr.next()
                        for kt in range(KT):
                            P.pe(lambda e, ps=ps, wt=wt, kt=kt, tt=tt: e.matmul(ps[:], lhsT=big[:, kt, tt * 128:(tt + 1) * 128],
                                                                              rhs=wt[:, kt, :], start=(kt == 0), stop=(kt == KT - 1)),
                                 reads=[wres, ("big", kt)], writes=[pres])
                        P.dve(lambda e, ps=ps, tt=tt, cb=cb: e.tensor_copy(out=yblk[:, tt, cb * 512:(cb + 1) * 512], in_=ps[:]),
                              reads=[pres], writes=[("yblk", tt, cb)])
                if _cstop < 1.8:
                    continue
                for tt in range(4):
                    tok0 = tb * 512 + tt * 128
                    yres = [("yblk", tt, cb) for cb in range(4)]
                    P.act(lambda e, tt=tt: e.activation(out=junk2[:], in_=yblk[:, tt, :], func=AF.Square, accum_out=ssv[:, tt, 0:1]),
                          reads=yres, writes=["junk2", ("ssv", tt, 0)])
                    rstd_from(ssv[:, tt, 0:1], ssv[:, tt, 1:2], ("ssv", tt, 0), ("ssv", tt, 1))
                    xt, xres = xr.next()
                    P.dma("sp", lambda e, xt=xt, tok0=tok0: e.dma_start(out=xt[:], in_=x_all[tok0:tok0 + 128, :]), writes=[xres], key=xres)
                    P.dve(lambda e, tt=tt: e.tensor_scalar(out=yblk[:, tt, :], in0=yblk[:, tt, :], scalar1=ssv[:, tt, 1:2], scalar2=None,
                                                           op0=ALU.mult), reads=yres + [("ssv", tt, 1)], writes=yres)
                    P.dve(lambda e, tt=tt: e.tensor_tensor(out=yblk[:, tt, :], in0=yblk[:, tt, :], in1=Gbc[:], op=ALU.mult),
                          reads=yres + ["Gbc"], writes=yres)
                    P.dve(lambda e, xt=xt, tt=tt: e.tensor_tensor(out=xt[:], in0=xt[:], in1=yblk[:, tt, :], op=ALU.add),
                          reads=yres + [xres], writes=[xres])
                    P.dma("sp", lambda e, xt=xt, tok0=tok0: e.dma_start(out=X1[tok0:tok0 + 128, :], in_=xt[:]), reads=[xres],
                          writes=[("X1", tb, tt)], key=xres)
                P.flush()
                if _cstop < 1.9:
                    continue
                make_gbc(3, c)
                if _cstop < 1.91:
                    P.flush()
                    continue
                for tt in range(4):
                    tok0 = tb * 512 + tt * 128
                    yres = [("yblk", tt, cb) for cb in range(4)]
                    xt, xres = xr.next()
                    P.dma("sp", lambda e, xt=xt, tok0=tok0: e.dma_start(out=xt[:], in_=X1[tok0:tok0 + 128, :]), writes=[xres], key=xres)
                    P.act(lambda e, xt=xt, tt=tt: e.activation(out=junk2[:], in_=xt[:], func=AF.Square, accum_out=ssv[:, tt, 2:3]),
                          reads=[xres], writes=["junk2", ("ssv", tt, 2)])
                    rstd_from(ssv[:, tt, 2:3], ssv[:, tt, 3:4], ("ssv", tt, 2), ("ssv", tt, 3))
                    P.dve(lambda e, xt=xt, tt=tt: e.tensor_scalar(out=yblk[:, tt, :], in0=xt[:], scalar1=ssv[:, tt, 3:4], scalar2=None,
                                                                  op0=ALU.mult), reads=[xres, ("ssv", tt, 3)], writes=yres)
                    P.dve(lambda e, tt=tt: e.tensor_tensor(out=yblk[:, tt, :], in0=yblk[:, tt, :], in1=Gbc[:], op=ALU.mult),
                          reads=yres + ["Gbc"], writes=yres)
                if _cstop < 1.92:
                    P.flush()
                    continue
                make_gbc(4, c)
                if _cstop < 1.93:
                    P.flush()
                    continue
                for tt in range(4):
                    yres = [("yblk", tt, cb) for cb in range(4)]
                    hb, hbres = hbr.next()
                    P.dve(lambda e, tt=tt, hb=hb: e.tensor_tensor(out=hb[:], in0=yblk[:, tt, :], in1=Gbc[:], op=ALU.add),
                          reads=yres + ["Gbc"], writes=[hbres])
                    if _cstop < 1.94:
                        continue
                    for q4 in range(4):
                        ps, pres = tpcr.next()
                        psb16 = ps[:].bitcast(BF16)
                        for j in range(4):
                            kt = q4 * 4 + j
                            P.pe(lambda e, psb16=psb16, j=j, kt=kt, hb=hb: e.transpose(psb16[:, j * 128:(j + 1) * 128],
                                                                                      hb[:, kt * 128:(kt + 1) * 128], identb[:]),
                                 reads=[hbres, "identb"], writes=[pres])
                        k0 = q4 * 4
                        if _cstop < 1.95:
                            continue
                        P.act(lambda e, psb16=psb16, k0=k0, tt=tt: e.copy(
                            out=h2T[:, k0:k0 + 4, tt * 128:(tt + 1) * 128], in_=psb16[:, 0:512].rearrange("p (j t) -> p j t", j=4)),
                            reads=[pres], writes=[("h2T", k0 + j) for j in range(4)])
                P.flush()
                if _cstop < 3:
                    continue
                for j2 in range(22):
                    wt, wres = load_slot([(wfi[:, :, j2 * 256:(j2 + 1) * 256], 0, 16, 0, 256),
                                          (wfi[:, :, DFF + j2 * 256:DFF + (j2 + 1) * 256], 0, 16, 256, 256)])
                    for jj in range(2):
                        j = j2 * 2 + jj
                        psa, pra = mmr.next()
                        psb, prb = mmr.next()
                        for kt in range(KT):
                            P.pe(lambda e, psa=psa, wt=wt, kt=kt, jj=jj: e.matmul(psa[:], lhsT=wt[:, kt, jj * 128:(jj + 1) * 128],
                                                                                rhs=h2T[:, kt, :], start=(kt == 0), stop=(kt == KT - 1)),
                                 reads=[wres, ("h2T", kt)], writes=[pra])
                        for kt in range(KT):
                            P.pe(lambda e, psb=psb, wt=wt, kt=kt, jj=jj: e.matmul(psb[:], lhsT=wt[:, kt, 256 + jj * 128:256 + (jj + 1) * 128],
                                                                                rhs=h2T[:, kt, :], start=(kt == 0), stop=(kt == KT - 1)),
                                 reads=[wres, ("h2T", kt)], writes=[prb])
                        mm, mres = m12.next()
                        P.act(lambda e, mm=mm, psa=psa: e.activation(out=mm[:, 0, :], in_=psa[:], func=AF.Silu), reads=[pra],
                              writes=[(mres, 0)])
                        P.dve(lambda e, mm=mm, psb=psb, j=j: e.tensor_tensor(out=big[:, j, :], in0=mm[:, 0, :], in1=psb[:], op=ALU.mult),
                              reads=[(mres, 0), prb], writes=[("big", j)])
                if _cstop < 4:
                    continue
                make_gbc(5, c)
                for cb in range(4):
                    for (k0, nk) in ((0, 16), (16, 16), (32, 12)):
                        wt, wres = load_slot([(wfo[:, k0:k0 + nk, cb * 512:(cb + 1) * 512], 0, nk, 0, 512)])
                        for tt in range(4):
                            for kk in range(nk):
                                P.pe(lambda e, wt=wt, kk=kk, k0=k0, tt=tt: e.matmul(
                                    big4[:, tt, :], lhsT=big[:, k0 + kk, tt * 128:(tt + 1) * 128], rhs=wt[:, kk, :],
                                    start=(k0 + kk == 0), stop=(k0 + kk == 43)), reads=[wres, ("big", k0 + kk)], writes=[("big4", tt)])
                    for tt in range(4):
                        P.dve(lambda e, tt=tt, cb=cb: e.tensor_copy(out=yblk[:, tt, cb * 512:(cb + 1) * 512], in_=big4[:, tt, :]),
                              reads=[("big4", tt)], writes=[("yblk", tt, cb)])
                for tt in range(4):
                    tok0 = tb * 512 + tt * 128
                    yres = [("yblk", tt, cb) for cb in range(4)]
                    P.act(lambda e, tt=tt: e.activation(out=junk2[:], in_=yblk[:, tt, :], func=AF.Square, accum_out=ssv[:, tt, 0:1]),
                          reads=yres, writes=["junk2", ("ssv", tt, 0)])
                    rstd_from(ssv[:, tt, 0:1], ssv[:, tt, 1:2], ("ssv", tt, 0), ("ssv", tt, 1))
                    xt, xres = xr.next()
                    P.dma("sp", lambda e, xt=xt, tok0=tok0: e.dma_start(out=xt[:], in_=X1[tok0:tok0 + 128, :]), reads=[("X1", tb, tt)],
                          writes=[xres], key=xres)
                    P.dve(lambda e, tt=tt: e.tensor_scalar(out=yblk[:, tt, :], in0=yblk[:, tt, :], scalar1=ssv[:, tt, 1:2], scalar2=None,
                                                           op0=ALU.mult), reads=yres + [("ssv", tt, 1)], writes=yres)
                    P.dve(lambda e, tt=tt: e.tensor_tensor(out=yblk[:, tt, :], in0=yblk[:, tt, :], in1=Gbc[:], op=ALU.mult),
                          reads=yres + ["Gbc"], writes=yres)
                    P.dve(lambda e, xt=xt, tt=tt: e.tensor_tensor(out=xt[:], in0=xt[:], in1=yblk[:, tt, :], op=ALU.add),
                          reads=yres + [xres], writes=[xres])
                    P.dma("sp", lambda e, xt=xt, tok0=tok0: e.dma_start(out=y_all[tok0:tok0 + 128, :], in_=xt[:]), reads=[xres], key=xres)
            P.flush()
        return nc


def shard_inputs(inputs):
    f = lambda a: np.ascontiguousarray(np.asarray(a, dtype=np.float32))
    xp = f(inputs["x_prompt"])
    xs = f(inputs["x_sample"])
    c = f(inputs["c"])
    c_ctx = f(inputs["c_ctx"])
    shared = {
        "w_mod": f(inputs["w_mod"])[0],
        "b_mod": f(inputs["b_mod"])[0].reshape(96, 128),
        "gvec": np.concatenate([f(inputs[k])[0].reshape(16, 128) for k in
                                ("norm_pre_mix", "norm_post_mix", "norm_pre_ffn", "norm_post_ffn")], axis=0),
        "w_in": f(inputs["w_in"])[0],
        "lb_logits": f(inputs["hgrn_lb_logits"]).reshape(2, 2, 8, 128).reshape(32, 128),
        "hg_g": f(inputs["hgrn_norm_g"])[0].reshape(1, 128),
        "b_if": np.concatenate([f(inputs["mlstm_b_i"])[0], f(inputs["mlstm_b_f"])[0]]).reshape(1, 32),
        "ml_g": f(inputs["mlstm_norm_g"])[0].reshape(1, 1024),
        "w_up_hg": f(inputs["w_up_hgrn"])[0],
        "w_up_ml": f(inputs["w_up_mlstm"])[0],
        "w_out": f(inputs["w_out"])[0],
        "w_ffn_in": f(inputs["w_ffn_in"])[0],
        "w_ffn_out": f(inputs["w_ffn_out"])[0],
    }
    maps = []
    for i in range(NCORES):
        m = dict(shared)
        m["x_all"] = np.concatenate([xp[4 * i:4 * i + 4].reshape(1024, D), xs[i]], axis=0)
        m["cond2"] = np.concatenate([c_ctx.reshape(16, 128), c[i].reshape(16, 128)], axis=0)
        m["st_s"] = f(inputs["state_hgrn_s"])[i, 0]
        m["st_c"] = f(inputs["state_mlstm_c"])[i, 0]
        m["st_n"] = f(inputs["state_mlstm_n"])[i, 0]
        m["st_m"] = f(inputs["state_mlstm_m"])[i, 0].reshape(1, 16)
        maps.append(m)
    return maps


def kernel(**inputs):
    maps = shard_inputs(inputs)
    nc = build_nc()
    rs = run_bass_kernel_spmd(nc, maps, core_ids=list(range(NCORES))).results
    rs2 = rs
    y = np.stack([r["y_all"] for r in rs2])
    y_prompt = y[:, :1024].reshape(32, 256, D)
    y_sample = y[:, 1024:]
    new_s = np.concatenate([r["o_s"] for r in rs], axis=0)[:, None]
    new_c = np.concatenate([r["o_c"] for r in rs], axis=0)[:, None]
    new_n = np.concatenate([r["o_n"] for r in rs], axis=0)[:, None]
    new_m = np.concatenate([r["o_m"].reshape(4, 2, 8) for r in rs], axis=0)[:, None]
    return (np.ascontiguousarray(y_prompt), np.ascontiguousarray(y_sample), np.ascontiguousarray(new_s),
            np.ascontiguousarray(new_c), np.ascontiguousarray(new_n), np.ascontiguousarray(new_m))
```

```python
import contextlib
import numpy as np
import concourse.bass as bass
import concourse.mybir as mybir
from concourse.bass_utils import run_bass_kernel_spmd

F32 = mybir.dt.float32
BF16 = mybir.dt.bfloat16
AF = mybir.ActivationFunctionType
ALU = mybir.AluOpType
AX = mybir.AxisListType

NCORES = 8
NT = 3072
D = 2048
KT = 16
NTT = 24
NTB = 6
NGRP = 12
NCHUNK = 48
DFF = 5632
EPS = 1e-6
NEG = -1.0e30
O_HQ, O_HFF, O_HFB, O_HI, O_HG = 0, 1024, 2048, 3072, 4096
O_MQ, O_MK, O_MV, O_MI, O_MF, O_MO, O_GA, O_GB = 5120, 5632, 6144, 7168, 7184, 7200, 8224, 10272
IN_W = 12320
SEQS = [(0, 4, True, 0), (4, 4, True, 1), (8, 4, True, 2), (12, 4, True, 3), (16, 32, False, 0)]


def bc(ap, dim, n):
    l = [list(x) for x in ap.ap]
    l[dim] = [0, n]
    return bass.AP(ap.tensor, ap.offset, l)


def ins_bc(ap, dim, n):
    l = [list(x) for x in ap.ap]
    l.insert(dim, [0, n])
    return bass.AP(ap.tensor, ap.offset, l)


class Prog:
    ENGS = ("pe", "act", "dve", "pool", "sp")

    def __init__(self, nc, stack, n_dma_sems=80):
        self.nc = nc
        self.eng_sem = {e: stack.enter_context(nc.semaphore("c_" + e)) for e in self.ENGS}
        self.dma_sems = [stack.enter_context(nc.semaphore("d%d" % i)) for i in range(n_dma_sems)]
        self.eng_cnt = {e: 0 for e in self.ENGS}
        self.sem_cnt = [0] * n_dma_sems
        self.pool_keys = {}
        self.discard = False
        self.defer = None
        self._reset()
        self.total_ops = 0

    def _reset(self):
        self.ops = []
        self.last_w = {}
        self.readers = {}
        self.key_idx = {}
        self.key_cnt = {}
        self.n_sp_keys = 0

    def add(self, eng, fn, reads=(), writes=(), dma_key=None, war=()):
        if self.discard:
            return -1
        if self.defer is not None:
            self.defer.append((eng, fn, tuple(reads), tuple(writes), dma_key, tuple(war)))
            return -1
        idx = len(self.ops)
        deps = set()
        for w in war:
            deps.update(self.readers.get(w, ()))
        for r in reads:
            if r in self.last_w:
                deps.add(self.last_w[r])
        for w in writes:
            if w in self.last_w:
                deps.add(self.last_w[w])
            deps.update(self.readers.get(w, ()))
        deps.discard(idx)
        for r in reads:
            self.readers.setdefault(r, []).append(idx)
        for w in writes:
            self.last_w[w] = idx
            self.readers[w] = []
        latest = {}
        pruned = set()
        for d in deps:
            dop = self.ops[d]
            if dop["dma_key"] is None:
                if dop["eng"] not in latest or latest[dop["eng"]] < d:
                    latest[dop["eng"]] = d
            else:
                pruned.add(d)
        pruned.update(latest.values())
        deps = pruned
        dwait = {}
        for d in deps:
            k = self.ops[d]["dma_key"]
            if k is not None:
                dwait[k] = self.key_cnt[k]
        if dma_key is not None:
            if dma_key not in self.key_idx:
                if eng == "pool":
                    if dma_key not in self.pool_keys:
                        self.pool_keys[dma_key] = len(self.dma_sems) - 1 - len(self.pool_keys)
                        assert len(self.pool_keys) <= 8
                    self.key_idx[dma_key] = self.pool_keys[dma_key]
                else:
                    self.n_sp_keys += 1
                    assert self.n_sp_keys <= len(self.dma_sems) - 8, "too many dma keys"
                    self.key_idx[dma_key] = self.n_sp_keys - 1
                self.key_cnt[dma_key] = self.sem_cnt[self.key_idx[dma_key]]
            self.key_cnt[dma_key] += 16
        self.ops.append(dict(eng=eng, fn=fn, deps=deps, dma_key=dma_key, idx=idx, dwait=dwait))
        return idx

    def replay_interleaved(self, main, other, shared):
        last_other = {}
        for i, op in enumerate(other):
            for r in op[2] + op[3]:
                if r in shared:
                    last_other[r] = i
        po = 0
        for op in main:
            need = -1
            for r in op[2] + op[3]:
                if r in last_other:
                    need = max(need, last_other[r])
            while po <= need:
                self.add(*other[po])
                po += 1
            self.add(*op)
            if po < len(other):
                self.add(*other[po])
                po += 1
        while po < len(other):
            self.add(*other[po])
            po += 1

    def pe(self, fn, reads=(), writes=()):
        return self.add("pe", fn, reads, writes)

    def act(self, fn, reads=(), writes=()):
        return self.add("act", fn, reads, writes)

    def dve(self, fn, reads=(), writes=()):
        return self.add("dve", fn, reads, writes)

    def pool(self, fn, reads=(), writes=()):
        return self.add("pool", fn, reads, writes)

    def dma(self, eng, fn, reads=(), writes=(), key=None, war=()):
        assert key is not None
        return self.add(eng, fn, reads, writes, dma_key=key, war=war)

    def flush(self):
        nc = self.nc
        ops = self.ops
        if not ops:
            return
        needed = set()
        for op in ops:
            for d in op["deps"]:
                dop = ops[d]
                if dop["dma_key"] is None and dop["eng"] == "pe" and op["eng"] == "pe" and op["dma_key"] is None:
                    continue
                needed.add(d)
        per_eng = {e: [op for op in ops if op["eng"] == e] for e in self.ENGS}
        for e in self.ENGS:
            comp = [op for op in per_eng[e] if op["dma_key"] is None]
            if comp:
                needed.add(comp[-1]["idx"])
        for op in ops:
            if op["dma_key"] is not None:
                ki = self.key_idx[op["dma_key"]]
                self.sem_cnt[ki] += 16
                op["sig"] = (self.dma_sems[ki], self.sem_cnt[ki], ("k", ki))
            elif op["idx"] in needed:
                self.eng_cnt[op["eng"]] += 1
                op["sig"] = (self.eng_sem[op["eng"]], self.eng_cnt[op["eng"]], ("e", op["eng"]))
            else:
                op["sig"] = None
        final_eng = dict(self.eng_cnt)
        final_keys = [(self.dma_sems[ki], self.sem_cnt[ki]) for ki in range(len(self.dma_sems)) if self.sem_cnt[ki] > 0]
        with nc.Block() as block:
            def make_body(e):
                def body(engine):
                    known = {}
                    for op in per_eng[e]:
                        waits = {}
                        for d in op["deps"]:
                            dop = ops[d]
                            if dop["sig"] is None:
                                continue
                            if dop["dma_key"] is None and dop["eng"] == "pe" and e == "pe" and op["dma_key"] is None:
                                continue
                            sem, val, sk = dop["sig"]
                            if dop["dma_key"] is not None:
                                val = op["dwait"][dop["dma_key"]]
                            if known.get(sk, 0) >= val:
                                continue
                            if sk not in waits or waits[sk][1] < val:
                                waits[sk] = (sem, val)
                        for sk, (sem, val) in waits.items():
                            engine.wait_ge(sem, val)
                            known[sk] = val
                        ins = op["fn"](engine)
                        if op["sig"] is not None:
                            ins.then_inc(op["sig"][0], 16 if op["dma_key"] is not None else 1)
                    for e2 in self.ENGS:
                        if final_eng[e2] > 0:
                            engine.wait_ge(self.eng_sem[e2], final_eng[e2])
                    for sem, val in final_keys:
                        engine.wait_ge(sem, val)
                return body
            block.tensor(make_body("pe"))
            block.scalar(make_body("act"))
            block.vector(make_body("dve"))
            block.gpsimd(make_body("pool"))
            block.sync(make_body("sp"))
        self.total_ops += len(ops)
        self._reset()


class Ring:
    def __init__(self, name, tiles):
        self.name = name
        self.tiles = tiles
        self.i = 0

    def next(self):
        k = self.i % len(self.tiles)
        self.i += 1
        return self.tiles[k], (self.name, k)


def build_nc(debug=False, stop_after=None, part=0):
    nc = bass.Bass("TRN2", target_bir_lowering=False)
    dbg_kind = "ExternalOutput" if debug else "Internal"

    def din(name, shape, dt=F32):
        return nc.dram_tensor(name, shape, dt, kind="ExternalInput").ap()

    def dout(name, shape, dt=F32):
        return nc.dram_tensor(name, shape, dt, kind="ExternalOutput").ap()

    def dscr(name, shape, dt=BF16):
        return nc.dram_tensor(name, shape, dt, kind=dbg_kind).ap()

    x_all = din("x_all", [NT, D])
    cond2 = din("cond2", [32, 128])
    st_s = din("st_s", [2, 8, 128, 128])
    st_c = din("st_c", [2, 8, 64, 128])
    st_n = din("st_n", [2, 8, 64])
    st_m = din("st_m", [1, 16])
    w_mod = din("w_mod", [D, 6 * D])
    b_mod = din("b_mod", [96, 128])
    gvec = din("gvec", [64, 128])
    w_in = din("w_in", [D, IN_W])
    lb_logits = din("lb_logits", [32, 128])
    hg_g = din("hg_g", [1, 128])
    b_if = din("b_if", [1, 32])
    ml_g = din("ml_g", [1, 1024])
    w_up_hg = din("w_up_hg", [1024, D])
    w_up_ml = din("w_up_ml", [1024, D])
    w_out = din("w_out", [D, D])
    w_ffn_in = din("w_ffn_in", [D, 2 * DFF])
    w_ffn_out = din("w_ffn_out", [DFF, D])

    y_all = dout("y_all", [NT, D])
    o_s = dout("o_s", [4, 2, 8, 128, 128])
    o_c = dout("o_c", [4, 2, 8, 64, 128])
    o_n = dout("o_n", [4, 2, 8, 64])
    o_m = dout("o_m", [4, 16])

    QF = dscr("QF", [2, 2, NGRP, 128, 8, 256])
    KF = dscr("KF", [2, 2, NGRP, 128, 8, 256])
    MQ = dscr("MQ", [512, NT])
    MKF = dscr("MKF", [512, NT])
    hkind = {0: dbg_kind, 1: "ExternalOutput", 2: "ExternalInput"}[part]
    SGA = nc.dram_tensor("SGA", [D, NT], BF16, kind=hkind).ap()
    SGB = nc.dram_tensor("SGB", [D, NT], BF16, kind=hkind).ap()
    VV = dscr("VV", [NT, 1024])
    HGATE = dscr("HGATE", [NT, 1024])
    MKT = dscr("MKT", [NT, 512])
    MVV = dscr("MVV", [NT, 1024])
    MOO = dscr("MOO", [NT, 1024])
    GATES = dscr("GATES", [NT, 32], F32)
    OFW = dscr("OFW", [NT, 2048], F32)
    OHG = nc.dram_tensor("OHG", [128, 8, NT], BF16, kind=hkind).ap()
    OML = nc.dram_tensor("OML", [128, 8, NT], BF16, kind=hkind).ap()
    X1 = dscr("X1", [NT, D], F32)

    with contextlib.ExitStack() as outer:
        def sbt(st, name, shape, dt=F32):
            return st.enter_context(nc.sbuf_tensor(name, shape, dt))

        def pst(st, name, shape, dt=F32):
            return st.enter_context(nc.psum_tensor(name, shape, dt))

        P = Prog(nc, outer)

        identf = sbt(outer, "identf", [128, 128])
        identb = sbt(outer, "identb", [128, 128], BF16)
        up01 = sbt(outer, "up01", [64, 64])
        low01 = sbt(outer, "low01", [64, 64])
        upneg = sbt(outer, "upneg", [64, 64])
        lowneg = sbt(outer, "lowneg", [64, 64])
        ones64 = sbt(outer, "ones64", [64, 64])
        onesb = sbt(outer, "onesb", [64, 2], BF16)
        epst = sbt(outer, "epst", [128, 1])
        modv = sbt(outer, "modv", [128, 6, 16, 2])
        lbv = sbt(outer, "lbv", [128, 2, 16])
        dec_all = sbt(outer, "dec_all", [128, 2, 8, NCHUNK])
        hgg_bc = sbt(outer, "hgg_bc", [64, 128])
        mlg_bc = sbt(outer, "mlg_bc", [64, 1024])
        bif_bc = sbt(outer, "bif_bc", [128, 32])

        P.pool(lambda e: e.memset(identf[:], 1.0), writes=["identf"])
        P.pool(lambda e: e.affine_select(out=identf[:], in_=identf[:], pattern=[[-1, 128]], compare_op=ALU.is_equal,
                                        fill=0.0, base=0, channel_multiplier=1), reads=["identf"], writes=["identf"])
        P.dve(lambda e: e.tensor_copy(out=identb[:], in_=identf[:]), reads=["identf"], writes=["identb"])
        for t, nm, val in ((up01, "up01", 1.0), (low01, "low01", 1.0), (upneg, "upneg", 0.0), (lowneg, "lowneg", 0.0),
                           (ones64, "ones64", 1.0), (onesb, "onesb", 1.0), (epst, "epst", EPS)):
            P.pool(lambda e, t=t, val=val: e.memset(t[:], val), writes=[nm])
        P.pool(lambda e: e.affine_select(out=up01[:], in_=up01[:], pattern=[[1, 64]], compare_op=ALU.is_ge, fill=0.0,
                                        base=0, channel_multiplier=-1), reads=["up01"], writes=["up01"])
        P.pool(lambda e: e.affine_select(out=upneg[:], in_=upneg[:], pattern=[[1, 64]], compare_op=ALU.is_ge, fill=NEG,
                                        base=0, channel_multiplier=-1), reads=["upneg"], writes=["upneg"])
        P.pool(lambda e: e.affine_select(out=low01[:], in_=low01[:], pattern=[[-1, 64]], compare_op=ALU.is_ge, fill=0.0,
                                        base=0, channel_multiplier=1), reads=["low01"], writes=["low01"])
        P.pool(lambda e: e.affine_select(out=lowneg[:], in_=lowneg[:], pattern=[[-1, 64]], compare_op=ALU.is_ge, fill=NEG,
                                        base=0, channel_multiplier=1), reads=["lowneg"], writes=["lowneg"])
        P.dma("sp", lambda e: e.dma_start(out=hgg_bc[:], in_=bass.AP(hg_g.tensor, hg_g.offset, [[0, 64], [1, 128]])),
              writes=["hgg_bc"], key="c0")
        P.dma("sp", lambda e: e.dma_start(out=mlg_bc[:], in_=bass.AP(ml_g.tensor, ml_g.offset, [[0, 64], [1, 1024]])),
              writes=["mlg_bc"], key="c0")
        P.dma("sp", lambda e: e.dma_start(out=bif_bc[:], in_=bass.AP(b_if.tensor, b_if.offset, [[0, 128], [1, 32]])),
              writes=["bif_bc"], key="c0")

        with contextlib.ExitStack() as ph:
            rows = sbt(ph, "rows", [128, 256])
            colsT = sbt(ph, "colsT", [128, 256])
            scT = sbt(ph, "scT", [128, 32], BF16)
            modT = sbt(ph, "modT", [128, 96, 2])
            wsl = [sbt(ph, "w0s%d" % i, [128, KT, 512], BF16) for i in range(3)]
            wring = Ring("wsl", wsl)
            tp0 = pst(ph, "tp0", [128, 512])
            modps = pst(ph, "modps", [128, 192])
            rows2 = sbt(ph, "rows2", [128, 128])
            P.dma("sp", lambda e: e.dma_start(out=rows[0:32, 0:128], in_=cond2), writes=["rows"], key="p0a")
            P.dma("sp", lambda e: e.dma_start(out=rows[32:128, 0:128], in_=b_mod), writes=["rows"], key="p0a")
            P.dma("sp", lambda e: e.dma_start(out=rows2[0:64, :], in_=gvec), writes=["rows2"], key="p0b")
            P.dma("sp", lambda e: e.dma_start(out=rows2[64:96, :], in_=lb_logits), writes=["rows2"], key="p0b")
            P.pe(lambda e: e.transpose(tp0[:, 0:128], rows[:, 0:128], identf[:]), reads=["rows", "identf"], writes=["tp0"])
            P.pe(lambda e: e.transpose(tp0[:, 128:224], rows2[0:96, :], identf[0:96, 0:96]), reads=["rows2", "identf"], writes=["tp0"])
            P.dve(lambda e: e.tensor_copy(out=colsT[:, 0:224], in_=tp0[:, 0:224]), reads=["tp0"], writes=["colsT"])
            P.act(lambda e: e.activation(out=scT[:], in_=tp0[:, 0:32], func=AF.Silu), reads=["tp0"], writes=["scT"])
            lg = colsT[:, 192:224].rearrange("p (d l h) -> p d l h", d=2, l=2)
            lb3 = lbv[:, 0, :].rearrange("p (d h) -> p d h", d=2)
            P.dve(lambda e: e.tensor_tensor(out=lb3, in0=lg[:, :, 0, :], in1=lg[:, :, 1, :], op=ALU.subtract),
                  reads=["colsT"], writes=["lbv"])
            P.act(lambda e: e.activation(out=lbv[:, 0, :], in_=lbv[:, 0, :], func=AF.Sigmoid), reads=["lbv"], writes=["lbv"])
            P.dve(lambda e: e.tensor_scalar(out=lbv[:, 1, :], in0=lbv[:, 0, :], scalar1=-1.0, scalar2=1.0, op0=ALU.mult,
                                            op1=ALU.add), reads=["lbv"], writes=["lbv"])
            wm = w_mod.rearrange("(kt p) n -> p kt n", p=128)
            for cb in range(24):
                wt, wres = wring.next()
                P.dma("pool", lambda e, wt=wt, cb=cb: e.dma_start(out=wt[:], in_=wm[:, :, cb * 512:(cb + 1) * 512]),
                      writes=[wres], key=wres)
                for j in range(4):
                    ct = cb * 4 + j
                    for kt in range(KT):
                        rhs = bass.AP(scT[:].tensor, scT[:, kt:kt + 1].offset, [list(scT[:].ap[0]), [16, 2]])
                        P.pe(lambda e, wt=wt, j=j, kt=kt, ct=ct, rhs=rhs: e.matmul(
                            modps[:, ct * 2:ct * 2 + 2], lhsT=wt[:, kt, j * 128:(j + 1) * 128], rhs=rhs,
                            start=(kt == 0), stop=(kt == KT - 1)), reads=[wres, "scT"], writes=["modps"])
            bmT = ins_bc(colsT[:, 32:128], 2, 2)
            P.dve(lambda e: e.tensor_tensor(out=modT[:], in0=modps[:].rearrange("p (c two) -> p c two", two=2), in1=bmT,
                                            op=ALU.add), reads=["modps", "colsT"], writes=["modT"])
            def gv(i):
                return ins_bc(colsT[:, 128 + 16 * i:128 + 16 * (i + 1)], 2, 2)
            def mch(q):
                return modT[:, 16 * q:16 * (q + 1), :]
            P.dve(lambda e: e.scalar_tensor_tensor(out=modv[:, 0], in0=mch(1), scalar=1.0, in1=gv(0), op0=ALU.add,
                                                   op1=ALU.mult), reads=["modT", "colsT"], writes=["modv0"])
            P.dve(lambda e: e.tensor_copy(out=modv[:, 1], in_=mch(0)), reads=["modT"], writes=["modv1"])
            P.dve(lambda e: e.tensor_tensor(out=modv[:, 2], in0=mch(2), in1=gv(1), op=ALU.mult), reads=["modT", "colsT"],
                  writes=["modv2"])
            P.dve(lambda e: e.scalar_tensor_tensor(out=modv[:, 3], in0=mch(4), scalar=1.0, in1=gv(2), op0=ALU.add,
                                                   op1=ALU.mult), reads=["modT", "colsT"], writes=["modv3"])
            P.dve(lambda e: e.tensor_copy(out=modv[:, 4], in_=mch(3)), reads=["modT"], writes=["modv4"])
            P.dve(lambda e: e.tensor_tensor(out=modv[:, 5], in0=mch(5), in1=gv(3), op=ALU.mult), reads=["modT", "colsT"],
                  writes=["modv5"])
            P.flush()
        if stop_after == "0":
            dbg = dout("dbg_modv", [128, 192])
            dbg2 = dout("dbg_lbv", [128, 32])
            P.dma("sp", lambda e: e.dma_start(out=dbg, in_=modv[:].rearrange("p a b c -> p (a b c)")), key="dbg")
            P.dma("sp", lambda e: e.dma_start(out=dbg2, in_=lbv[:].rearrange("p a b -> p (a b)")), key="dbg")
            P.flush()
            return nc

        P.discard = (part == 2)
        with contextlib.ExitStack() as ph:
            hT = sbt(ph, "hT", [128, KT, NT], BF16)
            wsl = [sbt(ph, "was%d" % i, [128, KT, 512], BF16) for i in range(3)]
            wring = Ring("wsl", wsl)
            mmps = Ring("mmps", [pst(ph, "mmps%d" % i, [128, 512]) for i in range(5)])
            win = w_in.rearrange("(kt p) n -> p kt n", p=128)
            with contextlib.ExitStack() as a1:
                xr = Ring("xt", [sbt(a1, "xt%d" % i, [128, D]) for i in range(2)])
                xnr = Ring("xn", [sbt(a1, "xn%d" % i, [128, D]) for i in range(2)])
                junk = sbt(a1, "junk", [128, D], BF16)
                ssr = Ring("ss", [sbt(a1, "ss%d" % i, [128, 2]) for i in range(2)])
                tpr = Ring("tpa", [pst(a1, "tpa%d" % i, [128, 512]) for i in range(2)])
                for tt in range(NTT):
                    c = 0 if tt < 8 else 1
                    xt, xres = xr.next()
                    xn, xnres = xnr.next()
                    ss, ssres = ssr.next()
                    P.dma("sp", lambda e, xt=xt, tt=tt: e.dma_start(out=xt[:], in_=x_all[tt * 128:(tt + 1) * 128, :]),
                          writes=[xres], key=xres)
                    P.act(lambda e, xt=xt, ss=ss: e.activation(out=junk[:], in_=xt[:], func=AF.Square, accum_out=ss[:, 0:1]),
                          reads=[xres], writes=["junk", ssres])
                    P.act(lambda e, ss=ss: e.activation(out=ss[:, 1:2], in_=ss[:, 0:1], func=AF.Ln, scale=1.0 / D, bias=epst[:]),
                          reads=[ssres, "epst"], writes=[ssres])
                    P.act(lambda e, ss=ss: e.activation(out=ss[:, 1:2], in_=ss[:, 1:2], func=AF.Exp, scale=-0.5),
                          reads=[ssres], writes=[ssres])
                    P.dve(lambda e, xt=xt, xn=xn, ss=ss: e.tensor_scalar(out=xn[:], in0=xt[:], scalar1=ss[:, 1:2], scalar2=None,
                                                                         op0=ALU.mult), reads=[xres, ssres], writes=[xnres])

                    for q4 in range(4):
                        tp, tpres = tpr.next()
                        for j in range(4):
                            kt = q4 * 4 + j
                            P.pe(lambda e, tp=tp, j=j, kt=kt, xn=xn: e.transpose(tp[:, j * 128:(j + 1) * 128],
                                                                               xn[:, kt * 128:(kt + 1) * 128], identf[:]),
                                 reads=[xnres, "identf"], writes=[tpres])
                        for j in range(4):
                            kt = q4 * 4 + j
                            dst = hT[:, kt, tt * 128:(tt + 1) * 128]
                            if True:
                                P.act(lambda e, tp=tp, j=j, kt=kt, dst=dst, c=c: e.activation(
                                    out=dst, in_=tp[:, j * 128:(j + 1) * 128], func=AF.Identity,
                                    scale=modv[:, 0, kt, c:c + 1], bias=modv[:, 1, kt, c:c + 1]),
                                    reads=[tpres, "modv0", "modv1"], writes=[("hT", tt, kt)])
                            else:
                                P.dve(lambda e, tp=tp, j=j, kt=kt, dst=dst, c=c: e.tensor_scalar(
                                    out=dst, in0=tp[:, j * 128:(j + 1) * 128], scalar1=modv[:, 0, kt, c:c + 1],
                                    scalar2=modv[:, 1, kt, c:c + 1], op0=ALU.mult, op1=ALU.add),
                                    reads=[tpres, "modv0", "modv1"], writes=[("hT", tt, kt)])
                if stop_after == "A1":
                    dbg = dout("dbg_hT", [128, KT * NT], BF16)
                    for kt_ in range(KT):
                        P.dma("sp", lambda e, kt_=kt_: e.dma_start(out=dbg[:, kt_ * NT:(kt_ + 1) * NT], in_=hT[:, kt_, :]),
                              reads=[("hT", tt, kt_) for tt in range(NTT)], key="dbg")
                    P.flush()
                    return nc

                P.flush()

            def load_w(cols):
                wt, wres = wring.next()
                off = 0
                first = True
                for c0, n in cols:
                    P.dma("pool", lambda e, wt=wt, c0=c0, n=n, off=off: e.dma_start(out=wt[:, :, off:off + n],
                                                                                  in_=win[:, :, c0:c0 + n]),
                          writes=[wres] if first else [], key=wres, war=() if first else ())
                    first = False
                    off += n
                return wt, wres

            def fm_mm(wt, wres, j, tb):
                ps, pres = mmps.next()
                for kt in range(KT):
                    P.pe(lambda e, ps=ps, wt=wt, j=j, kt=kt, tb=tb: e.matmul(
                        ps[:], lhsT=wt[:, kt, j * 128:(j + 1) * 128], rhs=hT[:, kt, tb * 512:(tb + 1) * 512],
                        start=(kt == 0), stop=(kt == KT - 1)),
                        reads=[wres] + [("hT", tb * 4 + i, kt) for i in range(4)], writes=[pres])
                return ps, pres

            def tm_mm(wt, wres, tt, n):
                ps, pres = mmps.next()
                for kt in range(KT):
                    P.pe(lambda e, ps=ps, wt=wt, kt=kt, tt=tt, n=n: e.matmul(
                        ps[:, 0:n], lhsT=hT[:, kt, tt * 128:(tt + 1) * 128], rhs=wt[:, kt, 0:n],
                        start=(kt == 0), stop=(kt == KT - 1)),
                        reads=[wres, ("hT", tt, kt)], writes=[pres])
                return ps, pres

            stg = Ring("stg", [sbt(ph, "stg%d" % i, [128, 512], BF16) for i in range(6)])

            def grp_dst(T, d, kind, h, tb):
                return T[d, kind, 2 * tb:2 * tb + 2, :, h, :].rearrange("g p t -> p g t")

            qTh = sbt(ph, "qTh", [128, NT])
            mask01 = sbt(ph, "mask01", [128, 512])
            P.pool(lambda e: e.memset(mask01[:], 1.0), writes=["mask01"])
            P.pool(lambda e: e.memset(mask01[:].rearrange("p (c j) -> p c j", j=64)[:, :, 0:1], 0.0), reads=["mask01"],
                   writes=["mask01"])
            NTMP = 2
            tmpn = ["sg", "f", "lf", "b", "t", "dm", "e1", "e2"]
            tmps = [{n: sbt(ph, "tmp_%s%d" % (n, i), [128, 512]) for n in tmpn} for i in range(NTMP)]
            smalls = [sbt(ph, "small%d" % i, [128, 3, 8]) for i in range(NTMP)]
            blk = 0
            for h in range(8):
                wt, wres = load_w([(O_HQ + h * 128, 128), (O_HFF + h * 128, 128), (O_HFB + h * 128, 128)])
                for tb in range(NTB):
                    ps, pres = fm_mm(wt, wres, 0, tb)
                    P.act(lambda e, ps=ps, tb=tb: e.activation(out=qTh[:, tb * 512:(tb + 1) * 512], in_=ps[:], func=AF.Silu),
                          reads=[pres], writes=[("qTh", tb)])
                for d in range(2):
                    col = d * 8 + h
                    for tb in range(NTB):
                        T = tmps[blk % NTMP]
                        sm = smalls[blk % NTMP]
                        R = lambda n, b=blk % NTMP: ("tmp", n, b)
                        blk += 1
                        ps, pres = fm_mm(wt, wres, 1 + d, tb)
                        P.act(lambda e, ps=ps, T=T: e.activation(out=T["sg"][:], in_=ps[:], func=AF.Sigmoid),
                              reads=[pres], writes=[R("sg")])
                        P.dve(lambda e, T=T, col=col: e.scalar_tensor_tensor(out=T["f"][:], in0=T["sg"][:], scalar=lbv[:, 1, col:col + 1],
                                                                             in1=bc(lbv[:, 0, col:col + 1], 1, 512), op0=ALU.mult,
                                                                             op1=ALU.add),
                              reads=[R("sg"), "lbv"], writes=[R("f")])
                        P.act(lambda e, T=T: e.activation(out=T["lf"][:], in_=T["f"][:], func=AF.Ln), reads=[R("f")],
                              writes=[R("lf")])
                        P.pool(lambda e, T=T: e.tensor_scalar(out=T["sg"][:], in0=T["f"][:], scalar1=-1.0, scalar2=1.0,
                                                              op0=ALU.mult, op1=ALU.add), reads=[R("f")], writes=[R("sg")])
                        P.dve(lambda e, T=T: e.tensor_tensor_scan(out=T["b"][:], data0=mask01[:], data1=T["lf"][:], initial=0.0,
                                                                  op0=ALU.mult, op1=ALU.add), reads=["mask01", R("lf")],
                              writes=[R("b")])
                        v3 = lambda t: t[:].rearrange("p (c j) -> p c j", j=64)
                        if d == 0:
                            src, rsrc, jref, js1, js2 = T["b"], R("b"), 32, 0, 63
                        else:
                            P.dve(lambda e, T=T: e.tensor_tensor(out=T["t"][:], in0=T["lf"][:], in1=T["b"][:], op=ALU.subtract),
                                  reads=[R("lf"), R("b")], writes=[R("t")])
                            src, rsrc, jref, js1, js2 = T["t"], R("t"), 31, 63, 0
                        P.dve(lambda e, T=T, src=src, jref=jref: e.tensor_tensor(
                            out=v3(T["dm"]), in0=v3(src), in1=bc(v3(src)[:, :, jref:jref + 1], 2, 64), op=ALU.subtract),
                            reads=[rsrc], writes=[R("dm")])
                        P.act(lambda e, T=T: e.activation(out=T["e1"][:], in_=T["dm"][:], func=AF.Exp), reads=[R("dm")],
                              writes=[R("e1")])
                        P.act(lambda e, T=T: e.activation(out=T["e2"][:], in_=T["dm"][:], func=AF.Exp, scale=-1.0),
                              reads=[R("dm")], writes=[R("e2")])
                        P.dve(lambda e, T=T, sm=sm, js1=js1: e.tensor_tensor(out=sm[:, 0, :], in0=v3(T["e2"])[:, :, js1],
                                                                           in1=v3(T["f"])[:, :, js1], op=ALU.mult),
                              reads=[R("e2"), R("f")], writes=[R("sm0")])
                        P.dve(lambda e, T=T, sm=sm, js2=js2: e.tensor_copy(out=sm[:, 1, :], in_=v3(T["e1"])[:, :, js2]),
                              reads=[R("e1")], writes=[R("sm1")])
                        P.dve(lambda e, sm=sm, d=d, h=h, tb=tb: e.tensor_tensor(out=dec_all[:, d, h, tb * 8:(tb + 1) * 8],
                                                                               in0=sm[:, 0, :], in1=sm[:, 1, :], op=ALU.mult),
                              reads=[R("sm0"), R("sm1")], writes=[("dec", d, h, tb)])
                        s1, r1 = stg.next()
                        P.dve(lambda e, T=T, s1=s1, tb=tb: e.tensor_tensor(out=s1[:], in0=qTh[:, tb * 512:(tb + 1) * 512],
                                                                          in1=T["e1"][:], op=ALU.mult),
                              reads=[("qTh", tb), R("e1")], writes=[r1])
                        P.dma("sp", lambda e, s1=s1, d=d, h=h, tb=tb: e.dma_start(
                            out=grp_dst(QF, d, 0, h, tb), in_=s1[:].rearrange("p (g t) -> p g t", g=2)), reads=[r1], key=r1)
                        P.pool(lambda e, T=T, sm=sm, tb=tb: e.tensor_tensor(
                            out=v3(T["lf"]), in0=qTh[:, tb * 512:(tb + 1) * 512].rearrange("p (c j) -> p c j", j=64),
                            in1=bc(sm[:, 0, :].rearrange("p (c o) -> p c o", o=1), 2, 64), op=ALU.mult),
                            reads=[("qTh", tb), R("sm0")], writes=[R("lf")])
                        s2, r2 = stg.next()
                        P.dve(lambda e, T=T, s2=s2: e.tensor_tensor(out=s2[:], in0=T["lf"][:], in1=T["e1"][:], op=ALU.mult),
                              reads=[R("lf"), R("e1")], writes=[r2])
                        P.dma("sp", lambda e, s2=s2, d=d, h=h, tb=tb: e.dma_start(
                            out=grp_dst(QF, d, 1, h, tb), in_=s2[:].rearrange("p (g t) -> p g t", g=2)), reads=[r2], key=r2)
                        s3, r3 = stg.next()
                        P.dve(lambda e, T=T, s3=s3: e.tensor_tensor(out=s3[:], in0=T["sg"][:], in1=T["e2"][:], op=ALU.mult),
                              reads=[R("sg"), R("e2")], writes=[r3])
                        P.dma("sp", lambda e, s3=s3, d=d, h=h, tb=tb: e.dma_start(
                            out=grp_dst(KF, d, 0, h, tb), in_=s3[:].rearrange("p (g t) -> p g t", g=2)), reads=[r3], key=r3)
                        P.pool(lambda e, T=T, sm=sm: e.tensor_tensor(
                            out=v3(T["b"]), in0=v3(T["sg"]), in1=bc(sm[:, 1, :].rearrange("p (c o) -> p c o", o=1), 2, 64),
                            op=ALU.mult), reads=[R("sg"), R("sm1")], writes=[R("b")])
                        s4, r4 = stg.next()
                        P.dve(lambda e, T=T, s4=s4: e.tensor_tensor(out=s4[:], in0=T["b"][:], in1=T["e2"][:], op=ALU.mult),
                              reads=[R("b"), R("e2")], writes=[r4])
                        P.dma("sp", lambda e, s4=s4, d=d, h=h, tb=tb: e.dma_start(
                            out=grp_dst(KF, d, 1, h, tb), in_=s4[:].rearrange("p (g t) -> p g t", g=2)), reads=[r4], key=r4)

            def fm_simple(c0, ntiles, dst, func, scale=1.0):
                for g0 in range(0, ntiles, 4):
                    wt, wres = load_w([(c0 + g0 * 128, 512)])
                    for j in range(4):
                        for tb in range(NTB):
                            ps, pres = fm_mm(wt, wres, j, tb)
                            s, r = stg.next()
                            P.act(lambda e, ps=ps, s=s: e.activation(out=s[:], in_=ps[:], func=func, scale=scale),
                                  reads=[pres], writes=[r])
                            row = (g0 + j) * 128
                            P.dma("sp", lambda e, s=s, row=row, tb=tb: e.dma_start(
                                out=dst[row:row + 128, tb * 512:(tb + 1) * 512], in_=s[:]), reads=[r], key=r)

            fm_simple(O_MQ, 4, MQ, AF.Identity)
            fm_simple(O_MK, 4, MKF, AF.Identity, scale=0.125)

            def tm_simple(c0, ncols, dst, func, scale=1.0):
                for g0 in range(0, ncols, 512):
                    wt, wres = load_w([(c0 + g0, 512)])
                    for tt in range(NTT):
                        ps, pres = tm_mm(wt, wres, tt, 512)
                        s, r = stg.next()
                        if func is None:
                            P.dve(lambda e, ps=ps, s=s: e.tensor_copy(out=s[:], in_=ps[:]), reads=[pres], writes=[r])
                        else:
                            P.act(lambda e, ps=ps, s=s: e.activation(out=s[:], in_=ps[:], func=func, scale=scale),
                                  reads=[pres], writes=[r])
                        P.dma("sp", lambda e, s=s, tt=tt, g0=g0: e.dma_start(
                            out=dst[tt * 128:(tt + 1) * 128, g0:g0 + 512], in_=s[:]), reads=[r], key=r)

            tm_simple(O_HI, 1024, VV, None)
            tm_simple(O_MV, 1024, MVV, None)
            tm_simple(O_MK, 512, MKT, AF.Identity, scale=0.125)
            tm_simple(O_HG, 1024, HGATE, AF.Silu)
            wt, wres = load_w([(O_MI, 32)])
            gst = Ring("gst", [sbt(ph, "gst%d" % i, [128, 32]) for i in range(2)])
            for tt in range(NTT):
                ps, pres = tm_mm(wt, wres, tt, 32)
                g, gr = gst.next()
                P.dve(lambda e, ps=ps, g=g: e.tensor_tensor(out=g[:], in0=ps[:, 0:32], in1=bif_bc[:], op=ALU.add),
                      reads=[pres, "bif_bc"], writes=[gr])
                P.act(lambda e, g=g: e.activation(out=g[:, 16:32], in_=g[:, 16:32], func=AF.Exp, scale=-1.0), reads=[gr],
                      writes=[gr])
                P.act(lambda e, g=g: e.activation(out=g[:, 16:32], in_=g[:, 16:32], func=AF.Ln, bias=1.0, scale=1.0),
                      reads=[gr], writes=[gr])
                P.dve(lambda e, g=g: e.tensor_scalar(out=g[:, 16:32], in0=g[:, 16:32], scalar1=-1.0, scalar2=None,
                                                     op0=ALU.mult), reads=[gr], writes=[gr])
                P.dma("sp", lambda e, g=g, tt=tt: e.dma_start(out=GATES[tt * 128:(tt + 1) * 128, :], in_=g[:]), reads=[gr],
                      key=gr)
            tm_simple(O_MO, 1024, MOO, AF.Sigmoid)
            fm_simple(O_GA, 16, SGA, AF.Sigmoid)
            fm_simple(O_GB, 16, SGB, AF.Sigmoid)
            P.flush()
        if stop_after == "A":
            return nc

        with contextlib.ExitStack() as ph:
            S = sbt(ph, "S", [128, 8, 128]); Sbf = sbt(ph, "Sbf", [128, 8, 128], BF16)
            C = sbt(ph, "C", [64, 8, 128]); Cbf = sbt(ph, "Cbf", [64, 8, 128], BF16)
            nst = sbt(ph, "nst", [64, 8]); nbf = sbt(ph, "nbf", [64, 8], BF16); mst = sbt(ph, "mst", [64, 8])
            nrow = sbt(ph, "nrow", [8, 64])
            NS = 2
            gt = []
            for i in range(NS):
                gt.append(dict(
                    qt=sbt(ph, "g_qt%d" % i, [128, 8, 256], BF16), qh=sbt(ph, "g_qh%d" % i, [128, 8, 256], BF16),
                    kt=sbt(ph, "g_kt%d" % i, [128, 8, 256], BF16), kh=sbt(ph, "g_kh%d" % i, [128, 8, 256], BF16),
                    vv=sbt(ph, "g_vv%d" % i, [64, 4, 1024], BF16), mvv=sbt(ph, "g_mvv%d" % i, [64, 4, 1024], BF16),
                    mkt=sbt(ph, "g_mkt%d" % i, [64, 4, 512], BF16), gat=sbt(ph, "g_gat%d" % i, [64, 4, 32]),
                    mq=sbt(ph, "g_mq%d" % i, [64, 8, 256], BF16), mkf=sbt(ph, "g_mkf%d" % i, [64, 8, 256], BF16)))
            cext = [dict(ofw=sbt(ph, "c_ofw%d" % i, [64, 2048]), hgate=sbt(ph, "c_hg%d" % i, [64, 1024], BF16),
                         moo=sbt(ph, "c_mo%d" % i, [64, 1024], BF16)) for i in range(1)]
            osbr = Ring("osb", [sbt(ph, "osb%d" % i, [64, 2048]) for i in range(2)])
            att_sb = sbt(ph, "att_sb", [64, 8, 64], BF16)
            khat_sb = sbt(ph, "khat_sb", [64, 1024], BF16)
            diag = sbt(ph, "diag", [64, 8, 64])
            masked = sbt(ph, "masked", [64, 8, 64])
            t1 = masked; Et = masked
            t2 = sbt(ph, "t2", [64, 8, 64]); Sint = t2
            PT = sbt(ph, "PT", [64, 8, 64], BF16); qTs = sbt(ph, "qTs", [64, 8, 64], BF16); wk = sbt(ph, "wk", [64, 8, 64], BF16)
            tC = sbt(ph, "tC", [64, 8, 128])
            sm = sbt(ph, "smB", [64, 16, 8])
            gsb = sbt(ph, "gsb", [64, 16])
            osum = sbt(ph, "osum", [64, 2048]); sqb = sbt(ph, "sqb", [64, 1024]); ohn = sbt(ph, "ohn", [64, 1024])
            ohb = sbt(ph, "ohb", [64, 2, 1024], BF16)
            stT = [[sbt(ph, "stT%d_%d" % (b, i), [128, 8, 256], BF16) for i in range(1)] for b in range(2)]
            psA = pst(ph, "psA", [128, 512]); psB = pst(ph, "psB", [128, 512]); psO = pst(ph, "psO", [128, 1024])
            psD = pst(ph, "psD", [128, 1024]); psU = pst(ph, "psU", [128, 512]); psS = pst(ph, "psS", [128, 512])
            psBb = psB[:].bitcast(BF16)
            SMN = ["u", "cmax", "cm", "M", "wexp", "mt", "ex", "aden", "rd", "am", "ml", "mnew", "d12a", "d12b", "so", "sl"]
            smv = {n: sm[:, i, :] for i, n in enumerate(SMN)}
            d12 = sm[:, 12:14, :]
            sosl = sm[:, 14:16, :]
            id64 = identf[0:64, 0:64]
            MQr = MQ.rearrange("(h d) t -> d h t", d=64)
            MKFr = MKF.rearrange("(h d) t -> d h t", d=64)
            gset_i = [0]
            cset_i = [0]
            stT_i = [0]

            def load_group(gi, d):
                k = gset_i[0] % NS
                gset_i[0] += 1
                G = gt[k]
                key = ("gset", k)
                tl = slice(gi * 256, (gi + 1) * 256)
                names = list(G.keys())
                srcs = dict(qt=QF[d, 0, gi], qh=QF[d, 1, gi], kt=KF[d, 0, gi], kh=KF[d, 1, gi],
                            vv=VV[tl, :].rearrange("(c p) n -> p c n", p=64), mvv=MVV[tl, :].rearrange("(c p) n -> p c n", p=64),
                            mkt=MKT[tl, :].rearrange("(c p) n -> p c n", p=64), gat=GATES[tl, :].rearrange("(c p) n -> p c n", p=64),
                            mq=MQr[:, :, tl], mkf=MKFr[:, :, tl])
                allres = [("g", n, k) for n in names]
                for i, n in enumerate(names):
                    P.dma("sp", lambda e, dst=G[n], src=srcs[n]: e.dma_start(out=dst[:], in_=src),
                          writes=[("g", n, k)], key=key, war=allres if i == 0 else ())
                return G, k

            def chunk_step(G, k, ck, d, last_of_pass, seq, bwd_out):
                ci = ck % 4
                cs = slice(ci * 64, (ci + 1) * 64)
                gr = lambda n: ("g", n, k)
                attmask = up01 if d == 0 else low01
                tri = up01 if d == 0 else low01
                cummask = lowneg if d == 0 else upneg
                emask = upneg if d == 0 else lowneg
                rn = lambda n: "up01" if n is up01 else "low01" if n is low01 else "upneg" if n is upneg else "lowneg"
                osb, osr = osbr.next()
                chain_h, chain_m = [], []
                P.defer = chain_h
                for h in range(8):
                    P.pe(lambda e, h=h: e.matmul(psA[0:64, h * 64:(h + 1) * 64], lhsT=G["kt"][:, h, cs], rhs=G["qt"][:, h, cs],
                                                 start=True, stop=True), reads=[gr("kt"), gr("qt")], writes=["psA"])
                P.dve(lambda e: e.tensor_tensor(out=att_sb[:], in0=psA[0:64, :].rearrange("p (h l) -> p h l", h=8),
                                                in1=ins_bc(attmask[:], 1, 8), op=ALU.mult), reads=["psA", rn(attmask)],
                      writes=["att_sb"])
                for h in range(8):
                    P.pe(lambda e, h=h: e.transpose(psBb[0:64, h * 128:(h + 1) * 128], G["kh"][:, h, cs], identb[:]),
                         reads=[gr("kh"), "identb"], writes=["psB"])
                P.act(lambda e: e.copy(out=khat_sb[:], in_=psBb[0:64, :]), reads=["psB"], writes=["khat_sb"])
                for h in range(8):
                    hs = slice(h * 128, (h + 1) * 128)
                    P.pe(lambda e, h=h, hs=hs: e.matmul(psO[0:64, hs], lhsT=att_sb[:, h, :], rhs=G["vv"][:, ci, hs], start=True,
                                                        stop=False), reads=["att_sb", gr("vv")], writes=["psO"])
                    P.pe(lambda e, h=h, hs=hs: e.matmul(psO[0:64, hs], lhsT=G["qh"][:, h, cs], rhs=Sbf[:, h, :], start=False,
                                                        stop=True), reads=[gr("qh"), "Sbf"], writes=["psO"])
                P.act(lambda e: e.copy(out=osb[:, 0:1024], in_=psO[0:64, :]), reads=["psO"], writes=[(osr, 0)])
                for h in range(8):
                    hs = slice(h * 128, (h + 1) * 128)
                    P.pe(lambda e, hs=hs: e.matmul(psD[:, hs], lhsT=khat_sb[:, hs], rhs=G["vv"][:, ci, hs], start=True, stop=True),
                         reads=["khat_sb", gr("vv")], writes=["psD"])
                decv = bc(dec_all[:, d, :, ck:ck + 1], 2, 128)
                P.dve(lambda e: e.tensor_tensor(out=S[:], in0=S[:], in1=decv, op=ALU.mult), reads=["S"] + [("dec", d, h, ck // 8) for h in range(8)],
                      writes=["S"])
                P.dve(lambda e: e.tensor_tensor(out=S[:], in0=S[:], in1=psD[:].rearrange("p (h v) -> p h v", h=8), op=ALU.add),
                      reads=["S", "psD"], writes=["S"])
                P.pool(lambda e: e.tensor_copy(out=Sbf[:], in_=S[:]), reads=["S"], writes=["Sbf"])
                P.defer = chain_m
                li = G["gat"][:, ci, d * 8:(d + 1) * 8]
                lfm = G["gat"][:, ci, 16 + d * 8:16 + (d + 1) * 8]
                P.pe(lambda e: e.matmul(psS[0:64, 0:8], lhsT=tri[:], rhs=lfm, start=True, stop=True), reads=[rn(tri), gr("gat")],
                     writes=["psS"])
                P.pe(lambda e: e.matmul(psS[0:64, 8:16], lhsT=ones64[:], rhs=lfm, start=True, stop=True), reads=["ones64", gr("gat")],
                     writes=["psS"])
                P.dve(lambda e: e.tensor_tensor(out=smv["u"], in0=li, in1=psS[0:64, 0:8], op=ALU.subtract), reads=[gr("gat"), "psS"],
                      writes=["u"])
                P.dve(lambda e: e.tensor_copy(out=gsb[:], in_=psS[0:64, 0:16]), reads=["psS"], writes=["gsb"])
                v1 = lambda a: a.rearrange("p (h o) -> p h o", o=1)
                P.pool(lambda e: e.tensor_tensor(out=diag[:], in0=ins_bc(id64, 1, 8), in1=bc(v1(smv["u"]), 2, 64), op=ALU.mult),
                       reads=["identf", "u"], writes=["diag"])
                P.pe(lambda e: e.matmul(psU[0:64, :], lhsT=ones64[:], rhs=diag[:].rearrange("p h s -> p (h s)"), start=True, stop=True),
                     reads=["ones64", "diag"], writes=["psU"])
                psU3 = psU[0:64, :].rearrange("p (h s) -> p h s", h=8)
                P.dve(lambda e: e.tensor_reduce(out=smv["cmax"], in_=psU3, axis=AX.X, op=ALU.max), reads=["psU"], writes=["cmax"])
                P.dve(lambda e: e.tensor_tensor(out=masked[:], in0=psU3, in1=ins_bc(cummask[:], 1, 8), op=ALU.add),
                      reads=["psU", rn(cummask)], writes=["masked"])
                P.dve(lambda e: e.tensor_reduce(out=smv["cm"], in_=masked[:], axis=AX.X, op=ALU.max), reads=["masked"], writes=["cm"])
                P.dve(lambda e: e.tensor_tensor(out=smv["M"], in0=smv["cm"], in1=mst[:], op=ALU.max), reads=["cm", "mst"], writes=["M"])
                P.pool(lambda e: e.tensor_tensor(out=diag[:], in0=ins_bc(id64, 1, 8), in1=bc(v1(smv["M"]), 2, 64), op=ALU.mult),
                       reads=["identf", "M", "diag"], writes=["diag"])
                P.pe(lambda e: e.matmul(psU[0:64, :], lhsT=ones64[:], rhs=diag[:].rearrange("p h s -> p (h s)"), start=True, stop=True),
                     reads=["ones64", "diag"], writes=["psU"])
                P.dve(lambda e: e.tensor_tensor(out=t1[:], in0=bc(v1(smv["u"]), 2, 64), in1=psU3, op=ALU.subtract), reads=["u", "psU", "masked"],
                      writes=["masked"])
                P.dve(lambda e: e.tensor_tensor(out=t1[:], in0=t1[:], in1=ins_bc(emask[:], 1, 8), op=ALU.add), reads=["masked", rn(emask)],
                      writes=["masked"])
                P.act(lambda e: e.activation(out=Et[:], in_=t1[:], func=AF.Exp), reads=["masked"], writes=["masked"])
                P.dve(lambda e: e.tensor_tensor(out=t2[:], in0=bc(v1(mst[:]), 2, 64), in1=psU3, op=ALU.subtract), reads=["mst", "psU"],
                      writes=["t2"])
                P.act(lambda e: e.activation(out=Sint[:], in_=t2[:], func=AF.Exp), reads=["t2"], writes=["t2"])
                for h in range(8):
                    P.pe(lambda e, h=h: e.matmul(psA[0:64, h * 64:(h + 1) * 64], lhsT=G["mkf"][:, h, cs], rhs=G["mq"][:, h, cs],
                                                 start=True, stop=True), reads=[gr("mkf"), gr("mq")], writes=["psA"])
                P.dve(lambda e: e.tensor_tensor(out=PT[:], in0=psA[0:64, :].rearrange("p (h l) -> p h l", h=8), in1=Et[:], op=ALU.mult),
                      reads=["psA", "masked"], writes=["PT"])
                P.dve(lambda e: e.tensor_tensor(out=qTs[:], in0=G["mq"][:, :, cs], in1=Sint[:], op=ALU.mult), reads=[gr("mq"), "t2"],
                      writes=["qTs"])
                P.dve(lambda e: e.tensor_tensor(out=smv["wexp"], in0=smv["u"], in1=smv["cmax"], op=ALU.subtract), reads=["u", "cmax"],
                      writes=["wexp"])
                P.act(lambda e: e.activation(out=smv["wexp"], in_=smv["wexp"], func=AF.Exp), reads=["wexp"], writes=["wexp"])
                P.dve(lambda e: e.tensor_tensor(out=wk[:], in0=G["mkt"][:, ci, :].rearrange("p (h x) -> p h x", h=8),
                                                in1=bc(v1(smv["wexp"]), 2, 64), op=ALU.mult), reads=[gr("mkt"), "wexp"], writes=["wk"])
                for h in range(8):
                    hs = slice(h * 128, (h + 1) * 128)
                    P.pe(lambda e, h=h, hs=hs: e.matmul(psO[0:64, hs], lhsT=PT[:, h, :], rhs=G["mvv"][:, ci, hs], start=True, stop=False),
                         reads=["PT", gr("mvv")], writes=["psO"])
                    P.pe(lambda e, h=h, hs=hs: e.matmul(psO[0:64, hs], lhsT=qTs[:, h, :], rhs=Cbf[:, h, :], start=False, stop=True),
                         reads=["qTs", "Cbf"], writes=["psO"])
                    P.pe(lambda e, h=h: e.matmul(psS[0:64, 16 + h:17 + h], lhsT=PT[:, h, :], rhs=onesb[:, 0:1], start=True, stop=False),
                         reads=["PT", "onesb"], writes=["psS"])
                    P.pe(lambda e, h=h: e.matmul(psS[0:64, 16 + h:17 + h], lhsT=qTs[:, h, :], rhs=nbf[:, h:h + 1], start=False, stop=True),
                         reads=["qTs", "nbf"], writes=["psS"])
                for h in range(8):
                    hs = slice(h * 128, (h + 1) * 128)
                    P.pe(lambda e, h=h, hs=hs: e.matmul(psD[0:64, hs], lhsT=wk[:, h, :], rhs=G["mvv"][:, ci, hs], start=True, stop=True),
                         reads=["wk", gr("mvv")], writes=["psD"])
                    P.pe(lambda e, h=h: e.matmul(psS[0:64, 24 + h:25 + h], lhsT=wk[:, h, :], rhs=onesb[:, 0:1], start=True, stop=True),
                         reads=["wk", "onesb"], writes=["psS"])
                P.dve(lambda e: e.tensor_tensor(out=smv["mt"], in0=gsb[:, 0:8], in1=smv["M"], op=ALU.add), reads=["gsb", "M"], writes=["mt"])
                P.act(lambda e: e.activation(out=smv["ex"], in_=smv["mt"], func=AF.Exp, scale=-1.0), reads=["mt"], writes=["ex"])
                P.act(lambda e: e.activation(out=smv["aden"], in_=psS[0:64, 16:24], func=AF.Abs), reads=["psS"], writes=["aden"])
                P.dve(lambda e: e.tensor_tensor(out=smv["rd"], in0=smv["aden"], in1=smv["ex"], op=ALU.max), reads=["aden", "ex"], writes=["rd"])
                P.dve(lambda e: e.reciprocal(out=smv["rd"], in_=smv["rd"]), reads=["rd"], writes=["rd"])
                P.dve(lambda e: e.tensor_tensor(out=osb[:, 1024:2048].rearrange("p (h v) -> p h v", h=8),
                                                in0=psO[0:64, :].rearrange("p (h v) -> p h v", h=8), in1=bc(v1(smv["rd"]), 2, 128),
                                                op=ALU.mult), reads=["psO", "rd"], writes=[(osr, 1)])
                P.dve(lambda e: e.tensor_tensor(out=smv["am"], in0=gsb[:, 8:16], in1=mst[:], op=ALU.add), reads=["gsb", "mst"], writes=["am"])
                P.dve(lambda e: e.tensor_tensor(out=smv["ml"], in0=gsb[:, 8:16], in1=smv["cmax"], op=ALU.add), reads=["gsb", "cmax"],
                      writes=["ml"])
                P.dve(lambda e: e.tensor_tensor(out=smv["mnew"], in0=smv["am"], in1=smv["ml"], op=ALU.max), reads=["am", "ml"],
                      writes=["mnew"])
                P.dve(lambda e: e.tensor_tensor(out=smv["d12a"], in0=smv["am"], in1=smv["mnew"], op=ALU.subtract), reads=["am", "mnew"],
                      writes=["d12a"])
                P.dve(lambda e: e.tensor_tensor(out=smv["d12b"], in0=smv["ml"], in1=smv["mnew"], op=ALU.subtract), reads=["ml", "mnew"],
                      writes=["d12b"])
                P.act(lambda e: e.activation(out=sosl, in_=d12, func=AF.Exp), reads=["d12a", "d12b"], writes=["sosl"])
                P.dve(lambda e: e.tensor_copy(out=mst[:], in_=smv["mnew"]), reads=["mnew"], writes=["mst"])
                P.dve(lambda e: e.tensor_tensor(out=tC[:], in0=psD[0:64, :].rearrange("p (h v) -> p h v", h=8),
                                                in1=bc(v1(smv["sl"]), 2, 128), op=ALU.mult), reads=["psD", "sosl"], writes=["tC"])
                P.dve(lambda e: e.tensor_tensor(out=C[:], in0=C[:], in1=bc(v1(smv["so"]), 2, 128), op=ALU.mult), reads=["C", "sosl"],
                      writes=["C"])
                P.dve(lambda e: e.tensor_tensor(out=C[:], in0=C[:], in1=tC[:], op=ALU.add), reads=["C", "tC"], writes=["C"])
                P.pool(lambda e: e.tensor_copy(out=Cbf[:], in_=C[:]), reads=["C"], writes=["Cbf"])
                P.dve(lambda e: e.tensor_tensor(out=nst[:], in0=nst[:], in1=smv["so"], op=ALU.mult), reads=["nst", "sosl"], writes=["nst"])
                P.dve(lambda e: e.tensor_tensor(out=smv["aden"], in0=psS[0:64, 24:32], in1=smv["sl"], op=ALU.mult),
                      reads=["psS", "sosl", "aden"], writes=["aden"])
                P.dve(lambda e: e.tensor_tensor(out=nst[:], in0=nst[:], in1=smv["aden"], op=ALU.add), reads=["nst", "aden"], writes=["nst"])
                P.pool(lambda e: e.tensor_copy(out=nbf[:], in_=nst[:]), reads=["nst"], writes=["nbf"])
                P.defer = None
                P.replay_interleaved(chain_m, chain_h, ("psA", "psO", "psD", "psB", "psS", "psU"))
                tok0 = ck * 64
                if not bwd_out:
                    P.dma("sp", lambda e: e.dma_start(out=OFW[tok0:tok0 + 64, :], in_=osb[:]), reads=[(osr, 0), (osr, 1)],
                          writes=[("OFW", ck)], key=osr)
                    return
                X = cext[0]
                xk = ("cext", 0)
                cset_i[0] += 1
                P.dma("sp", lambda e: e.dma_start(out=X["ofw"][:], in_=OFW[tok0:tok0 + 64, :]), reads=[("OFW", ck)], writes=[(xk, "ofw")],
                      key=xk, war=[(xk, "hgate"), (xk, "moo")])
                P.dma("sp", lambda e: e.dma_start(out=X["hgate"][:], in_=HGATE[tok0:tok0 + 64, :]), writes=[(xk, "hgate")], key=xk)
                P.dma("sp", lambda e: e.dma_start(out=X["moo"][:], in_=MOO[tok0:tok0 + 64, :]), writes=[(xk, "moo")], key=xk)
                P.dve(lambda e: e.tensor_tensor(out=osum[:], in0=osb[:], in1=X["ofw"][:], op=ALU.add), reads=[(osr, 0), (osr, 1), (xk, "ofw")],
                      writes=["osum"])
                o3 = lambda a: a.rearrange("p (h v) -> p h v", h=8)
                P.pool(lambda e: e.tensor_tensor(out=sqb[:], in0=osum[:, 0:1024], in1=osum[:, 0:1024], op=ALU.mult), reads=["osum"],
                       writes=["sqb"])
                P.dve(lambda e: e.tensor_reduce(out=smv["am"], in_=o3(sqb[:]), axis=AX.X, op=ALU.add), reads=["sqb"], writes=["am"])
                P.act(lambda e: e.activation(out=smv["am"], in_=smv["am"], func=AF.Ln, scale=1.0 / 128, bias=epst[0:64, :]), reads=["am", "epst"],
                      writes=["am"])
                P.act(lambda e: e.activation(out=smv["am"], in_=smv["am"], func=AF.Exp, scale=-0.5), reads=["am"], writes=["am"])
                P.dve(lambda e: e.tensor_tensor(out=o3(ohn[:]), in0=o3(osum[:, 0:1024]), in1=bc(v1(smv["am"]), 2, 128), op=ALU.mult),
                      reads=["osum", "am"], writes=["ohn"])
                P.dve(lambda e: e.tensor_tensor(out=o3(ohn[:]), in0=o3(ohn[:]), in1=ins_bc(hgg_bc[:], 1, 8), op=ALU.mult),
                      reads=["ohn", "hgg_bc"], writes=["ohn"])
                P.dve(lambda e: e.tensor_tensor(out=ohb[:, 0, :], in0=ohn[:], in1=X["hgate"][:], op=ALU.mult), reads=["ohn", (xk, "hgate")],
                      writes=["ohb0"])
                P.dve(lambda e: e.tensor_reduce(out=smv["ml"], in_=o3(osum[:, 1024:2048]), axis=AX.X, op=ALU.add), reads=["osum"],
                      writes=["ml"])
                P.dve(lambda e: e.tensor_scalar(out=smv["ml"], in0=smv["ml"], scalar1=1.0 / 128, scalar2=None, op0=ALU.mult), reads=["ml"],
                      writes=["ml"])
                P.dve(lambda e: e.tensor_tensor(out=o3(ohn[:]), in0=o3(osum[:, 1024:2048]), in1=bc(v1(smv["ml"]), 2, 128), op=ALU.subtract),
                      reads=["osum", "ml", "ohn"], writes=["ohn"])
                P.pool(lambda e: e.tensor_tensor(out=sqb[:], in0=ohn[:], in1=ohn[:], op=ALU.mult), reads=["ohn", "sqb"], writes=["sqb"])
                P.dve(lambda e: e.tensor_reduce(out=smv["mnew"], in_=o3(sqb[:]), axis=AX.X, op=ALU.add), reads=["sqb"], writes=["mnew"])
                P.act(lambda e: e.activation(out=smv["mnew"], in_=smv["mnew"], func=AF.Ln, scale=1.0 / 128, bias=epst[0:64, :]),
                      reads=["mnew", "epst"], writes=["mnew"])
                P.act(lambda e: e.activation(out=smv["mnew"], in_=smv["mnew"], func=AF.Exp, scale=-0.5), reads=["mnew"], writes=["mnew"])
                P.dve(lambda e: e.tensor_tensor(out=o3(ohn[:]), in0=o3(ohn[:]), in1=bc(v1(smv["mnew"]), 2, 128), op=ALU.mult),
                      reads=["ohn", "mnew"], writes=["ohn"])
                P.dve(lambda e: e.tensor_tensor(out=ohn[:], in0=ohn[:], in1=mlg_bc[:], op=ALU.mult), reads=["ohn", "mlg_bc"], writes=["ohn"])
                P.dve(lambda e: e.tensor_tensor(out=ohb[:, 1, :], in0=ohn[:], in1=X["moo"][:], op=ALU.mult), reads=["ohn", (xk, "moo")],
                      writes=["ohb1"])
                si = 0
                for b, dst in ((0, OHG), (1, OML)):
                    psT = psBb[:, 0:512].rearrange("p (h t) -> p h t", h=8)
                    for h in range(8):
                        P.pe(lambda e, h=h, b=b: e.transpose(psBb[:, h * 64:(h + 1) * 64], ohb[:, b, h * 128:(h + 1) * 128], identb[0:64, 0:64]),
                             reads=["ohb%d" % b, "identb"], writes=["psB"])
                    P.act(lambda e, b=b, psT=psT: e.copy(out=stT[b][si][:, :, cs], in_=psT), reads=["psB"], writes=[("stT", b, si)])
                    if last_of_pass:
                        g = ck // 4
                        P.dma("sp", lambda e, b=b, dst=dst, g=g: e.dma_start(out=dst[:, :, g * 256:(g + 1) * 256], in_=stT[b][si][:]),
                              reads=[("stT", b, si)], key=("stT", b, si))
                if last_of_pass:
                    stT_i[0] += 1

            for (c0, nchk, is_prompt, li_) in SEQS:
                for d in range(2):
                    if is_prompt:
                        P.pool(lambda e: e.memset(S[:], 0.0), writes=["S"])
                        P.pool(lambda e: e.memset(Sbf[:], 0.0), writes=["Sbf"])
                        P.pool(lambda e: e.memset(C[:], 0.0), writes=["C"])
                        P.pool(lambda e: e.memset(Cbf[:], 0.0), writes=["Cbf"])
                        P.pool(lambda e: e.memset(nst[:], 0.0), writes=["nst"])
                        P.pool(lambda e: e.memset(nbf[:], 0.0), writes=["nbf"])
                        P.pool(lambda e: e.memset(mst[:], 0.0), writes=["mst"])
                    else:
                        P.dma("sp", lambda e, d=d: e.dma_start(out=S[:], in_=st_s[d].rearrange("h k v -> k h v")), writes=["S"], key="st0")
                        P.dma("sp", lambda e, d=d: e.dma_start(out=C[:], in_=st_c[d].rearrange("h k v -> k h v")), writes=["C"], key="st0")
                        P.dma("sp", lambda e, d=d: e.dma_start(out=nrow[:], in_=st_n[d]), writes=["nrow"], key="st0")
                        P.dma("sp", lambda e, d=d: e.dma_start(out=mst[:], in_=bass.AP(st_m.tensor, st_m.offset + d * 8, [[0, 64], [1, 8]])),
                              writes=["mst"], key="st0")
                        P.pe(lambda e: e.transpose(psS[0:64, 32:40], nrow[:], identf[0:8, 0:8]), reads=["nrow", "identf"], writes=["psS"])
                        P.dve(lambda e: e.tensor_copy(out=nst[:], in_=psS[0:64, 32:40]), reads=["psS"], writes=["nst"])
                        P.pool(lambda e: e.tensor_copy(out=Sbf[:], in_=S[:]), reads=["S"], writes=["Sbf"])
                        P.pool(lambda e: e.tensor_copy(out=Cbf[:], in_=C[:]), reads=["C"], writes=["Cbf"])
                        P.pool(lambda e: e.tensor_copy(out=nbf[:], in_=nst[:]), reads=["nst"], writes=["nbf"])
                    order = list(range(c0, c0 + nchk)) if d == 0 else list(range(c0 + nchk - 1, c0 - 1, -1))
                    G = None
                    cur_g = None
                    for i, ck in enumerate(order):
                        if ck // 4 != cur_g:
                            cur_g = ck // 4
                            G, k = load_group(cur_g, d)
                        last_in_grp = (i == len(order) - 1) or (order[i + 1] // 4 != cur_g)
                        chunk_step(G, k, ck, d, last_in_grp, li_, d == 1)
                    if is_prompt:
                        b = li_
                        P.dma("sp", lambda e, b=b, d=d: e.dma_start(out=o_s[b, d].rearrange("h k v -> k h v"), in_=S[:]), reads=["S"],
                              key="fin")
                        P.dma("sp", lambda e, b=b, d=d: e.dma_start(out=o_c[b, d].rearrange("h k v -> k h v"), in_=C[:]), reads=["C"],
                              key="fin")
                        P.pe(lambda e: e.transpose(psS[0:8, 64:128], nst[:], id64), reads=["nst", "identf"], writes=["psS"])
                        P.dve(lambda e: e.tensor_copy(out=nrow[:], in_=psS[0:8, 64:128]), reads=["psS"], writes=["nrow"])
                        P.dma("sp", lambda e, b=b, d=d: e.dma_start(out=o_n[b, d], in_=nrow[:]), reads=["nrow"], key="fin")
                        P.dma("sp", lambda e, b=b, d=d: e.dma_start(out=o_m[b:b + 1, d * 8:(d + 1) * 8], in_=mst[0:1, :]), reads=["mst"],
                              key="fin")
            P.flush()
        if stop_after == "B" or part == 1:
            return nc
        P.discard = False

        with contextlib.ExitStack() as ph:
            actA = sbt(ph, "actA", [128, 16, 512], BF16)
            big = sbt(ph, "big", [128, 44, 512], BF16)
            h2T = sbt(ph, "h2T", [128, 16, 512], BF16)
            hbr = Ring("hb", [sbt(ph, "hb%d" % i, [128, D], BF16) for i in range(2)])
            wsl = [sbt(ph, "wcs%d" % i, [128, KT, 512], BF16) for i in range(2)]
            wring = Ring("wsl", wsl)
            sgr = Ring("sg", [sbt(ph, "sgt%d" % i, [128, 2, 512], BF16) for i in range(2)])
            xr = Ring("xc", [sbt(ph, "xc%d" % i, [128, D]) for i in range(2)])
            yblk = sbt(ph, "yblk", [128, 4, D])
            Gbc = sbt(ph, "Gbc", [128, D])
            tmpb = Ring("tmpb", [sbt(ph, "tmpb%d" % i, [128, 128]) for i in range(2)])
            junkc = sbt(ph, "junkc", [128, 512], BF16)
            junk2 = sbt(ph, "junk2", [128, D], BF16)
            m12 = Ring("m12", [sbt(ph, "m12_%d" % i, [128, 2, 512]) for i in range(2)])
            ssp = sbt(ph, "ssp", [128, 4, 4])
            ssv = sbt(ph, "ssv", [128, 4, 4])
            mmr = Ring("mmc", [pst(ph, "mmc%d" % i, [128, 512]) for i in range(3)])
            tpcr = Ring("tpc", [pst(ph, "tpc%d" % i, [128, 512]) for i in range(1)])
            big4 = pst(ph, "big4", [128, 4, 512])
            wuh = w_up_hg.rearrange("(kt p) n -> p kt n", p=128)
            wum = w_up_ml.rearrange("(kt p) n -> p kt n", p=128)
            wo = w_out.rearrange("(kt p) n -> p kt n", p=128)
            wfi = w_ffn_in.rearrange("(kt p) n -> p kt n", p=128)
            wfo = w_ffn_out.rearrange("(kt p) n -> p kt n", p=128)

            def load_slot(pieces):
                wt, wres = wring.next()
                first = True
                for (src, k0, nk, c0, ncol) in pieces:
                    P.dma("pool", lambda e, wt=wt, src=src, k0=k0, nk=nk, c0=c0, ncol=ncol: e.dma_start(
                        out=wt[:, k0:k0 + nk, c0:c0 + ncol], in_=src), writes=[wres] if first else [], key=wres)
                    first = False
                return wt, wres

            def make_gbc(q, c):
                for q4 in range(4):
                    ps, pres = mmr.next()
                    for j in range(4):
                        kt = q4 * 4 + j
                        tb_, tres = tmpb.next()
                        P.dve(lambda e, tb_=tb_, kt=kt: e.tensor_copy(out=tb_[:], in_=bc(modv[:, q, kt, c:c + 1], 1, 128)),
                              reads=["modv%d" % q], writes=[tres])
                        P.pe(lambda e, ps=ps, tb_=tb_, j=j: e.matmul(ps[:, j * 128:(j + 1) * 128], lhsT=tb_[:], rhs=identf[:], start=True,
                                                                    stop=True), reads=[tres, "identf"], writes=[pres])
                    P.act(lambda e, ps=ps, q4=q4: e.copy(out=Gbc[:, q4 * 512:(q4 + 1) * 512], in_=ps[:]), reads=[pres], writes=["Gbc"])

            def rstd_from(ssap, outap, res_in, res_out):
                P.act(lambda e: e.activation(out=outap, in_=ssap, func=AF.Ln, scale=1.0 / D, bias=epst[:]), reads=[res_in, "epst"],
                      writes=[res_out])
                P.act(lambda e: e.activation(out=outap, in_=outap, func=AF.Exp, scale=-0.5), reads=[res_out], writes=[res_out])

            _cstop = 9
            _ntb = NTB
            for tb in range(_ntb):
                c = 0 if tb < 2 else 1
                tsl = slice(tb * 512, (tb + 1) * 512)
                P.dma("sp", lambda e, tsl=tsl: e.dma_start(out=actA[:, 0:8, :], in_=OHG[:, :, tsl]),
                      writes=[("actA", kt) for kt in range(8)], key="actA")
                P.dma("sp", lambda e, tsl=tsl: e.dma_start(out=actA[:, 8:16, :], in_=OML[:, :, tsl]),
                      writes=[("actA", kt) for kt in range(8, 16)], key="actA")
                for jb in range(4):
                    wt, wres = load_slot([(wuh[:, :, jb * 512:(jb + 1) * 512], 0, 8, 0, 512),
                                          (wum[:, :, jb * 512:(jb + 1) * 512], 8, 8, 0, 512)])
                    for jj in range(4):
                        j = jb * 4 + jj
                        ps1, pr1 = mmr.next()
                        ps2, pr2 = mmr.next()
                        for kt in range(8):
                            P.pe(lambda e, ps1=ps1, wt=wt, kt=kt, jj=jj: e.matmul(ps1[:], lhsT=wt[:, kt, jj * 128:(jj + 1) * 128],
                                                                                rhs=actA[:, kt, :], start=(kt == 0), stop=(kt == 7)),
                                 reads=[wres, ("actA", kt)], writes=[pr1])
                        for kt in range(8, 16):
                            P.pe(lambda e, ps2=ps2, wt=wt, kt=kt, jj=jj: e.matmul(ps2[:], lhsT=wt[:, kt, jj * 128:(jj + 1) * 128],
                                                                                rhs=actA[:, kt, :], start=(kt == 8), stop=(kt == 15)),
                                 reads=[wres, ("actA", kt)], writes=[pr2])
                        sg, sgres = sgr.next()
                        P.dma("sp", lambda e, sg=sg, j=j, tsl=tsl: e.dma_start(out=sg[:, 0, :], in_=SGA[j * 128:(j + 1) * 128, tsl]),
                              writes=[sgres], key=sgres)
                        P.dma("sp", lambda e, sg=sg, j=j, tsl=tsl: e.dma_start(out=sg[:, 1, :], in_=SGB[j * 128:(j + 1) * 128, tsl]),
                              key=sgres)
                        mm, mres = m12.next()
                        P.dve(lambda e, mm=mm, ps1=ps1, sg=sg: e.tensor_tensor(out=mm[:, 0, :], in0=ps1[:], in1=sg[:, 0, :], op=ALU.mult),
                              reads=[pr1, sgres], writes=[(mres, 0)])
                        P.dve(lambda e, mm=mm, ps2=ps2, sg=sg: e.tensor_tensor(out=mm[:, 1, :], in0=ps2[:], in1=sg[:, 1, :], op=ALU.mult),
                              reads=[pr2, sgres], writes=[(mres, 1)])
                        P.pool(lambda e, mm=mm, j=j: e.tensor_tensor(out=big[:, j, :], in0=mm[:, 0, :], in1=mm[:, 1, :], op=ALU.add),
                               reads=[(mres, 0), (mres, 1)], writes=[("big", j)])
                if _cstop < 1.2:
                    continue
                make_gbc(2, c)
                if _cstop < 1.5:
                    continue
                for cb in range(4):
                    wt, wres = load_slot([(wo[:, :, cb * 512:(cb + 1) * 512], 0, 16, 0, 512)])
                    for tt in range(4):
                        ps, pres = mmr.next()
                        for kt in range(KT):
                            P.pe(lambda e, ps=ps, wt=wt, kt=kt, tt=tt: e.matmul(ps[:], lhsT=big[:, kt, tt * 128:(tt + 1) * 128],
                                                                              rhs=wt[:, kt, :], start=(kt == 0), stop=(kt == KT - 1)),
                                 reads=[wres, ("big", kt)], writes=[pres])
                        P.dve(lambda e, ps=ps, tt=tt, cb=cb: e.tensor_copy(out=yblk[:, tt, cb * 512:(cb + 1) * 512], in_=ps[:]),
                              reads=[pres], writes=[("yblk", tt, cb)])
                if _cstop < 1.8:
                    continue
                for tt in range(4):
                    tok0 = tb * 512 + tt * 128
                    yres = [("yblk", tt, cb) for cb in range(4)]
                    P.act(lambda e, tt=tt: e.activation(out=junk2[:], in_=yblk[:, tt, :], func=AF.Square, accum_out=ssv[:, tt, 0:1]),
                          reads=yres, writes=["junk2", ("ssv", tt, 0)])
                    rstd_from(ssv[:, tt, 0:1], ssv[:, tt, 1:2], ("ssv", tt, 0), ("ssv", tt, 1))
                    xt, xres = xr.next()
                    P.dma("sp", lambda e, xt=xt, tok0=tok0: e.dma_start(out=xt[:], in_=x_all[tok0:tok0 + 128, :]), writes=[xres], key=xres)
                    P.dve(lambda e, tt=tt: e.tensor_scalar(out=yblk[:, tt, :], in0=yblk[:, tt, :], scalar1=ssv[:, tt, 1:2], scalar2=None,
                                                           op0=ALU.mult), reads=yres + [("ssv", tt, 1)], writes=yres)
                    P.dve(lambda e, tt=tt: e.tensor_tensor(out=yblk[:, tt, :], in0=yblk[:, tt, :], in1=Gbc[:], op=ALU.mult),
                          reads=yres + ["Gbc"], writes=yres)
                    P.dve(lambda e, xt=xt, tt=tt: e.tensor_tensor(out=xt[:], in0=xt[:], in1=yblk[:, tt, :], op=ALU.add),
                          reads=yres + [xres], writes=[xres])
                    P.dma("sp", lambda e, xt=xt, tok0=tok0: e.dma_start(out=X1[tok0:tok0 + 128, :], in_=xt[:]), reads=[xres],
                          writes=[("X1", tb, tt)], key=xres)
                P.flush()
                if _cstop < 1.9:
                    continue
                make_gbc(3, c)
                if _cstop < 1.91:
                    P.flush()
                    continue
                for tt in range(4):
                    tok0 = tb * 512 + tt * 128
                    yres = [("yblk", tt, cb) for cb in range(4)]
                    xt, xres = xr.next()
                    P.dma("sp", lambda e, xt=xt, tok0=tok0: e.dma_start(out=xt[:], in_=X1[tok0:tok0 + 128, :]), writes=[xres], key=xres)
                    P.act(lambda e, xt=xt, tt=tt: e.activation(out=junk2[:], in_=xt[:], func=AF.Square, accum_out=ssv[:, tt, 2:3]),
                          reads=[xres], writes=["junk2", ("ssv", tt, 2)])
                    rstd_from(ssv[:, tt, 2:3], ssv[:, tt, 3:4], ("ssv", tt, 2), ("ssv", tt, 3))
                    P.dve(lambda e, xt=xt, tt=tt: e.tensor_scalar(out=yblk[:, tt, :], in0=xt[:], scalar1=ssv[:, tt, 3:4], scalar2=None,
                                                                  op0=ALU.mult), reads=[xres, ("ssv", tt, 3)], writes=yres)
                    P.dve(lambda e, tt=tt: e.tensor_tensor(out=yblk[:, tt, :], in0=yblk[:, tt, :], in1=Gbc[:], op=ALU.mult),
                          reads=yres + ["Gbc"], writes=yres)
                if _cstop < 1.92:
                    P.flush()
                    continue
                make_gbc(4, c)
                if _cstop < 1.93:
                    P.flush()
                    continue
                for tt in range(4):
                    yres = [("yblk", tt, cb) for cb in range(4)]
                    hb, hbres = hbr.next()
                    P.dve(lambda e, tt=tt, hb=hb: e.tensor_tensor(out=hb[:], in0=yblk[:, tt, :], in1=Gbc[:], op=ALU.add),
                          reads=yres + ["Gbc"], writes=[hbres])
                    if _cstop < 1.94:
                        continue
                    for q4 in range(4):
                        ps, pres = tpcr.next()
                        psb16 = ps[:].bitcast(BF16)
                        for j in range(4):
                            kt = q4 * 4 + j
                            P.pe(lambda e, psb16=psb16, j=j, kt=kt, hb=hb: e.transpose(psb16[:, j * 128:(j + 1) * 128],
                                                                                      hb[:, kt * 128:(kt + 1) * 128], identb[:]),
                                 reads=[hbres, "identb"], writes=[pres])
                        k0 = q4 * 4
                        if _cstop < 1.95:
                            continue
                        P.act(lambda e, psb16=psb16, k0=k0, tt=tt: e.copy(
                            out=h2T[:, k0:k0 + 4, tt * 128:(tt + 1) * 128], in_=psb16[:, 0:512].rearrange("p (j t) -> p j t", j=4)),
                            reads=[pres], writes=[("h2T", k0 + j) for j in range(4)])
                P.flush()
                if _cstop < 3:
                    continue
                for j2 in range(22):
                    wt, wres = load_slot([(wfi[:, :, j2 * 256:(j2 + 1) * 256], 0, 16, 0, 256),
                                          (wfi[:, :, DFF + j2 * 256:DFF + (j2 + 1) * 256], 0, 16, 256, 256)])
                    for jj in range(2):
                        j = j2 * 2 + jj
                        psa, pra = mmr.next()
                        psb, prb = mmr.next()
                        for kt in range(KT):
                            P.pe(lambda e, psa=psa, wt=wt, kt=kt, jj=jj: e.matmul(psa[:], lhsT=wt[:, kt, jj * 128:(jj + 1) * 128],
                                                                                rhs=h2T[:, kt, :], start=(kt == 0), stop=(kt == KT - 1)),
                                 reads=[wres, ("h2T", kt)], writes=[pra])
                        for kt in range(KT):
                            P.pe(lambda e, psb=psb, wt=wt, kt=kt, jj=jj: e.matmul(psb[:], lhsT=wt[:, kt, 256 + jj * 128:256 + (jj + 1) * 128],
                                                                                rhs=h2T[:, kt, :], start=(kt == 0), stop=(kt == KT - 1)),
                                 reads=[wres, ("h2T", kt)], writes=[prb])
                        mm, mres = m12.next()
                        P.act(lambda e, mm=mm, psa=psa: e.activation(out=mm[:, 0, :], in_=psa[:], func=AF.Silu), reads=[pra],
                              writes=[(mres, 0)])
                        P.dve(lambda e, mm=mm, psb=psb, j=j: e.tensor_tensor(out=big[:, j, :], in0=mm[:, 0, :], in1=psb[:], op=ALU.mult),
                              reads=[(mres, 0), prb], writes=[("big", j)])
                if _cstop < 4:
                    continue
                make_gbc(5, c)
                for cb in range(4):
                    for (k0, nk) in ((0, 16), (16, 16), (32, 12)):
                        wt, wres = load_slot([(wfo[:, k0:k0 + nk, cb * 512:(cb + 1) * 512], 0, nk, 0, 512)])
                        for tt in range(4):
                            for kk in range(nk):
                                P.pe(lambda e, wt=wt, kk=kk, k0=k0, tt=tt: e.matmul(
                                    big4[:, tt, :], lhsT=big[:, k0 + kk, tt * 128:(tt + 1) * 128], rhs=wt[:, kk, :],
                                    start=(k0 + kk == 0), stop=(k0 + kk == 43)), reads=[wres, ("big", k0 + kk)], writes=[("big4", tt)])
                    for tt in range(4):
                        P.dve(lambda e, tt=tt, cb=cb: e.tensor_copy(out=yblk[:, tt, cb * 512:(cb + 1) * 512], in_=big4[:, tt, :]),
                              reads=[("big4", tt)], writes=[("yblk", tt, cb)])
                for tt in range(4):
                    tok0 = tb * 512 + tt * 128
                    yres = [("yblk", tt, cb) for cb in range(4)]
                    P.act(lambda e, tt=tt: e.activation(out=junk2[:], in_=yblk[:, tt, :], func=AF.Square, accum_out=ssv[:, tt, 0:1]),
                          reads=yres, writes=["junk2", ("ssv", tt, 0)])
                    rstd_from(ssv[:, tt, 0:1], ssv[:, tt, 1:2], ("ssv", tt, 0), ("ssv", tt, 1))
                    xt, xres = xr.next()
                    P.dma("sp", lambda e, xt=xt, tok0=tok0: e.dma_start(out=xt[:], in_=X1[tok0:tok0 + 128, :]), reads=[("X1", tb, tt)],
                          writes=[xres], key=xres)
                    P.dve(lambda e, tt=tt: e.tensor_scalar(out=yblk[:, tt, :], in0=yblk[:, tt, :], scalar1=ssv[:, tt, 1:2], scalar2=None,
                                                           op0=ALU.mult), reads=yres + [("ssv", tt, 1)], writes=yres)
                    P.dve(lambda e, tt=tt: e.tensor_tensor(out=yblk[:, tt, :], in0=yblk[:, tt, :], in1=Gbc[:], op=ALU.mult),
                          reads=yres + ["Gbc"], writes=yres)
                    P.dve(lambda e, xt=xt, tt=tt: e.tensor_tensor(out=xt[:], in0=xt[:], in1=yblk[:, tt, :], op=ALU.add),
                          reads=yres + [xres], writes=[xres])
                    P.dma("sp", lambda e, xt=xt, tok0=tok0: e.dma_start(out=y_all[tok0:tok0 + 128, :], in_=xt[:]), reads=[xres], key=xres)
            P.flush()
        return nc


def shard_inputs(inputs):
    f = lambda a: np.ascontiguousarray(np.asarray(a, dtype=np.float32))
    xp = f(inputs["x_prompt"])
    xs = f(inputs["x_sample"])
    c = f(inputs["c"])
    c_ctx = f(inputs["c_ctx"])
    shared = {
        "w_mod": f(inputs["w_mod"])[0],
        "b_mod": f(inputs["b_mod"])[0].reshape(96, 128),
        "gvec": np.concatenate([f(inputs[k])[0].reshape(16, 128) for k in
                                ("norm_pre_mix", "norm_post_mix", "norm_pre_ffn", "norm_post_ffn")], axis=0),
        "w_in": f(inputs["w_in"])[0],
        "lb_logits": f(inputs["hgrn_lb_logits"]).reshape(2, 2, 8, 128).reshape(32, 128),
        "hg_g": f(inputs["hgrn_norm_g"])[0].reshape(1, 128),
        "b_if": np.concatenate([f(inputs["mlstm_b_i"])[0], f(inputs["mlstm_b_f"])[0]]).reshape(1, 32),
        "ml_g": f(inputs["mlstm_norm_g"])[0].reshape(1, 1024),
        "w_up_hg": f(inputs["w_up_hgrn"])[0],
        "w_up_ml": f(inputs["w_up_mlstm"])[0],
        "w_out": f(inputs["w_out"])[0],
        "w_ffn_in": f(inputs["w_ffn_in"])[0],
        "w_ffn_out": f(inputs["w_ffn_out"])[0],
    }
    maps = []
    for i in range(NCORES):
        m = dict(shared)
        m["x_all"] = np.concatenate([xp[4 * i:4 * i + 4].reshape(1024, D), xs[i]], axis=0)
        m["cond2"] = np.concatenate([c_ctx.reshape(16, 128), c[i].reshape(16, 128)], axis=0)
        m["st_s"] = f(inputs["state_hgrn_s"])[i, 0]
        m["st_c"] = f(inputs["state_mlstm_c"])[i, 0]
        m["st_n"] = f(inputs["state_mlstm_n"])[i, 0]
        m["st_m"] = f(inputs["state_mlstm_m"])[i, 0].reshape(1, 16)
        maps.append(m)
    return maps


def kernel(**inputs):
    maps = shard_inputs(inputs)
    nc = build_nc()
    rs = run_bass_kernel_spmd(nc, maps, core_ids=list(range(NCORES))).results
    rs2 = rs
    y = np.stack([r["y_all"] for r in rs2])
    y_prompt = y[:, :1024].reshape(32, 256, D)
    y_sample = y[:, 1024:]
    new_s = np.concatenate([r["o_s"] for r in rs], axis=0)[:, None]
    new_c = np.concatenate([r["o_c"] for r in rs], axis=0)[:, None]
    new_n = np.concatenate([r["o_n"] for r in rs], axis=0)[:, None]
    new_m = np.concatenate([r["o_m"].reshape(4, 2, 8) for r in rs], axis=0)[:, None]
    return (np.ascontiguousarray(y_prompt), np.ascontiguousarray(y_sample), np.ascontiguousarray(new_s),
            np.ascontiguousarray(new_c), np.ascontiguousarray(new_n), np.ascontiguousarray(new_m))
```

```python
import contextlib
import numpy as np
import concourse.bass as bass
import concourse.mybir as mybir
from concourse.bass_utils import run_bass_kernel_spmd

F32 = mybir.dt.float32
BF16 = mybir.dt.bfloat16
AF = mybir.ActivationFunctionType
ALU = mybir.AluOpType
AX = mybir.AxisListType

NCORES = 8
NT = 3072
D = 2048
KT = 16
NTT = 24
NTB = 6
NGRP = 12
NCHUNK = 48
DFF = 5632
EPS = 1e-6
NEG = -1.0e30
O_HQ, O_HFF, O_HFB, O_HI, O_HG = 0, 1024, 2048, 3072, 4096
O_MQ, O_MK, O_MV, O_MI, O_MF, O_MO, O_GA, O_GB = 5120, 5632, 6144, 7168, 7184, 7200, 8224, 10272
IN_W = 12320
SEQS = [(0, 4, True, 0), (4, 4, True, 1), (8, 4, True, 2), (12, 4, True, 3), (16, 32, False, 0)]


def bc(ap, dim, n):
    l = [list(x) for x in ap.ap]
    l[dim] = [0, n]
    return bass.AP(ap.tensor, ap.offset, l)


def ins_bc(ap, dim, n):
    l = [list(x) for x in ap.ap]
    l.insert(dim, [0, n])
    return bass.AP(ap.tensor, ap.offset, l)


class Prog:
    ENGS = ("pe", "act", "dve", "pool", "sp")

    def __init__(self, nc, stack, n_dma_sems=80):
        self.nc = nc
        self.eng_sem = {e: stack.enter_context(nc.semaphore("c_" + e)) for e in self.ENGS}
        self.dma_sems = [stack.enter_context(nc.semaphore("d%d" % i)) for i in range(n_dma_sems)]
        self.eng_cnt = {e: 0 for e in self.ENGS}
        self.sem_cnt = [0] * n_dma_sems
        self.pool_keys = {}
        self.discard = False
        self.defer = None
        self._reset()
        self.total_ops = 0

    def _reset(self):
        self.ops = []
        self.last_w = {}
        self.readers = {}
        self.key_idx = {}
        self.key_cnt = {}
        self.n_sp_keys = 0

    def add(self, eng, fn, reads=(), writes=(), dma_key=None, war=()):
        if self.discard:
            return -1
        if self.defer is not None:
            self.defer.append((eng, fn, tuple(reads), tuple(writes), dma_key, tuple(war)))
            return -1
        idx = len(self.ops)
        deps = set()
        for w in war:
            deps.update(self.readers.get(w, ()))
        for r in reads:
            if r in self.last_w:
                deps.add(self.last_w[r])
        for w in writes:
            if w in self.last_w:
                deps.add(self.last_w[w])
            deps.update(self.readers.get(w, ()))
        deps.discard(idx)
        for r in reads:
            self.readers.setdefault(r, []).append(idx)
        for w in writes:
            self.last_w[w] = idx
            self.readers[w] = []
        latest = {}
        pruned = set()
        for d in deps:
            dop = self.ops[d]
            if dop["dma_key"] is None:
                if dop["eng"] not in latest or latest[dop["eng"]] < d:
                    latest[dop["eng"]] = d
            else:
                pruned.add(d)
        pruned.update(latest.values())
        deps = pruned
        dwait = {}
        for d in deps:
            k = self.ops[d]["dma_key"]
            if k is not None:
                dwait[k] = self.key_cnt[k]
        if dma_key is not None:
            if dma_key not in self.key_idx:
                if eng == "pool":
                    if dma_key not in self.pool_keys:
                        self.pool_keys[dma_key] = len(self.dma_sems) - 1 - len(self.pool_keys)
                        assert len(self.pool_keys) <= 8
                    self.key_idx[dma_key] = self.pool_keys[dma_key]
                else:
                    self.n_sp_keys += 1
                    assert self.n_sp_keys <= len(self.dma_sems) - 8, "too many dma keys"
                    self.key_idx[dma_key] = self.n_sp_keys - 1
                self.key_cnt[dma_key] = self.sem_cnt[self.key_idx[dma_key]]
            self.key_cnt[dma_key] += 16
        self.ops.append(dict(eng=eng, fn=fn, deps=deps, dma_key=dma_key, idx=idx, dwait=dwait))
        return idx

    def replay_interleaved(self, main, other, shared):
        last_other = {}
        for i, op in enumerate(other):
            for r in op[2] + op[3]:
                if r in shared:
                    last_other[r] = i
        po = 0
        for op in main:
            need = -1
            for r in op[2] + op[3]:
                if r in last_other:
                    need = max(need, last_other[r])
            while po <= need:
                self.add(*other[po])
                po += 1
            self.add(*op)
            if po < len(other):
                self.add(*other[po])
                po += 1
        while po < len(other):
            self.add(*other[po])
            po += 1

    def pe(self, fn, reads=(), writes=()):
        return self.add("pe", fn, reads, writes)

    def act(self, fn, reads=(), writes=()):
        return self.add("act", fn, reads, writes)

    def dve(self, fn, reads=(), writes=()):
        return self.add("dve", fn, reads, writes)

    def pool(self, fn, reads=(), writes=()):
        return self.add("pool", fn, reads, writes)

    def dma(self, eng, fn, reads=(), writes=(), key=None, war=()):
        assert key is not None
        return self.add(eng, fn, reads, writes, dma_key=key, war=war)

    def flush(self):
        nc = self.nc
        ops = self.ops
        if not ops:
            return
        needed = set()
        for op in ops:
            for d in op["deps"]:
                dop = ops[d]
                if dop["dma_key"] is None and dop["eng"] == "pe" and op["eng"] == "pe" and op["dma_key"] is None:
                    continue
                needed.add(d)
        per_eng = {e: [op for op in ops if op["eng"] == e] for e in self.ENGS}
        for e in self.ENGS:
            comp = [op for op in per_eng[e] if op["dma_key"] is None]
            if comp:
                needed.add(comp[-1]["idx"])
        for op in ops:
            if op["dma_key"] is not None:
                ki = self.key_idx[op["dma_key"]]
                self.sem_cnt[ki] += 16
                op["sig"] = (self.dma_sems[ki], self.sem_cnt[ki], ("k", ki))
            elif op["idx"] in needed:
                self.eng_cnt[op["eng"]] += 1
                op["sig"] = (self.eng_sem[op["eng"]], self.eng_cnt[op["eng"]], ("e", op["eng"]))
            else:
                op["sig"] = None
        final_eng = dict(self.eng_cnt)
        final_keys = [(self.dma_sems[ki], self.sem_cnt[ki]) for ki in range(len(self.dma_sems)) if self.sem_cnt[ki] > 0]
        with nc.Block() as block:
            def make_body(e):
                def body(engine):
                    known = {}
                    for op in per_eng[e]:
                        waits = {}
                        for d in op["deps"]:
                            dop = ops[d]
                            if dop["sig"] is None:
                                continue
                            if dop["dma_key"] is None and dop["eng"] == "pe" and e == "pe" and op["dma_key"] is None:
                                continue
                            sem, val, sk = dop["sig"]
                            if dop["dma_key"] is not None:
                                val = op["dwait"][dop["dma_key"]]
                            if known.get(sk, 0) >= val:
                                continue
                            if sk not in waits or waits[sk][1] < val:
                                waits[sk] = (sem, val)
                        for sk, (sem, val) in waits.items():
                            engine.wait_ge(sem, val)
                            known[sk] = val
                        ins = op["fn"](engine)
                        if op["sig"] is not None:
                            ins.then_inc(op["sig"][0], 16 if op["dma_key"] is not None else 1)
                    for e2 in self.ENGS:
                        if final_eng[e2] > 0:
                            engine.wait_ge(self.eng_sem[e2], final_eng[e2])
                    for sem, val in final_keys:
                        engine.wait_ge(sem, val)
                return body
            block.tensor(make_body("pe"))
            block.scalar(make_body("act"))
            block.vector(make_body("dve"))
            block.gpsimd(make_body("pool"))
            block.sync(make_body("sp"))
        self.total_ops += len(ops)
        self._reset()


class Ring:
    def __init__(self, name, tiles):
        self.name = name
        self.tiles = tiles
        self.i = 0

    def next(self):
        k = self.i % len(self.tiles)
        self.i += 1
        return self.tiles[k], (self.name, k)


def build_nc(debug=False, stop_after=None, part=0):
    nc = bass.Bass("TRN2", target_bir_lowering=False)
    dbg_kind = "ExternalOutput" if debug else "Internal"

    def din(name, shape, dt=F32):
        return nc.dram_tensor(name, shape, dt, kind="ExternalInput").ap()

    def dout(name, shape, dt=F32):
        return nc.dram_tensor(name, shape, dt, kind="ExternalOutput").ap()

    def dscr(name, shape, dt=BF16):
        return nc.dram_tensor(name, shape, dt, kind=dbg_kind).ap()

    x_all = din("x_all", [NT, D])
    cond2 = din("cond2", [32, 128])
    st_s = din("st_s", [2, 8, 128, 128])
    st_c = din("st_c", [2, 8, 64, 128])
    st_n = din("st_n", [2, 8, 64])
    st_m = din("st_m", [1, 16])
    w_mod = din("w_mod", [D, 6 * D])
    b_mod = din("b_mod", [96, 128])
    gvec = din("gvec", [64, 128])
    w_in = din("w_in", [D, IN_W])
    lb_logits = din("lb_logits", [32, 128])
    hg_g = din("hg_g", [1, 128])
    b_if = din("b_if", [1, 32])
    ml_g = din("ml_g", [1, 1024])
    w_up_hg = din("w_up_hg", [1024, D])
    w_up_ml = din("w_up_ml", [1024, D])
    w_out = din("w_out", [D, D])
    w_ffn_in = din("w_ffn_in", [D, 2 * DFF])
    w_ffn_out = din("w_ffn_out", [DFF, D])

    y_all = dout("y_all", [NT, D])
    o_s = dout("o_s", [4, 2, 8, 128, 128])
    o_c = dout("o_c", [4, 2, 8, 64, 128])
    o_n = dout("o_n", [4, 2, 8, 64])
    o_m = dout("o_m", [4, 16])

    QF = dscr("QF", [2, 2, NGRP, 128, 8, 256])
    KF = dscr("KF", [2, 2, NGRP, 128, 8, 256])
    MQ = dscr("MQ", [512, NT])
    MKF = dscr("MKF", [512, NT])
    hkind = {0: dbg_kind, 1: "ExternalOutput", 2: "ExternalInput"}[part]
    SGA = nc.dram_tensor("SGA", [D, NT], BF16, kind=hkind).ap()
    SGB = nc.dram_tensor("SGB", [D, NT], BF16, kind=hkind).ap()
    VV = dscr("VV", [NT, 1024])
    HGATE = dscr("HGATE", [NT, 1024])
    MKT = dscr("MKT", [NT, 512])
    MVV = dscr("MVV", [NT, 1024])
    MOO = dscr("MOO", [NT, 1024])
    GATES = dscr("GATES", [NT, 32], F32)
    OFW = dscr("OFW", [NT, 2048], F32)
    OHG = nc.dram_tensor("OHG", [128, 8, NT], BF16, kind=hkind).ap()
    OML = nc.dram_tensor("OML", [128, 8, NT], BF16, kind=hkind).ap()
    X1 = dscr("X1", [NT, D], F32)

    with contextlib.ExitStack() as outer:
        def sbt(st, name, shape, dt=F32):
            return st.enter_context(nc.sbuf_tensor(name, shape, dt))

        def pst(st, name, shape, dt=F32):
            return st.enter_context(nc.psum_tensor(name, shape, dt))

        P = Prog(nc, outer)

        identf = sbt(outer, "identf", [128, 128])
        identb = sbt(outer, "identb", [128, 128], BF16)
        up01 = sbt(outer, "up01", [64, 64])
        low01 = sbt(outer, "low01", [64, 64])
        upneg = sbt(outer, "upneg", [64, 64])
        lowneg = sbt(outer, "lowneg", [64, 64])
        ones64 = sbt(outer, "ones64", [64, 64])
        onesb = sbt(outer, "onesb", [64, 2], BF16)
        epst = sbt(outer, "epst", [128, 1])
        modv = sbt(outer, "modv", [128, 6, 16, 2])
        lbv = sbt(outer, "lbv", [128, 2, 16])
        dec_all = sbt(outer, "dec_all", [128, 2, 8, NCHUNK])
        hgg_bc = sbt(outer, "hgg_bc", [64, 128])
        mlg_bc = sbt(outer, "mlg_bc", [64, 1024])
        bif_bc = sbt(outer, "bif_bc", [128, 32])

        P.pool(lambda e: e.memset(identf[:], 1.0), writes=["identf"])
        P.pool(lambda e: e.affine_select(out=identf[:], in_=identf[:], pattern=[[-1, 128]], compare_op=ALU.is_equal,
                                        fill=0.0, base=0, channel_multiplier=1), reads=["identf"], writes=["identf"])
        P.dve(lambda e: e.tensor_copy(out=identb[:], in_=identf[:]), reads=["identf"], writes=["identb"])
        for t, nm, val in ((up01, "up01", 1.0), (low01, "low01", 1.0), (upneg, "upneg", 0.0), (lowneg, "lowneg", 0.0),
                           (ones64, "ones64", 1.0), (onesb, "onesb", 1.0), (epst, "epst", EPS)):
            P.pool(lambda e, t=t, val=val: e.memset(t[:], val), writes=[nm])
        P.pool(lambda e: e.affine_select(out=up01[:], in_=up01[:], pattern=[[1, 64]], compare_op=ALU.is_ge, fill=0.0,
                                        base=0, channel_multiplier=-1), reads=["up01"], writes=["up01"])
        P.pool(lambda e: e.affine_select(out=upneg[:], in_=upneg[:], pattern=[[1, 64]], compare_op=ALU.is_ge, fill=NEG,
                                        base=0, channel_multiplier=-1), reads=["upneg"], writes=["upneg"])
        P.pool(lambda e: e.affine_select(out=low01[:], in_=low01[:], pattern=[[-1, 64]], compare_op=ALU.is_ge, fill=0.0,
                                        base=0, channel_multiplier=1), reads=["low01"], writes=["low01"])
        P.pool(lambda e: e.affine_select(out=lowneg[:], in_=lowneg[:], pattern=[[-1, 64]], compare_op=ALU.is_ge, fill=NEG,
                                        base=0, channel_multiplier=1), reads=["lowneg"], writes=["lowneg"])
        P.dma("sp", lambda e: e.dma_start(out=hgg_bc[:], in_=bass.AP(hg_g.tensor, hg_g.offset, [[0, 64], [1, 128]])),
              writes=["hgg_bc"], key="c0")
        P.dma("sp", lambda e: e.dma_start(out=mlg_bc[:], in_=bass.AP(ml_g.tensor, ml_g.offset, [[0, 64], [1, 1024]])),
              writes=["mlg_bc"], key="c0")
        P.dma("sp", lambda e: e.dma_start(out=bif_bc[:], in_=bass.AP(b_if.tensor, b_if.offset, [[0, 128], [1, 32]])),
              writes=["bif_bc"], key="c0")

        with contextlib.ExitStack() as ph:
            rows = sbt(ph, "rows", [128, 256])
            colsT = sbt(ph, "colsT", [128, 256])
            scT = sbt(ph, "scT", [128, 32], BF16)
            modT = sbt(ph, "modT", [128, 96, 2])
            wsl = [sbt(ph, "w0s%d" % i, [128, KT, 512], BF16) for i in range(3)]
            wring = Ring("wsl", wsl)
            tp0 = pst(ph, "tp0", [128, 512])
            modps = pst(ph, "modps", [128, 192])
            rows2 = sbt(ph, "rows2", [128, 128])
            P.dma("sp", lambda e: e.dma_start(out=rows[0:32, 0:128], in_=cond2), writes=["rows"], key="p0a")
            P.dma("sp", lambda e: e.dma_start(out=rows[32:128, 0:128], in_=b_mod), writes=["rows"], key="p0a")
            P.dma("sp", lambda e: e.dma_start(out=rows2[0:64, :], in_=gvec), writes=["rows2"], key="p0b")
            P.dma("sp", lambda e: e.dma_start(out=rows2[64:96, :], in_=lb_logits), writes=["rows2"], key="p0b")
            P.pe(lambda e: e.transpose(tp0[:, 0:128], rows[:, 0:128], identf[:]), reads=["rows", "identf"], writes=["tp0"])
            P.pe(lambda e: e.transpose(tp0[:, 128:224], rows2[0:96, :], identf[0:96, 0:96]), reads=["rows2", "identf"], writes=["tp0"])
            P.dve(lambda e: e.tensor_copy(out=colsT[:, 0:224], in_=tp0[:, 0:224]), reads=["tp0"], writes=["colsT"])
            P.act(lambda e: e.activation(out=scT[:], in_=tp0[:, 0:32], func=AF.Silu), reads=["tp0"], writes=["scT"])
            lg = colsT[:, 192:224].rearrange("p (d l h) -> p d l h", d=2, l=2)
            lb3 = lbv[:, 0, :].rearrange("p (d h) -> p d h", d=2)
            P.dve(lambda e: e.tensor_tensor(out=lb3, in0=lg[:, :, 0, :], in1=lg[:, :, 1, :], op=ALU.subtract),
                  reads=["colsT"], writes=["lbv"])
            P.act(lambda e: e.activation(out=lbv[:, 0, :], in_=lbv[:, 0, :], func=AF.Sigmoid), reads=["lbv"], writes=["lbv"])
            P.dve(lambda e: e.tensor_scalar(out=lbv[:, 1, :], in0=lbv[:, 0, :], scalar1=-1.0, scalar2=1.0, op0=ALU.mult,
                                            op1=ALU.add), reads=["lbv"], writes=["lbv"])
            wm = w_mod.rearrange("(kt p) n -> p kt n", p=128)
            for cb in range(24):
                wt, wres = wring.next()
                P.dma("pool", lambda e, wt=wt, cb=cb: e.dma_start(out=wt[:], in_=wm[:, :, cb * 512:(cb + 1) * 512]),
                      writes=[wres], key=wres)
                for j in range(4):
                    ct = cb * 4 + j
                    for kt in range(KT):
                        rhs = bass.AP(scT[:].tensor, scT[:, kt:kt + 1].offset, [list(scT[:].ap[0]), [16, 2]])
                        P.pe(lambda e, wt=wt, j=j, kt=kt, ct=ct, rhs=rhs: e.matmul(
                            modps[:, ct * 2:ct * 2 + 2], lhsT=wt[:, kt, j * 128:(j + 1) * 128], rhs=rhs,
                            start=(kt == 0), stop=(kt == KT - 1)), reads=[wres, "scT"], writes=["modps"])
            bmT = ins_bc(colsT[:, 32:128], 2, 2)
            P.dve(lambda e: e.tensor_tensor(out=modT[:], in0=modps[:].rearrange("p (c two) -> p c two", two=2), in1=bmT,
                                            op=ALU.add), reads=["modps", "colsT"], writes=["modT"])
            def gv(i):
                return ins_bc(colsT[:, 128 + 16 * i:128 + 16 * (i + 1)], 2, 2)
            def mch(q):
                return modT[:, 16 * q:16 * (q + 1), :]
            P.dve(lambda e: e.scalar_tensor_tensor(out=modv[:, 0], in0=mch(1), scalar=1.0, in1=gv(0), op0=ALU.add,
                                                   op1=ALU.mult), reads=["modT", "colsT"], writes=["modv0"])
            P.dve(lambda e: e.tensor_copy(out=modv[:, 1], in_=mch(0)), reads=["modT"], writes=["modv1"])
            P.dve(lambda e: e.tensor_tensor(out=modv[:, 2], in0=mch(2), in1=gv(1), op=ALU.mult), reads=["modT", "colsT"],
                  writes=["modv2"])
            P.dve(lambda e: e.scalar_tensor_tensor(out=modv[:, 3], in0=mch(4), scalar=1.0, in1=gv(2), op0=ALU.add,
                                                   op1=ALU.mult), reads=["modT", "colsT"], writes=["modv3"])
            P.dve(lambda e: e.tensor_copy(out=modv[:, 4], in_=mch(3)), reads=["modT"], writes=["modv4"])
            P.dve(lambda e: e.tensor_tensor(out=modv[:, 5], in0=mch(5), in1=gv(3), op=ALU.mult), reads=["modT", "colsT"],
                  writes=["modv5"])
            P.flush()
        if stop_after == "0":
            dbg = dout("dbg_modv", [128, 192])
            dbg2 = dout("dbg_lbv", [128, 32])
            P.dma("sp", lambda e: e.dma_start(out=dbg, in_=modv[:].rearrange("p a b c -> p (a b c)")), key="dbg")
            P.dma("sp", lambda e: e.dma_start(out=dbg2, in_=lbv[:].rearrange("p a b -> p (a b)")), key="dbg")
            P.flush()
            return nc

        P.discard = (part == 2)
        with contextlib.ExitStack() as ph:
            hT = sbt(ph, "hT", [128, KT, NT], BF16)
            wsl = [sbt(ph, "was%d" % i, [128, KT, 512], BF16) for i in range(3)]
            wring = Ring("wsl", wsl)
            mmps = Ring("mmps", [pst(ph, "mmps%d" % i, [128, 512]) for i in range(5)])
            win = w_in.rearrange("(kt p) n -> p kt n", p=128)
            with contextlib.ExitStack() as a1:
                xr = Ring("xt", [sbt(a1, "xt%d" % i, [128, D]) for i in range(2)])
                xnr = Ring("xn", [sbt(a1, "xn%d" % i, [128, D]) for i in range(2)])
                junk = sbt(a1, "junk", [128, D], BF16)
                ssr = Ring("ss", [sbt(a1, "ss%d" % i, [128, 2]) for i in range(2)])
                tpr = Ring("tpa", [pst(a1, "tpa%d" % i, [128, 512]) for i in range(2)])
                for tt in range(NTT):
                    c = 0 if tt < 8 else 1
                    xt, xres = xr.next()
                    xn, xnres = xnr.next()
                    ss, ssres = ssr.next()
                    P.dma("sp", lambda e, xt=xt, tt=tt: e.dma_start(out=xt[:], in_=x_all[tt * 128:(tt + 1) * 128, :]),
                          writes=[xres], key=xres)
                    P.act(lambda e, xt=xt, ss=ss: e.activation(out=junk[:], in_=xt[:], func=AF.Square, accum_out=ss[:, 0:1]),
                          reads=[xres], writes=["junk", ssres])
                    P.act(lambda e, ss=ss: e.activation(out=ss[:, 1:2], in_=ss[:, 0:1], func=AF.Ln, scale=1.0 / D, bias=epst[:]),
                          reads=[ssres, "epst"], writes=[ssres])
                    P.act(lambda e, ss=ss: e.activation(out=ss[:, 1:2], in_=ss[:, 1:2], func=AF.Exp, scale=-0.5),
                          reads=[ssres], writes=[ssres])
                    P.dve(lambda e, xt=xt, xn=xn, ss=ss: e.tensor_scalar(out=xn[:], in0=xt[:], scalar1=ss[:, 1:2], scalar2=None,
                                                                         op0=ALU.mult), reads=[xres, ssres], writes=[xnres])

                    for q4 in range(4):
                        tp, tpres = tpr.next()
                        for j in range(4):
                            kt = q4 * 4 + j
                            P.pe(lambda e, tp=tp, j=j, kt=kt, xn=xn: e.transpose(tp[:, j * 128:(j + 1) * 128],
                                                                               xn[:, kt * 128:(kt + 1) * 128], identf[:]),
                                 reads=[xnres, "identf"], writes=[tpres])
                        for j in range(4):
                            kt = q4 * 4 + j
                            dst = hT[:, kt, tt * 128:(tt + 1) * 128]
                            if True:
                                P.act(lambda e, tp=tp, j=j, kt=kt, dst=dst, c=c: e.activation(
                                    out=dst, in_=tp[:, j * 128:(j + 1) * 128], func=AF.Identity,
                                    scale=modv[:, 0, kt, c:c + 1], bias=modv[:, 1, kt, c:c + 1]),
                                    reads=[tpres, "modv0", "modv1"], writes=[("hT", tt, kt)])
                            else:
                                P.dve(lambda e, tp=tp, j=j, kt=kt, dst=dst, c=c: e.tensor_scalar(
                                    out=dst, in0=tp[:, j * 128:(j + 1) * 128], scalar1=modv[:, 0, kt, c:c + 1],
                                    scalar2=modv[:, 1, kt, c:c + 1], op0=ALU.mult, op1=ALU.add),
                                    reads=[tpres, "modv0", "modv1"], writes=[("hT", tt, kt)])
                if stop_after == "A1":
                    dbg = dout("dbg_hT", [128, KT * NT], BF16)
                    for kt_ in range(KT):
                        P.dma("sp", lambda e, kt_=kt_: e.dma_start(out=dbg[:, kt_ * NT:(kt_ + 1) * NT], in_=hT[:, kt_, :]),
                              reads=[("hT", tt, kt_) for tt in range(NTT)], key="dbg")
                    P.flush()
                    return nc

                P.flush()

            def load_w(cols):
                wt, wres = wring.next()
                off = 0
                first = True
                for c0, n in cols:
                    P.dma("pool", lambda e, wt=wt, c0=c0, n=n, off=off: e.dma_start(out=wt[:, :, off:off + n],
                                                                                  in_=win[:, :, c0:c0 + n]),
                          writes=[wres] if first else [], key=wres, war=() if first else ())
                    first = False
                    off += n
                return wt, wres

            def fm_mm(wt, wres, j, tb):
                ps, pres = mmps.next()
                for kt in range(KT):
                    P.pe(lambda e, ps=ps, wt=wt, j=j, kt=kt, tb=tb: e.matmul(
                        ps[:], lhsT=wt[:, kt, j * 128:(j + 1) * 128], rhs=hT[:, kt, tb * 512:(tb + 1) * 512],
                        start=(kt == 0), stop=(kt == KT - 1)),
                        reads=[wres] + [("hT", tb * 4 + i, kt) for i in range(4)], writes=[pres])
                return ps, pres

            def tm_mm(wt, wres, tt, n):
                ps, pres = mmps.next()
                for kt in range(KT):
                    P.pe(lambda e, ps=ps, wt=wt, kt=kt, tt=tt, n=n: e.matmul(
                        ps[:, 0:n], lhsT=hT[:, kt, tt * 128:(tt + 1) * 128], rhs=wt[:, kt, 0:n],
                        start=(kt == 0), stop=(kt == KT - 1)),
                        reads=[wres, ("hT", tt, kt)], writes=[pres])
                return ps, pres

            stg = Ring("stg", [sbt(ph, "stg%d" % i, [128, 512], BF16) for i in range(6)])

            def grp_dst(T, d, kind, h, tb):
                return T[d, kind, 2 * tb:2 * tb + 2, :, h, :].rearrange("g p t -> p g t")

            qTh = sbt(ph, "qTh", [128, NT])
            mask01 = sbt(ph, "mask01", [128, 512])
            P.pool(lambda e: e.memset(mask01[:], 1.0), writes=["mask01"])
            P.pool(lambda e: e.memset(mask01[:].rearrange("p (c j) -> p c j", j=64)[:, :, 0:1], 0.0), reads=["mask01"],
                   writes=["mask01"])
            NTMP = 2
            tmpn = ["sg", "f", "lf", "b", "t", "dm", "e1", "e2"]
            tmps = [{n: sbt(ph, "tmp_%s%d" % (n, i), [128, 512]) for n in tmpn} for i in range(NTMP)]
            smalls = [sbt(ph, "small%d" % i, [128, 3, 8]) for i in range(NTMP)]
            blk = 0
            for h in range(8):
                wt, wres = load_w([(O_HQ + h * 128, 128), (O_HFF + h * 128, 128), (O_HFB + h * 128, 128)])
                for tb in range(NTB):
                    ps, pres = fm_mm(wt, wres, 0, tb)
                    P.act(lambda e, ps=ps, tb=tb: e.activation(out=qTh[:, tb * 512:(tb + 1) * 512], in_=ps[:], func=AF.Silu),
                          reads=[pres], writes=[("qTh", tb)])
                for d in range(2):
                    col = d * 8 + h
                    for tb in range(NTB):
                        T = tmps[blk % NTMP]
                        sm = smalls[blk % NTMP]
                        R = lambda n, b=blk % NTMP: ("tmp", n, b)
                        blk += 1
                        ps, pres = fm_mm(wt, wres, 1 + d, tb)
                        P.act(lambda e, ps=ps, T=T: e.activation(out=T["sg"][:], in_=ps[:], func=AF.Sigmoid),
                              reads=[pres], writes=[R("sg")])
                        P.dve(lambda e, T=T, col=col: e.scalar_tensor_tensor(out=T["f"][:], in0=T["sg"][:], scalar=lbv[:, 1, col:col + 1],
                                                                             in1=bc(lbv[:, 0, col:col + 1], 1, 512), op0=ALU.mult,
                                                                             op1=ALU.add),
                              reads=[R("sg"), "lbv"], writes=[R("f")])
                        P.act(lambda e, T=T: e.activation(out=T["lf"][:], in_=T["f"][:], func=AF.Ln), reads=[R("f")],
                              writes=[R("lf")])
                        P.pool(lambda e, T=T: e.tensor_scalar(out=T["sg"][:], in0=T["f"][:], scalar1=-1.0, scalar2=1.0,
                                                              op0=ALU.mult, op1=ALU.add), reads=[R("f")], writes=[R("sg")])
                        P.dve(lambda e, T=T: e.tensor_tensor_scan(out=T["b"][:], data0=mask01[:], data1=T["lf"][:], initial=0.0,
                                                                  op0=ALU.mult, op1=ALU.add), reads=["mask01", R("lf")],
                              writes=[R("b")])
                        v3 = lambda t: t[:].rearrange("p (c j) -> p c j", j=64)
                        if d == 0:
                            src, rsrc, jref, js1, js2 = T["b"], R("b"), 32, 0, 63
                        else:
                            P.dve(lambda e, T=T: e.tensor_tensor(out=T["t"][:], in0=T["lf"][:], in1=T["b"][:], op=ALU.subtract),
                                  reads=[R("lf"), R("b")], writes=[R("t")])
                            src, rsrc, jref, js1, js2 = T["t"], R("t"), 31, 63, 0
                        P.dve(lambda e, T=T, src=src, jref=jref: e.tensor_tensor(
                            out=v3(T["dm"]), in0=v3(src), in1=bc(v3(src)[:, :, jref:jref + 1], 2, 64), op=ALU.subtract),
                            reads=[rsrc], writes=[R("dm")])
                        P.act(lambda e, T=T: e.activation(out=T["e1"][:], in_=T["dm"][:], func=AF.Exp), reads=[R("dm")],
                              writes=[R("e1")])
                        P.act(lambda e, T=T: e.activation(out=T["e2"][:], in_=T["dm"][:], func=AF.Exp, scale=-1.0),
                              reads=[R("dm")], writes=[R("e2")])
                        P.dve(lambda e, T=T, sm=sm, js1=js1: e.tensor_tensor(out=sm[:, 0, :], in0=v3(T["e2"])[:, :, js1],
                                                                           in1=v3(T["f"])[:, :, js1], op=ALU.mult),
                              reads=[R("e2"), R("f")], writes=[R("sm0")])
                        P.dve(lambda e, T=T, sm=sm, js2=js2: e.tensor_copy(out=sm[:, 1, :], in_=v3(T["e1"])[:, :, js2]),
                              reads=[R("e1")], writes=[R("sm1")])
                        P.dve(lambda e, sm=sm, d=d, h=h, tb=tb: e.tensor_tensor(out=dec_all[:, d, h, tb * 8:(tb + 1) * 8],
                                                                               in0=sm[:, 0, :], in1=sm[:, 1, :], op=ALU.mult),
                              reads=[R("sm0"), R("sm1")], writes=[("dec", d, h, tb)])
                        s1, r1 = stg.next()
                        P.dve(lambda e, T=T, s1=s1, tb=tb: e.tensor_tensor(out=s1[:], in0=qTh[:, tb * 512:(tb + 1) * 512],
                                                                          in1=T["e1"][:], op=ALU.mult),
                              reads=[("qTh", tb), R("e1")], writes=[r1])
                        P.dma("sp", lambda e, s1=s1, d=d, h=h, tb=tb: e.dma_start(
                            out=grp_dst(QF, d, 0, h, tb), in_=s1[:].rearrange("p (g t) -> p g t", g=2)), reads=[r1], key=r1)
                        P.pool(lambda e, T=T, sm=sm, tb=tb: e.tensor_tensor(
                            out=v3(T["lf"]), in0=qTh[:, tb * 512:(tb + 1) * 512].rearrange("p (c j) -> p c j", j=64),
                            in1=bc(sm[:, 0, :].rearrange("p (c o) -> p c o", o=1), 2, 64), op=ALU.mult),
                            reads=[("qTh", tb), R("sm0")], writes=[R("lf")])
                        s2, r2 = stg.next()
                        P.dve(lambda e, T=T, s2=s2: e.tensor_tensor(out=s2[:], in0=T["lf"][:], in1=T["e1"][:], op=ALU.mult),
                              reads=[R("lf"), R("e1")], writes=[r2])
                        P.dma("sp", lambda e, s2=s2, d=d, h=h, tb=tb: e.dma_start(
                            out=grp_dst(QF, d, 1, h, tb), in_=s2[:].rearrange("p (g t) -> p g t", g=2)), reads=[r2], key=r2)
                        s3, r3 = stg.next()
                        P.dve(lambda e, T=T, s3=s3: e.tensor_tensor(out=s3[:], in0=T["sg"][:], in1=T["e2"][:], op=ALU.mult),
                              reads=[R("sg"), R("e2")], writes=[r3])
                        P.dma("sp", lambda e, s3=s3, d=d, h=h, tb=tb: e.dma_start(
                            out=grp_dst(KF, d, 0, h, tb), in_=s3[:].rearrange("p (g t) -> p g t", g=2)), reads=[r3], key=r3)
                        P.pool(lambda e, T=T, sm=sm: e.tensor_tensor(
                            out=v3(T["b"]), in0=v3(T["sg"]), in1=bc(sm[:, 1, :].rearrange("p (c o) -> p c o", o=1), 2, 64),
                            op=ALU.mult), reads=[R("sg"), R("sm1")], writes=[R("b")])
                        s4, r4 = stg.next()
                        P.dve(lambda e, T=T, s4=s4: e.tensor_tensor(out=s4[:], in0=T["b"][:], in1=T["e2"][:], op=ALU.mult),
                              reads=[R("b"), R("e2")], writes=[r4])
                        P.dma("sp", lambda e, s4=s4, d=d, h=h, tb=tb: e.dma_start(
                            out=grp_dst(KF, d, 1, h, tb), in_=s4[:].rearrange("p (g t) -> p g t", g=2)), reads=[r4], key=r4)

            def fm_simple(c0, ntiles, dst, func, scale=1.0):
                for g0 in range(0, ntiles, 4):
                    wt, wres = load_w([(c0 + g0 * 128, 512)])
                    for j in range(4):
                        for tb in range(NTB):
                            ps, pres = fm_mm(wt, wres, j, tb)
                            s, r = stg.next()
                            P.act(lambda e, ps=ps, s=s: e.activation(out=s[:], in_=ps[:], func=func, scale=scale),
                                  reads=[pres], writes=[r])
                            row = (g0 + j) * 128
                            P.dma("sp", lambda e, s=s, row=row, tb=tb: e.dma_start(
                                out=dst[row:row + 128, tb * 512:(tb + 1) * 512], in_=s[:]), reads=[r], key=r)

            fm_simple(O_MQ, 4, MQ, AF.Identity)
            fm_simple(O_MK, 4, MKF, AF.Identity, scale=0.125)

            def tm_simple(c0, ncols, dst, func, scale=1.0):
                for g0 in range(0, ncols, 512):
                    wt, wres = load_w([(c0 + g0, 512)])
                    for tt in range(NTT):
                        ps, pres = tm_mm(wt, wres, tt, 512)
                        s, r = stg.next()
                        if func is None:
                            P.dve(lambda e, ps=ps, s=s: e.tensor_copy(out=s[:], in_=ps[:]), reads=[pres], writes=[r])
                        else:
                            P.act(lambda e, ps=ps, s=s: e.activation(out=s[:], in_=ps[:], func=func, scale=scale),
                                  reads=[pres], writes=[r])
                        P.dma("sp", lambda e, s=s, tt=tt, g0=g0: e.dma_start(
                            out=dst[tt * 128:(tt + 1) * 128, g0:g0 + 512], in_=s[:]), reads=[r], key=r)

            tm_simple(O_HI, 1024, VV, None)
            tm_simple(O_MV, 1024, MVV, None)
            tm_simple(O_MK, 512, MKT, AF.Identity, scale=0.125)
            tm_simple(O_HG, 1024, HGATE, AF.Silu)
            wt, wres = load_w([(O_MI, 32)])
            gst = Ring("gst", [sbt(ph, "gst%d" % i, [128, 32]) for i in range(2)])
            for tt in range(NTT):
                ps, pres = tm_mm(wt, wres, tt, 32)
                g, gr = gst.next()
                P.dve(lambda e, ps=ps, g=g: e.tensor_tensor(out=g[:], in0=ps[:, 0:32], in1=bif_bc[:], op=ALU.add),
                      reads=[pres, "bif_bc"], writes=[gr])
                P.act(lambda e, g=g: e.activation(out=g[:, 16:32], in_=g[:, 16:32], func=AF.Exp, scale=-1.0), reads=[gr],
                      writes=[gr])
                P.act(lambda e, g=g: e.activation(out=g[:, 16:32], in_=g[:, 16:32], func=AF.Ln, bias=1.0, scale=1.0),
                      reads=[gr], writes=[gr])
                P.dve(lambda e, g=g: e.tensor_scalar(out=g[:, 16:32], in0=g[:, 16:32], scalar1=-1.0, scalar2=None,
                                                     op0=ALU.mult), reads=[gr], writes=[gr])
                P.dma("sp", lambda e, g=g, tt=tt: e.dma_start(out=GATES[tt * 128:(tt + 1) * 128, :], in_=g[:]), reads=[gr],
                      key=gr)
            tm_simple(O_MO, 1024, MOO, AF.Sigmoid)
            fm_simple(O_GA, 16, SGA, AF.Sigmoid)
            fm_simple(O_GB, 16, SGB, AF.Sigmoid)
            P.flush()
        if stop_after == "A":
            return nc

        with contextlib.ExitStack() as ph:
            S = sbt(ph, "S", [128, 8, 128]); Sbf = sbt(ph, "Sbf", [128, 8, 128], BF16)
            C = sbt(ph, "C", [64, 8, 128]); Cbf = sbt(ph, "Cbf", [64, 8, 128], BF16)
            nst = sbt(ph, "nst", [64, 8]); nbf = sbt(ph, "nbf", [64, 8], BF16); mst = sbt(ph, "mst", [64, 8])
            nrow = sbt(ph, "nrow", [8, 64])
            NS = 2
            gt = []
            for i in range(NS):
                gt.append(dict(
                    qt=sbt(ph, "g_qt%d" % i, [128, 8, 256], BF16), qh=sbt(ph, "g_qh%d" % i, [128, 8, 256], BF16),
                    kt=sbt(ph, "g_kt%d" % i, [128, 8, 256], BF16), kh=sbt(ph, "g_kh%d" % i, [128, 8, 256], BF16),
                    vv=sbt(ph, "g_vv%d" % i, [64, 4, 1024], BF16), mvv=sbt(ph, "g_mvv%d" % i, [64, 4, 1024], BF16),
                    mkt=sbt(ph, "g_mkt%d" % i, [64, 4, 512], BF16), gat=sbt(ph, "g_gat%d" % i, [64, 4, 32]),
                    mq=sbt(ph, "g_mq%d" % i, [64, 8, 256], BF16), mkf=sbt(ph, "g_mkf%d" % i, [64, 8, 256], BF16)))
            cext = [dict(ofw=sbt(ph, "c_ofw%d" % i, [64, 2048]), hgate=sbt(ph, "c_hg%d" % i, [64, 1024], BF16),
                         moo=sbt(ph, "c_mo%d" % i, [64, 1024], BF16)) for i in range(1)]
            osbr = Ring("osb", [sbt(ph, "osb%d" % i, [64, 2048]) for i in range(2)])
            att_sb = sbt(ph, "att_sb", [64, 8, 64], BF16)
            khat_sb = sbt(ph, "khat_sb", [64, 1024], BF16)
            diag = sbt(ph, "diag", [64, 8, 64])
            masked = sbt(ph, "masked", [64, 8, 64])
            t1 = masked; Et = masked
            t2 = sbt(ph, "t2", [64, 8, 64]); Sint = t2
            PT = sbt(ph, "PT", [64, 8, 64], BF16); qTs = sbt(ph, "qTs", [64, 8, 64], BF16); wk = sbt(ph, "wk", [64, 8, 64], BF16)
            tC = sbt(ph, "tC", [64, 8, 128])
            sm = sbt(ph, "smB", [64, 16, 8])
            gsb = sbt(ph, "gsb", [64, 16])
            osum = sbt(ph, "osum", [64, 2048]); sqb = sbt(ph, "sqb", [64, 1024]); ohn = sbt(ph, "ohn", [64, 1024])
            ohb = sbt(ph, "ohb", [64, 2, 1024], BF16)
            stT = [[sbt(ph, "stT%d_%d" % (b, i), [128, 8, 256], BF16) for i in range(1)] for b in range(2)]
            psA = pst(ph, "psA", [128, 512]); psB = pst(ph, "psB", [128, 512]); psO = pst(ph, "psO", [128, 1024])
            psD = pst(ph, "psD", [128, 1024]); psU = pst(ph, "psU", [128, 512]); psS = pst(ph, "psS", [128, 512])
            psBb = psB[:].bitcast(BF16)
            SMN = ["u", "cmax", "cm", "M", "wexp", "mt", "ex", "aden", "rd", "am", "ml", "mnew", "d12a", "d12b", "so", "sl"]
            smv = {n: sm[:, i, :] for i, n in enumerate(SMN)}
            d12 = sm[:, 12:14, :]
            sosl = sm[:, 14:16, :]
            id64 = identf[0:64, 0:64]
            MQr = MQ.rearrange("(h d) t -> d h t", d=64)
            MKFr = MKF.rearrange("(h d) t -> d h t", d=64)
            gset_i = [0]
            cset_i = [0]
            stT_i = [0]

            def load_group(gi, d):
                k = gset_i[0] % NS
                gset_i[0] += 1
                G = gt[k]
                key = ("gset", k)
                tl = slice(gi * 256, (gi + 1) * 256)
                names = list(G.keys())
                srcs = dict(qt=QF[d, 0, gi], qh=QF[d, 1, gi], kt=KF[d, 0, gi], kh=KF[d, 1, gi],
                            vv=VV[tl, :].rearrange("(c p) n -> p c n", p=64), mvv=MVV[tl, :].rearrange("(c p) n -> p c n", p=64),
                            mkt=MKT[tl, :].rearrange("(c p) n -> p c n", p=64), gat=GATES[tl, :].rearrange("(c p) n -> p c n", p=64),
                            mq=MQr[:, :, tl], mkf=MKFr[:, :, tl])
                allres = [("g", n, k) for n in names]
                for i, n in enumerate(names):
                    P.dma("sp", lambda e, dst=G[n], src=srcs[n]: e.dma_start(out=dst[:], in_=src),
                          writes=[("g", n, k)], key=key, war=allres if i == 0 else ())
                return G, k

            def chunk_step(G, k, ck, d, last_of_pass, seq, bwd_out):
                ci = ck % 4
                cs = slice(ci * 64, (ci + 1) * 64)
                gr = lambda n: ("g", n, k)
                attmask = up01 if d == 0 else low01
                tri = up01 if d == 0 else low01
                cummask = lowneg if d == 0 else upneg
                emask = upneg if d == 0 else lowneg
                rn = lambda n: "up01" if n is up01 else "low01" if n is low01 else "upneg" if n is upneg else "lowneg"
                osb, osr = osbr.next()
                chain_h, chain_m = [], []
                P.defer = chain_h
                for h in range(8):
                    P.pe(lambda e, h=h: e.matmul(psA[0:64, h * 64:(h + 1) * 64], lhsT=G["kt"][:, h, cs], rhs=G["qt"][:, h, cs],
                                                 start=True, stop=True), reads=[gr("kt"), gr("qt")], writes=["psA"])
                P.dve(lambda e: e.tensor_tensor(out=att_sb[:], in0=psA[0:64, :].rearrange("p (h l) -> p h l", h=8),
                                                in1=ins_bc(attmask[:], 1, 8), op=ALU.mult), reads=["psA", rn(attmask)],
                      writes=["att_sb"])
                for h in range(8):
                    P.pe(lambda e, h=h: e.transpose(psBb[0:64, h * 128:(h + 1) * 128], G["kh"][:, h, cs], identb[:]),
                         reads=[gr("kh"), "identb"], writes=["psB"])
                P.act(lambda e: e.copy(out=khat_sb[:], in_=psBb[0:64, :]), reads=["psB"], writes=["khat_sb"])
                for h in range(8):
                    hs = slice(h * 128, (h + 1) * 128)
                    P.pe(lambda e, h=h, hs=hs: e.matmul(psO[0:64, hs], lhsT=att_sb[:, h, :], rhs=G["vv"][:, ci, hs], start=True,
                                                        stop=False), reads=["att_sb", gr("vv")], writes=["psO"])
                    P.pe(lambda e, h=h, hs=hs: e.matmul(psO[0:64, hs], lhsT=G["qh"][:, h, cs], rhs=Sbf[:, h, :], start=False,
                                                        stop=True), reads=[gr("qh"), "Sbf"], writes=["psO"])
                P.act(lambda e: e.copy(out=osb[:, 0:1024], in_=psO[0:64, :]), reads=["psO"], writes=[(osr, 0)])
                for h in range(8):
                    hs = slice(h * 128, (h + 1) * 128)
                    P.pe(lambda e, hs=hs: e.matmul(psD[:, hs], lhsT=khat_sb[:, hs], rhs=G["vv"][:, ci, hs], start=True, stop=True),
                         reads=["khat_sb", gr("vv")], writes=["psD"])
                decv = bc(dec_all[:, d, :, ck:ck + 1], 2, 128)
                P.dve(lambda e: e.tensor_tensor(out=S[:], in0=S[:], in1=decv, op=ALU.mult), reads=["S"] + [("dec", d, h, ck // 8) for h in range(8)],
                      writes=["S"])
                P.dve(lambda e: e.tensor_tensor(out=S[:], in0=S[:], in1=psD[:].rearrange("p (h v) -> p h v", h=8), op=ALU.add),
                      reads=["S", "psD"], writes=["S"])
                P.act(lambda e: e.copy(out=Sbf[:], in_=S[:]), reads=["S"], writes=["Sbf"])
                P.defer = chain_m
                li = G["gat"][:, ci, d * 8:(d + 1) * 8]
                lfm = G["gat"][:, ci, 16 + d * 8:16 + (d + 1) * 8]
                P.pe(lambda e: e.matmul(psS[0:64, 0:8], lhsT=tri[:], rhs=lfm, start=True, stop=True), reads=[rn(tri), gr("gat")],
                     writes=["psS"])
                P.pe(lambda e: e.matmul(psS[0:64, 8:16], lhsT=ones64[:], rhs=lfm, start=True, stop=True), reads=["ones64", gr("gat")],
                     writes=["psS"])
                P.dve(lambda e: e.tensor_tensor(out=smv["u"], in0=li, in1=psS[0:64, 0:8], op=ALU.subtract), reads=[gr("gat"), "psS"],
                      writes=["u"])
                P.dve(lambda e: e.tensor_copy(out=gsb[:], in_=psS[0:64, 0:16]), reads=["psS"], writes=["gsb"])
                v1 = lambda a: a.rearrange("p (h o) -> p h o", o=1)
                P.dve(lambda e: e.tensor_tensor(out=diag[:], in0=ins_bc(id64, 1, 8), in1=bc(v1(smv["u"]), 2, 64), op=ALU.mult),
                      reads=["identf", "u"], writes=["diag"])
                P.pe(lambda e: e.matmul(psU[0:64, :], lhsT=ones64[:], rhs=diag[:].rearrange("p h s -> p (h s)"), start=True, stop=True),
                     reads=["ones64", "diag"], writes=["psU"])
                psU3 = psU[0:64, :].rearrange("p (h s) -> p h s", h=8)
                P.dve(lambda e: e.tensor_reduce(out=smv["cmax"], in_=psU3, axis=AX.X, op=ALU.max), reads=["psU"], writes=["cmax"])
                P.dve(lambda e: e.tensor_tensor(out=masked[:], in0=psU3, in1=ins_bc(cummask[:], 1, 8), op=ALU.add),
                      reads=["psU", rn(cummask)], writes=["masked"])
                P.dve(lambda e: e.tensor_reduce(out=smv["cm"], in_=masked[:], axis=AX.X, op=ALU.max), reads=["masked"], writes=["cm"])
                P.dve(lambda e: e.tensor_tensor(out=smv["M"], in0=smv["cm"], in1=mst[:], op=ALU.max), reads=["cm", "mst"], writes=["M"])
                P.dve(lambda e: e.tensor_tensor(out=diag[:], in0=ins_bc(id64, 1, 8), in1=bc(v1(smv["M"]), 2, 64), op=ALU.mult),
                      reads=["identf", "M", "diag"], writes=["diag"])
                P.pe(lambda e: e.matmul(psU[0:64, :], lhsT=ones64[:], rhs=diag[:].rearrange("p h s -> p (h s)"), start=True, stop=True),
                     reads=["ones64", "diag"], writes=["psU"])
                P.dve(lambda e: e.tensor_tensor(out=t1[:], in0=bc(v1(smv["u"]), 2, 64), in1=psU3, op=ALU.subtract), reads=["u", "psU", "masked"],
                      writes=["masked"])
                P.dve(lambda e: e.tensor_tensor(out=t1[:], in0=t1[:], in1=ins_bc(emask[:], 1, 8), op=ALU.add), reads=["masked", rn(emask)],
                      writes=["masked"])
                P.act(lambda e: e.activation(out=Et[:], in_=t1[:], func=AF.Exp), reads=["masked"], writes=["masked"])
                P.dve(lambda e: e.tensor_tensor(out=t2[:], in0=bc(v1(mst[:]), 2, 64), in1=psU3, op=ALU.subtract), reads=["mst", "psU"],
                      writes=["t2"])
                P.act(lambda e: e.activation(out=Sint[:], in_=t2[:], func=AF.Exp), reads=["t2"], writes=["t2"])
                for h in range(8):
                    P.pe(lambda e, h=h: e.matmul(psA[0:64, h * 64:(h + 1) * 64], lhsT=G["mkf"][:, h, cs], rhs=G["mq"][:, h, cs],
                                                 start=True, stop=True), reads=[gr("mkf"), gr("mq")], writes=["psA"])
                P.dve(lambda e: e.tensor_tensor(out=PT[:], in0=psA[0:64, :].rearrange("p (h l) -> p h l", h=8), in1=Et[:], op=ALU.mult),
                      reads=["psA", "masked"], writes=["PT"])
                P.dve(lambda e: e.tensor_tensor(out=qTs[:], in0=G["mq"][:, :, cs], in1=Sint[:], op=ALU.mult), reads=[gr("mq"), "t2"],
                      writes=["qTs"])
                P.dve(lambda e: e.tensor_tensor(out=smv["wexp"], in0=smv["u"], in1=smv["cmax"], op=ALU.subtract), reads=["u", "cmax"],
                      writes=["wexp"])
                P.act(lambda e: e.activation(out=smv["wexp"], in_=smv["wexp"], func=AF.Exp), reads=["wexp"], writes=["wexp"])
                P.dve(lambda e: e.tensor_tensor(out=wk[:], in0=G["mkt"][:, ci, :].rearrange("p (h x) -> p h x", h=8),
                                                in1=bc(v1(smv["wexp"]), 2, 64), op=ALU.mult), reads=[gr("mkt"), "wexp"], writes=["wk"])
                for h in range(8):
                    hs = slice(h * 128, (h + 1) * 128)
                    P.pe(lambda e, h=h, hs=hs: e.matmul(psO[0:64, hs], lhsT=PT[:, h, :], rhs=G["mvv"][:, ci, hs], start=True, stop=False),
                         reads=["PT", gr("mvv")], writes=["psO"])
                    P.pe(lambda e, h=h, hs=hs: e.matmul(psO[0:64, hs], lhsT=qTs[:, h, :], rhs=Cbf[:, h, :], start=False, stop=True),
                         reads=["qTs", "Cbf"], writes=["psO"])
                    P.pe(lambda e, h=h: e.matmul(psS[0:64, 16 + h:17 + h], lhsT=PT[:, h, :], rhs=onesb[:, 0:1], start=True, stop=False),
                         reads=["PT", "onesb"], writes=["psS"])
                    P.pe(lambda e, h=h: e.matmul(psS[0:64, 16 + h:17 + h], lhsT=qTs[:, h, :], rhs=nbf[:, h:h + 1], start=False, stop=True),
                         reads=["qTs", "nbf"], writes=["psS"])
                for h in range(8):
                    hs = slice(h * 128, (h + 1) * 128)
                    P.pe(lambda e, h=h, hs=hs: e.matmul(psD[0:64, hs], lhsT=wk[:, h, :], rhs=G["mvv"][:, ci, hs], start=True, stop=True),
                         reads=["wk", gr("mvv")], writes=["psD"])
                    P.pe(lambda e, h=h: e.matmul(psS[0:64, 24 + h:25 + h], lhsT=wk[:, h, :], rhs=onesb[:, 0:1], start=True, stop=True),
                         reads=["wk", "onesb"], writes=["psS"])
                P.dve(lambda e: e.tensor_tensor(out=smv["mt"], in0=gsb[:, 0:8], in1=smv["M"], op=ALU.add), reads=["gsb", "M"], writes=["mt"])
                P.act(lambda e: e.activation(out=smv["ex"], in_=smv["mt"], func=AF.Exp, scale=-1.0), reads=["mt"], writes=["ex"])
                P.act(lambda e: e.activation(out=smv["aden"], in_=psS[0:64, 16:24], func=AF.Abs), reads=["psS"], writes=["aden"])
                P.dve(lambda e: e.tensor_tensor(out=smv["rd"], in0=smv["aden"], in1=smv["ex"], op=ALU.max), reads=["aden", "ex"], writes=["rd"])
                P.dve(lambda e: e.reciprocal(out=smv["rd"], in_=smv["rd"]), reads=["rd"], writes=["rd"])
                P.dve(lambda e: e.tensor_tensor(out=osb[:, 1024:2048].rearrange("p (h v) -> p h v", h=8),
                                                in0=psO[0:64, :].rearrange("p (h v) -> p h v", h=8), in1=bc(v1(smv["rd"]), 2, 128),
                                                op=ALU.mult), reads=["psO", "rd"], writes=[(osr, 1)])
                P.dve(lambda e: e.tensor_tensor(out=smv["am"], in0=gsb[:, 8:16], in1=mst[:], op=ALU.add), reads=["gsb", "mst"], writes=["am"])
                P.dve(lambda e: e.tensor_tensor(out=smv["ml"], in0=gsb[:, 8:16], in1=smv["cmax"], op=ALU.add), reads=["gsb", "cmax"],
                      writes=["ml"])
                P.dve(lambda e: e.tensor_tensor(out=smv["mnew"], in0=smv["am"], in1=smv["ml"], op=ALU.max), reads=["am", "ml"],
                      writes=["mnew"])
                P.dve(lambda e: e.tensor_tensor(out=smv["d12a"], in0=smv["am"], in1=smv["mnew"], op=ALU.subtract), reads=["am", "mnew"],
                      writes=["d12a"])
                P.dve(lambda e: e.tensor_tensor(out=smv["d12b"], in0=smv["ml"], in1=smv["mnew"], op=ALU.subtract), reads=["ml", "mnew"],
                      writes=["d12b"])
                P.act(lambda e: e.activation(out=sosl, in_=d12, func=AF.Exp), reads=["d12a", "d12b"], writes=["sosl"])
                P.dve(lambda e: e.tensor_copy(out=mst[:], in_=smv["mnew"]), reads=["mnew"], writes=["mst"])
                P.dve(lambda e: e.tensor_tensor(out=tC[:], in0=psD[0:64, :].rearrange("p (h v) -> p h v", h=8),
                                                in1=bc(v1(smv["sl"]), 2, 128), op=ALU.mult), reads=["psD", "sosl"], writes=["tC"])
                P.dve(lambda e: e.tensor_tensor(out=C[:], in0=C[:], in1=bc(v1(smv["so"]), 2, 128), op=ALU.mult), reads=["C", "sosl"],
                      writes=["C"])
                P.dve(lambda e: e.tensor_tensor(out=C[:], in0=C[:], in1=tC[:], op=ALU.add), reads=["C", "tC"], writes=["C"])
                P.act(lambda e: e.copy(out=Cbf[:], in_=C[:]), reads=["C"], writes=["Cbf"])
                P.dve(lambda e: e.tensor_tensor(out=nst[:], in0=nst[:], in1=smv["so"], op=ALU.mult), reads=["nst", "sosl"], writes=["nst"])
                P.dve(lambda e: e.tensor_tensor(out=smv["aden"], in0=psS[0:64, 24:32], in1=smv["sl"], op=ALU.mult),
                      reads=["psS", "sosl", "aden"], writes=["aden"])
                P.dve(lambda e: e.tensor_tensor(out=nst[:], in0=nst[:], in1=smv["aden"], op=ALU.add), reads=["nst", "aden"], writes=["nst"])
                P.act(lambda e: e.copy(out=nbf[:], in_=nst[:]), reads=["nst"], writes=["nbf"])
                P.defer = None
                P.replay_interleaved(chain_m, chain_h, ("psA", "psO", "psD", "psB", "psS", "psU"))
                tok0 = ck * 64
                if not bwd_out:
                    P.dma("sp", lambda e: e.dma_start(out=OFW[tok0:tok0 + 64, :], in_=osb[:]), reads=[(osr, 0), (osr, 1)],
                          writes=[("OFW", ck)], key=osr)
                    return
                X = cext[0]
                xk = ("cext", 0)
                cset_i[0] += 1
                P.dma("sp", lambda e: e.dma_start(out=X["ofw"][:], in_=OFW[tok0:tok0 + 64, :]), reads=[("OFW", ck)], writes=[(xk, "ofw")],
                      key=xk, war=[(xk, "hgate"), (xk, "moo")])
                P.dma("sp", lambda e: e.dma_start(out=X["hgate"][:], in_=HGATE[tok0:tok0 + 64, :]), writes=[(xk, "hgate")], key=xk)
                P.dma("sp", lambda e: e.dma_start(out=X["moo"][:], in_=MOO[tok0:tok0 + 64, :]), writes=[(xk, "moo")], key=xk)
                P.dve(lambda e: e.tensor_tensor(out=osum[:], in0=osb[:], in1=X["ofw"][:], op=ALU.add), reads=[(osr, 0), (osr, 1), (xk, "ofw")],
                      writes=["osum"])
                o3 = lambda a: a.rearrange("p (h v) -> p h v", h=8)
                P.pool(lambda e: e.tensor_tensor(out=sqb[:], in0=osum[:, 0:1024], in1=osum[:, 0:1024], op=ALU.mult), reads=["osum"],
                       writes=["sqb"])
                P.dve(lambda e: e.tensor_reduce(out=smv["am"], in_=o3(sqb[:]), axis=AX.X, op=ALU.add), reads=["sqb"], writes=["am"])
                P.act(lambda e: e.activation(out=smv["am"], in_=smv["am"], func=AF.Ln, scale=1.0 / 128, bias=epst[0:64, :]), reads=["am", "epst"],
                      writes=["am"])
                P.act(lambda e: e.activation(out=smv["am"], in_=smv["am"], func=AF.Exp, scale=-0.5), reads=["am"], writes=["am"])
                P.dve(lambda e: e.tensor_tensor(out=o3(ohn[:]), in0=o3(osum[:, 0:1024]), in1=bc(v1(smv["am"]), 2, 128), op=ALU.mult),
                      reads=["osum", "am"], writes=["ohn"])
                P.dve(lambda e: e.tensor_tensor(out=o3(ohn[:]), in0=o3(ohn[:]), in1=ins_bc(hgg_bc[:], 1, 8), op=ALU.mult),
                      reads=["ohn", "hgg_bc"], writes=["ohn"])
                P.dve(lambda e: e.tensor_tensor(out=ohb[:, 0, :], in0=ohn[:], in1=X["hgate"][:], op=ALU.mult), reads=["ohn", (xk, "hgate")],
                      writes=["ohb0"])
                P.dve(lambda e: e.tensor_reduce(out=smv["ml"], in_=o3(osum[:, 1024:2048]), axis=AX.X, op=ALU.add), reads=["osum"],
                      writes=["ml"])
                P.dve(lambda e: e.tensor_scalar(out=smv["ml"], in0=smv["ml"], scalar1=1.0 / 128, scalar2=None, op0=ALU.mult), reads=["ml"],
                      writes=["ml"])
                P.dve(lambda e: e.tensor_tensor(out=o3(ohn[:]), in0=o3(osum[:, 1024:2048]), in1=bc(v1(smv["ml"]), 2, 128), op=ALU.subtract),
                      reads=["osum", "ml", "ohn"], writes=["ohn"])
                P.pool(lambda e: e.tensor_tensor(out=sqb[:], in0=ohn[:], in1=ohn[:], op=ALU.mult), reads=["ohn", "sqb"], writes=["sqb"])
                P.dve(lambda e: e.tensor_reduce(out=smv["mnew"], in_=o3(sqb[:]), axis=AX.X, op=ALU.add), reads=["sqb"], writes=["mnew"])
                P.act(lambda e: e.activation(out=smv["mnew"], in_=smv["mnew"], func=AF.Ln, scale=1.0 / 128, bias=epst[0:64, :]),
                      reads=["mnew", "epst"], writes=["mnew"])
                P.act(lambda e: e.activation(out=smv["mnew"], in_=smv["mnew"], func=AF.Exp, scale=-0.5), reads=["mnew"], writes=["mnew"])
                P.dve(lambda e: e.tensor_tensor(out=o3(ohn[:]), in0=o3(ohn[:]), in1=bc(v1(smv["mnew"]), 2, 128), op=ALU.mult),
                      reads=["ohn", "mnew"], writes=["ohn"])
                P.dve(lambda e: e.tensor_tensor(out=ohn[:], in0=ohn[:], in1=mlg_bc[:], op=ALU.mult), reads=["ohn", "mlg_bc"], writes=["ohn"])
                P.dve(lambda e: e.tensor_tensor(out=ohb[:, 1, :], in0=ohn[:], in1=X["moo"][:], op=ALU.mult), reads=["ohn", (xk, "moo")],
                      writes=["ohb1"])
                si = 0
                for b, dst in ((0, OHG), (1, OML)):
                    psT = psBb[:, 0:512].rearrange("p (h t) -> p h t", h=8)
                    for h in range(8):
                        P.pe(lambda e, h=h, b=b: e.transpose(psBb[:, h * 64:(h + 1) * 64], ohb[:, b, h * 128:(h + 1) * 128], identb[0:64, 0:64]),
                             reads=["ohb%d" % b, "identb"], writes=["psB"])
                    P.act(lambda e, b=b, psT=psT: e.copy(out=stT[b][si][:, :, cs], in_=psT), reads=["psB"], writes=[("stT", b, si)])
                    if last_of_pass:
                        g = ck // 4
                        P.dma("sp", lambda e, b=b, dst=dst, g=g: e.dma_start(out=dst[:, :, g * 256:(g + 1) * 256], in_=stT[b][si][:]),
                              reads=[("stT", b, si)], key=("stT", b, si))
                if last_of_pass:
                    stT_i[0] += 1

            for (c0, nchk, is_prompt, li_) in SEQS:
                for d in range(2):
                    if is_prompt:
                        P.pool(lambda e: e.memset(S[:], 0.0), writes=["S"])
                        P.pool(lambda e: e.memset(Sbf[:], 0.0), writes=["Sbf"])
                        P.pool(lambda e: e.memset(C[:], 0.0), writes=["C"])
                        P.pool(lambda e: e.memset(Cbf[:], 0.0), writes=["Cbf"])
                        P.pool(lambda e: e.memset(nst[:], 0.0), writes=["nst"])
                        P.pool(lambda e: e.memset(nbf[:], 0.0), writes=["nbf"])
                        P.pool(lambda e: e.memset(mst[:], 0.0), writes=["mst"])
                    else:
                        P.dma("sp", lambda e, d=d: e.dma_start(out=S[:], in_=st_s[d].rearrange("h k v -> k h v")), writes=["S"], key="st0")
                        P.dma("sp", lambda e, d=d: e.dma_start(out=C[:], in_=st_c[d].rearrange("h k v -> k h v")), writes=["C"], key="st0")
                        P.dma("sp", lambda e, d=d: e.dma_start(out=nrow[:], in_=st_n[d]), writes=["nrow"], key="st0")
                        P.dma("sp", lambda e, d=d: e.dma_start(out=mst[:], in_=bass.AP(st_m.tensor, st_m.offset + d * 8, [[0, 64], [1, 8]])),
                              writes=["mst"], key="st0")
                        P.pe(lambda e: e.transpose(psS[0:64, 32:40], nrow[:], identf[0:8, 0:8]), reads=["nrow", "identf"], writes=["psS"])
                        P.dve(lambda e: e.tensor_copy(out=nst[:], in_=psS[0:64, 32:40]), reads=["psS"], writes=["nst"])
                        P.pool(lambda e: e.tensor_copy(out=Sbf[:], in_=S[:]), reads=["S"], writes=["Sbf"])
                        P.pool(lambda e: e.tensor_copy(out=Cbf[:], in_=C[:]), reads=["C"], writes=["Cbf"])
                        P.pool(lambda e: e.tensor_copy(out=nbf[:], in_=nst[:]), reads=["nst"], writes=["nbf"])
                    order = list(range(c0, c0 + nchk)) if d == 0 else list(range(c0 + nchk - 1, c0 - 1, -1))
                    G = None
                    cur_g = None
                    for i, ck in enumerate(order):
                        if ck // 4 != cur_g:
                            cur_g = ck // 4
                            G, k = load_group(cur_g, d)
                        last_in_grp = (i == len(order) - 1) or (order[i + 1] // 4 != cur_g)
                        chunk_step(G, k, ck, d, last_in_grp, li_, d == 1)
                    if is_prompt:
                        b = li_
                        P.dma("sp", lambda e, b=b, d=d: e.dma_start(out=o_s[b, d].rearrange("h k v -> k h v"), in_=S[:]), reads=["S"],
                              key="fin")
                        P.dma("sp", lambda e, b=b, d=d: e.dma_start(out=o_c[b, d].rearrange("h k v -> k h v"), in_=C[:]), reads=["C"],
                              key="fin")
                        P.pe(lambda e: e.transpose(psS[0:8, 64:128], nst[:], id64), reads=["nst", "identf"], writes=["psS"])
                        P.dve(lambda e: e.tensor_copy(out=nrow[:], in_=psS[0:8, 64:128]), reads=["psS"], writes=["nrow"])
                        P.dma("sp", lambda e, b=b, d=d: e.dma_start(out=o_n[b, d], in_=nrow[:]), reads=["nrow"], key="fin")
                        P.dma("sp", lambda e, b=b, d=d: e.dma_start(out=o_m[b:b + 1, d * 8:(d + 1) * 8], in_=mst[0:1, :]), reads=["mst"],
                              key="fin")
            P.flush()
        if stop_after == "B" or part == 1:
            return nc
        P.discard = False

        with contextlib.ExitStack() as ph:
            actA = sbt(ph, "actA", [128, 16, 512], BF16)
            big = sbt(ph, "big", [128, 44, 512], BF16)
            h2T = sbt(ph, "h2T", [128, 16, 512], BF16)
            hbr = Ring("hb", [sbt(ph, "hb%d" % i, [128, D], BF16) for i in range(2)])
            wsl = [sbt(ph, "wcs%d" % i, [128, KT, 512], BF16) for i in range(2)]
            wring = Ring("wsl", wsl)
            sgr = Ring("sg", [sbt(ph, "sgt%d" % i, [128, 2, 512], BF16) for i in range(2)])
            xr = Ring("xc", [sbt(ph, "xc%d" % i, [128, D]) for i in range(2)])
            yblk = sbt(ph, "yblk", [128, 4, D])
            Gbc = sbt(ph, "Gbc", [128, D])
            tmpb = Ring("tmpb", [sbt(ph, "tmpb%d" % i, [128, 128]) for i in range(2)])
            junkc = sbt(ph, "junkc", [128, 512], BF16)
            junk2 = sbt(ph, "junk2", [128, D], BF16)
            m12 = Ring("m12", [sbt(ph, "m12_%d" % i, [128, 2, 512]) for i in range(2)])
            ssp = sbt(ph, "ssp", [128, 4, 4])
            ssv = sbt(ph, "ssv", [128, 4, 4])
            mmr = Ring("mmc", [pst(ph, "mmc%d" % i, [128, 512]) for i in range(3)])
            tpcr = Ring("tpc", [pst(ph, "tpc%d" % i, [128, 512]) for i in range(1)])
            big4 = pst(ph, "big4", [128, 4, 512])
            wuh = w_up_hg.rearrange("(kt p) n -> p kt n", p=128)
            wum = w_up_ml.rearrange("(kt p) n -> p kt n", p=128)
            wo = w_out.rearrange("(kt p) n -> p kt n", p=128)
            wfi = w_ffn_in.rearrange("(kt p) n -> p kt n", p=128)
            wfo = w_ffn_out.rearrange("(kt p) n -> p kt n", p=128)

            def load_slot(pieces):
                wt, wres = wring.next()
                first = True
                for (src, k0, nk, c0, ncol) in pieces:
                    P.dma("pool", lambda e, wt=wt, src=src, k0=k0, nk=nk, c0=c0, ncol=ncol: e.dma_start(
                        out=wt[:, k0:k0 + nk, c0:c0 + ncol], in_=src), writes=[wres] if first else [], key=wres)
                    first = False
                return wt, wres

            def make_gbc(q, c):
                for q4 in range(4):
                    ps, pres = mmr.next()
                    for j in range(4):
                        kt = q4 * 4 + j
                        tb_, tres = tmpb.next()
                        P.dve(lambda e, tb_=tb_, kt=kt: e.tensor_copy(out=tb_[:], in_=bc(modv[:, q, kt, c:c + 1], 1, 128)),
                              reads=["modv%d" % q], writes=[tres])
                        P.pe(lambda e, ps=ps, tb_=tb_, j=j: e.matmul(ps[:, j * 128:(j + 1) * 128], lhsT=tb_[:], rhs=identf[:], start=True,
                                                                    stop=True), reads=[tres, "identf"], writes=[pres])
                    P.act(lambda e, ps=ps, q4=q4: e.copy(out=Gbc[:, q4 * 512:(q4 + 1) * 512], in_=ps[:]), reads=[pres], writes=["Gbc"])

            def rstd_from(ssap, outap, res_in, res_out):
                P.act(lambda e: e.activation(out=outap, in_=ssap, func=AF.Ln, scale=1.0 / D, bias=epst[:]), reads=[res_in, "epst"],
                      writes=[res_out])
                P.act(lambda e: e.activation(out=outap, in_=outap, func=AF.Exp, scale=-0.5), reads=[res_out], writes=[res_out])

            _cstop = 9
            _ntb = NTB
            for tb in range(_ntb):
                c = 0 if tb < 2 else 1
                tsl = slice(tb * 512, (tb + 1) * 512)
                P.dma("sp", lambda e, tsl=tsl: e.dma_start(out=actA[:, 0:8, :], in_=OHG[:, :, tsl]),
                      writes=[("actA", kt) for kt in range(8)], key="actA")
                P.dma("sp", lambda e, tsl=tsl: e.dma_start(out=actA[:, 8:16, :], in_=OML[:, :, tsl]),
                      writes=[("actA", kt) for kt in range(8, 16)], key="actA")
                for jb in range(4):
                    wt, wres = load_slot([(wuh[:, :, jb * 512:(jb + 1) * 512], 0, 8, 0, 512),
                                          (wum[:, :, jb * 512:(jb + 1) * 512], 8, 8, 0, 512)])
                    for jj in range(4):
                        j = jb * 4 + jj
                        ps1, pr1 = mmr.next()
                        ps2, pr2 = mmr.next()
                        for kt in range(8):
                            P.pe(lambda e, ps1=ps1, wt=wt, kt=kt, jj=jj: e.matmul(ps1[:], lhsT=wt[:, kt, jj * 128:(jj + 1) * 128],
                                                                                rhs=actA[:, kt, :], start=(kt == 0), stop=(kt == 7)),
                                 reads=[wres, ("actA", kt)], writes=[pr1])
                        for kt in range(8, 16):
                            P.pe(lambda e, ps2=ps2, wt=wt, kt=kt, jj=jj: e.matmul(ps2[:], lhsT=wt[:, kt, jj * 128:(jj + 1) * 128],
                                                                                rhs=actA[:, kt, :], start=(kt == 8), stop=(kt == 15)),
                                 reads=[wres, ("actA", kt)], writes=[pr2])
                        sg, sgres = sgr.next()
                        P.dma("sp", lambda e, sg=sg, j=j, tsl=tsl: e.dma_start(out=sg[:, 0, :], in_=SGA[j * 128:(j + 1) * 128, tsl]),
                              writes=[sgres], key=sgres)
                        P.dma("sp", lambda e, sg=sg, j=j, tsl=tsl: e.dma_start(out=sg[:, 1, :], in_=SGB[j * 128:(j + 1) * 128, tsl]),
                              key=sgres)
                        mm, mres = m12.next()
                        P.dve(lambda e, mm=mm, ps1=ps1, sg=sg: e.tensor_tensor(out=mm[:, 0, :], in0=ps1[:], in1=sg[:, 0, :], op=ALU.mult),
                              reads=[pr1, sgres], writes=[(mres, 0)])
                        P.dve(lambda e, mm=mm, ps2=ps2, sg=sg: e.tensor_tensor(out=mm[:, 1, :], in0=ps2[:], in1=sg[:, 1, :], op=ALU.mult),
                              reads=[pr2, sgres], writes=[(mres, 1)])
                        P.pool(lambda e, mm=mm, j=j: e.tensor_tensor(out=big[:, j, :], in0=mm[:, 0, :], in1=mm[:, 1, :], op=ALU.add),
                               reads=[(mres, 0), (mres, 1)], writes=[("big", j)])
                if _cstop < 1.2:
                    continue
                make_gbc(2, c)
                if _cstop < 1.5:
                    continue
                for cb in range(4):
                    wt, wres = load_slot([(wo[:, :, cb * 512:(cb + 1) * 512], 0, 16, 0, 512)])
                    for tt in range(4):
                        ps, pres = mmr.next()
                        for kt in range(KT):
                            P.pe(lambda e, ps=ps, wt=wt, kt=kt, tt=tt: e.matmul(ps[:], lhsT=big[:, kt, tt * 128:(tt + 1) * 128],
                                                                              rhs=wt[:, kt, :], start=(kt == 0), stop=(kt == KT - 1)),
                                 reads=[wres, ("big", kt)], writes=[pres])
                        P.dve(lambda e, ps=ps, tt=tt, cb=cb: e.tensor_copy(out=yblk[:, tt, cb * 512:(cb + 1) * 512], in_=ps[:]),
                              reads=[pres], writes=[("yblk", tt, cb)])
                if _cstop < 1.8:
                    continue
                for tt in range(4):
                    tok0 = tb * 512 + tt * 128
                    yres = [("yblk", tt, cb) for cb in range(4)]
                    P.act(lambda e, tt=tt: e.activation(out=junk2[:], in_=yblk[:, tt, :], func=AF.Square, accum_out=ssv[:, tt, 0:1]),
                          reads=yres, writes=["junk2", ("ssv", tt, 0)])
                    rstd_from(ssv[:, tt, 0:1], ssv[:, tt, 1:2], ("ssv", tt, 0), ("ssv", tt, 1))
                    xt, xres = xr.next()
                    P.dma("sp", lambda e, xt=xt, tok0=tok0: e.dma_start(out=xt[:], in_=x_all[tok0:tok0 + 128, :]), writes=[xres], key=xres)
                    P.dve(lambda e, tt=tt: e.tensor_scalar(out=yblk[:, tt, :], in0=yblk[:, tt, :], scalar1=ssv[:, tt, 1:2], scalar2=None,
                                                           op0=ALU.mult), reads=yres + [("ssv", tt, 1)], writes=yres)
                    P.dve(lambda e, tt=tt: e.tensor_tensor(out=yblk[:, tt, :], in0=yblk[:, tt, :], in1=Gbc[:], op=ALU.mult),
                          reads=yres + ["Gbc"], writes=yres)
                    P.dve(lambda e, xt=xt, tt=tt: e.tensor_tensor(out=xt[:], in0=xt[:], in1=yblk[:, tt, :], op=ALU.add),
                          reads=yres + [xres], writes=[xres])
                    P.dma("sp", lambda e, xt=xt, tok0=tok0: e.dma_start(out=X1[tok0:tok0 + 128, :], in_=xt[:]), reads=[xres],
                          writes=[("X1", tb, tt)], key=xres)
                make_gbc(3, c)
                if _cstop < 1.91:
                    P.flush()
                    continue
                for tt in range(4):
                    tok0 = tb * 512 + tt * 128
                    yres = [("yblk", tt, cb) for cb in range(4)]
                    xt, xres = xr.next()
                    P.dma("sp", lambda e, xt=xt, tok0=tok0: e.dma_start(out=xt[:], in_=X1[tok0:tok0 + 128, :]), writes=[xres], key=xres)
                    P.act(lambda e, xt=xt, tt=tt: e.activation(out=junk2[:], in_=xt[:], func=AF.Square, accum_out=ssv[:, tt, 2:3]),
                          reads=[xres], writes=["junk2", ("ssv", tt, 2)])
                    rstd_from(ssv[:, tt, 2:3], ssv[:, tt, 3:4], ("ssv", tt, 2), ("ssv", tt, 3))
                    P.dve(lambda e, xt=xt, tt=tt: e.tensor_scalar(out=yblk[:, tt, :], in0=xt[:], scalar1=ssv[:, tt, 3:4], scalar2=None,
                                                                  op0=ALU.mult), reads=[xres, ("ssv", tt, 3)], writes=yres)
                    P.dve(lambda e, tt=tt: e.tensor_tensor(out=yblk[:, tt, :], in0=yblk[:, tt, :], in1=Gbc[:], op=ALU.mult),
                          reads=yres + ["Gbc"], writes=yres)
                if _cstop < 1.92:
                    P.flush()
                    continue
                make_gbc(4, c)
                if _cstop < 1.93:
                    P.flush()
                    continue
                for tt in range(4):
                    yres = [("yblk", tt, cb) for cb in range(4)]
                    hb, hbres = hbr.next()
                    P.dve(lambda e, tt=tt, hb=hb: e.tensor_tensor(out=hb[:], in0=yblk[:, tt, :], in1=Gbc[:], op=ALU.add),
                          reads=yres + ["Gbc"], writes=[hbres])
                    if _cstop < 1.94:
                        continue
                    for q4 in range(4):
                        ps, pres = tpcr.next()
                        psb16 = ps[:].bitcast(BF16)
                        for j in range(4):
                            kt = q4 * 4 + j
                            P.pe(lambda e, psb16=psb16, j=j, kt=kt, hb=hb: e.transpose(psb16[:, j * 128:(j + 1) * 128],
                                                                                      hb[:, kt * 128:(kt + 1) * 128], identb[:]),
                                 reads=[hbres, "identb"], writes=[pres])
                        k0 = q4 * 4
                        if _cstop < 1.95:
                            continue
                        P.act(lambda e, psb16=psb16, k0=k0, tt=tt: e.copy(
                            out=h2T[:, k0:k0 + 4, tt * 128:(tt + 1) * 128], in_=psb16[:, 0:512].rearrange("p (j t) -> p j t", j=4)),
                            reads=[pres], writes=[("h2T", k0 + j) for j in range(4)])
                if _cstop < 3:
                    continue
                for j2 in range(22):
                    wt, wres = load_slot([(wfi[:, :, j2 * 256:(j2 + 1) * 256], 0, 16, 0, 256),
                                          (wfi[:, :, DFF + j2 * 256:DFF + (j2 + 1) * 256], 0, 16, 256, 256)])
                    for jj in range(2):
                        j = j2 * 2 + jj
                        psa, pra = mmr.next()
                        psb, prb = mmr.next()
                        for kt in range(KT):
                            P.pe(lambda e, psa=psa, wt=wt, kt=kt, jj=jj: e.matmul(psa[:], lhsT=wt[:, kt, jj * 128:(jj + 1) * 128],
                                                                                rhs=h2T[:, kt, :], start=(kt == 0), stop=(kt == KT - 1)),
                                 reads=[wres, ("h2T", kt)], writes=[pra])
                        for kt in range(KT):
                            P.pe(lambda e, psb=psb, wt=wt, kt=kt, jj=jj: e.matmul(psb[:], lhsT=wt[:, kt, 256 + jj * 128:256 + (jj + 1) * 128],
                                                                                rhs=h2T[:, kt, :], start=(kt == 0), stop=(kt == KT - 1)),
                                 reads=[wres, ("h2T", kt)], writes=[prb])
                        mm, mres = m12.next()
                        P.act(lambda e, mm=mm, psa=psa: e.activation(out=mm[:, 0, :], in_=psa[:], func=AF.Silu), reads=[pra],
                              writes=[(mres, 0)])
                        P.dve(lambda e, mm=mm, psb=psb, j=j: e.tensor_tensor(out=big[:, j, :], in0=mm[:, 0, :], in1=psb[:], op=ALU.mult),
                              reads=[(mres, 0), prb], writes=[("big", j)])
                if _cstop < 4:
                    continue
                make_gbc(5, c)
                for cb in range(4):
                    for (k0, nk) in ((0, 16), (16, 16), (32, 12)):
                        wt, wres = load_slot([(wfo[:, k0:k0 + nk, cb * 512:(cb + 1) * 512], 0, nk, 0, 512)])
                        for tt in range(4):
                            for kk in range(nk):
                                P.pe(lambda e, wt=wt, kk=kk, k0=k0, tt=tt: e.matmul(
                                    big4[:, tt, :], lhsT=big[:, k0 + kk, tt * 128:(tt + 1) * 128], rhs=wt[:, kk, :],
                                    start=(k0 + kk == 0), stop=(k0 + kk == 43)), reads=[wres, ("big", k0 + kk)], writes=[("big4", tt)])
                    for tt in range(4):
                        P.dve(lambda e, tt=tt, cb=cb: e.tensor_copy(out=yblk[:, tt, cb * 512:(cb + 1) * 512], in_=big4[:, tt, :]),
                              reads=[("big4", tt)], writes=[("yblk", tt, cb)])
                for tt in range(4):
                    tok0 = tb * 512 + tt * 128
                    yres = [("yblk", tt, cb) for cb in range(4)]
                    P.act(lambda e, tt=tt: e.activation(out=junk2[:], in_=yblk[:, tt, :], func=AF.Square, accum_out=ssv[:, tt, 0:1]),
                          reads=yres, writes=["junk2", ("ssv", tt, 0)])
                    rstd_from(ssv[:, tt, 0:1], ssv[:, tt, 1:2], ("ssv", tt, 0), ("ssv", tt, 1))
                    xt, xres = xr.next()
                    P.dma("sp", lambda e, xt=xt, tok0=tok0: e.dma_start(out=xt[:], in_=X1[tok0:tok0 + 128, :]), reads=[("X1", tb, tt)],
                          writes=[xres], key=xres)
                    P.dve(lambda e, tt=tt: e.tensor_scalar(out=yblk[:, tt, :], in0=yblk[:, tt, :], scalar1=ssv[:, tt, 1:2], scalar2=None,
                                                           op0=ALU.mult), reads=yres + [("ssv", tt, 1)], writes=yres)
                    P.dve(lambda e, tt=tt: e.tensor_tensor(out=yblk[:, tt, :], in0=yblk[:, tt, :], in1=Gbc[:], op=ALU.mult),
                          reads=yres + ["Gbc"], writes=yres)
                    P.dve(lambda e, xt=xt, tt=tt: e.tensor_tensor(out=xt[:], in0=xt[:], in1=yblk[:, tt, :], op=ALU.add),
                          reads=yres + [xres], writes=[xres])
                    P.dma("sp", lambda e, xt=xt, tok0=tok0: e.dma_start(out=y_all[tok0:tok0 + 128, :], in_=xt[:]), reads=[xres], key=xres)
            P.flush()
        return nc


def shard_inputs(inputs):
    f = lambda a: np.ascontiguousarray(np.asarray(a, dtype=np.float32))
    xp = f(inputs["x_prompt"])
    xs = f(inputs["x_sample"])
    c = f(inputs["c"])
    c_ctx = f(inputs["c_ctx"])
    shared = {
        "w_mod": f(inputs["w_mod"])[0],
        "b_mod": f(inputs["b_mod"])[0].reshape(96, 128),
        "gvec": np.concatenate([f(inputs[k])[0].reshape(16, 128) for k in
                                ("norm_pre_mix", "norm_post_mix", "norm_pre_ffn", "norm_post_ffn")], axis=0),
        "w_in": f(inputs["w_in"])[0],
        "lb_logits": f(inputs["hgrn_lb_logits"]).reshape(2, 2, 8, 128).reshape(32, 128),
        "hg_g": f(inputs["hgrn_norm_g"])[0].reshape(1, 128),
        "b_if": np.concatenate([f(inputs["mlstm_b_i"])[0], f(inputs["mlstm_b_f"])[0]]).reshape(1, 32),
        "ml_g": f(inputs["mlstm_norm_g"])[0].reshape(1, 1024),
        "w_up_hg": f(inputs["w_up_hgrn"])[0],
        "w_up_ml": f(inputs["w_up_mlstm"])[0],
        "w_out": f(inputs["w_out"])[0],
        "w_ffn_in": f(inputs["w_ffn_in"])[0],
        "w_ffn_out": f(inputs["w_ffn_out"])[0],
    }
    maps = []
    for i in range(NCORES):
        m = dict(shared)
        m["x_all"] = np.concatenate([xp[4 * i:4 * i + 4].reshape(1024, D), xs[i]], axis=0)
        m["cond2"] = np.concatenate([c_ctx.reshape(16, 128), c[i].reshape(16, 128)], axis=0)
        m["st_s"] = f(inputs["state_hgrn_s"])[i, 0]
        m["st_c"] = f(inputs["state_mlstm_c"])[i, 0]
        m["st_n"] = f(inputs["state_mlstm_n"])[i, 0]
        m["st_m"] = f(inputs["state_mlstm_m"])[i, 0].reshape(1, 16)
        maps.append(m)
    return maps


def kernel(**inputs):
    maps = shard_inputs(inputs)
    nc = build_nc()
    rs = run_bass_kernel_spmd(nc, maps, core_ids=list(range(NCORES))).results
    rs2 = rs
    y = np.stack([r["y_all"] for r in rs2])
    y_prompt = y[:, :1024].reshape(32, 256, D)
    y_sample = y[:, 1024:]
    new_s = np.concatenate([r["o_s"] for r in rs], axis=0)[:, None]
    new_c = np.concatenate([r["o_c"] for r in rs], axis=0)[:, None]
    new_n = np.concatenate([r["o_n"] for r in rs], axis=0)[:, None]
    new_m = np.concatenate([r["o_m"].reshape(4, 2, 8) for r in rs], axis=0)[:, None]
    return (np.ascontiguousarray(y_prompt), np.ascontiguousarray(y_sample), np.ascontiguousarray(new_s),
            np.ascontiguousarray(new_c), np.ascontiguousarray(new_n), np.ascontiguousarray(new_m))
```

```python
import contextlib
import numpy as np
import concourse.bass as bass
import concourse.mybir as mybir
from concourse.bass_utils import run_bass_kernel_spmd

F32 = mybir.dt.float32
BF16 = mybir.dt.bfloat16
AF = mybir.ActivationFunctionType
ALU = mybir.AluOpType
AX = mybir.AxisListType

NCORES = 8
NT = 3072
D = 2048
KT = 16
NTT = 24
NTB = 6
NGRP = 12
NCHUNK = 48
DFF = 5632
EPS = 1e-6
NEG = -1.0e30
O_HQ, O_HFF, O_HFB, O_HI, O_HG = 0, 1024, 2048, 3072, 4096
O_MQ, O_MK, O_MV, O_MI, O_MF, O_MO, O_GA, O_GB = 5120, 5632, 6144, 7168, 7184, 7200, 8224, 10272
IN_W = 12320
SEQS = [(0, 4, True, 0), (4, 4, True, 1), (8, 4, True, 2), (12, 4, True, 3), (16, 32, False, 0)]


def bc(ap, dim, n):
    l = [list(x) for x in ap.ap]
    l[dim] = [0, n]
    return bass.AP(ap.tensor, ap.offset, l)


def ins_bc(ap, dim, n):
    l = [list(x) for x in ap.ap]
    l.insert(dim, [0, n])
    return bass.AP(ap.tensor, ap.offset, l)


class Prog:
    ENGS = ("pe", "act", "dve", "pool", "sp")

    def __init__(self, nc, stack, n_dma_sems=80):
        self.nc = nc
        self.eng_sem = {e: stack.enter_context(nc.semaphore("c_" + e)) for e in self.ENGS}
        self.dma_sems = [stack.enter_context(nc.semaphore("d%d" % i)) for i in range(n_dma_sems)]
        self.eng_cnt = {e: 0 for e in self.ENGS}
        self.sem_cnt = [0] * n_dma_sems
        self.pool_keys = {}
        self.discard = False
        self.defer = None
        self._reset()
        self.total_ops = 0

    def _reset(self):
        self.ops = []
        self.last_w = {}
        self.readers = {}
        self.key_idx = {}
        self.key_cnt = {}
        self.n_sp_keys = 0

    def add(self, eng, fn, reads=(), writes=(), dma_key=None, war=()):
        if self.discard:
            return -1
        if self.defer is not None:
            self.defer.append((eng, fn, tuple(reads), tuple(writes), dma_key, tuple(war)))
            return -1
        idx = len(self.ops)
        deps = set()
        for w in war:
            deps.update(self.readers.get(w, ()))
        for r in reads:
            if r in self.last_w:
                deps.add(self.last_w[r])
        for w in writes:
            if w in self.last_w:
                deps.add(self.last_w[w])
            deps.update(self.readers.get(w, ()))
        deps.discard(idx)
        for r in reads:
            self.readers.setdefault(r, []).append(idx)
        for w in writes:
            self.last_w[w] = idx
            self.readers[w] = []
        latest = {}
        pruned = set()
        for d in deps:
            dop = self.ops[d]
            if dop["dma_key"] is None:
                if dop["eng"] not in latest or latest[dop["eng"]] < d:
                    latest[dop["eng"]] = d
            else:
                pruned.add(d)
        pruned.update(latest.values())
        deps = pruned
        dwait = {}
        for d in deps:
            k = self.ops[d]["dma_key"]
            if k is not None:
                dwait[k] = self.key_cnt[k]
        if dma_key is not None:
            if dma_key not in self.key_idx:
                if eng == "pool":
                    if dma_key not in self.pool_keys:
                        self.pool_keys[dma_key] = len(self.dma_sems) - 1 - len(self.pool_keys)
                        assert len(self.pool_keys) <= 8
                    self.key_idx[dma_key] = self.pool_keys[dma_key]
                else:
                    self.n_sp_keys += 1
                    assert self.n_sp_keys <= len(self.dma_sems) - 8, "too many dma keys"
                    self.key_idx[dma_key] = self.n_sp_keys - 1
                self.key_cnt[dma_key] = self.sem_cnt[self.key_idx[dma_key]]
            self.key_cnt[dma_key] += 16
        self.ops.append(dict(eng=eng, fn=fn, deps=deps, dma_key=dma_key, idx=idx, dwait=dwait))
        return idx

    def replay_interleaved(self, main, other, shared):
        last_other = {}
        for i, op in enumerate(other):
            for r in op[2] + op[3]:
                if r in shared:
                    last_other[r] = i
        po = 0
        for op in main:
            need = -1
            for r in op[2] + op[3]:
                if r in last_other:
                    need = max(need, last_other[r])
            while po <= need:
                self.add(*other[po])
                po += 1
            self.add(*op)
            if po < len(other):
                self.add(*other[po])
                po += 1
        while po < len(other):
            self.add(*other[po])
            po += 1

    def pe(self, fn, reads=(), writes=()):
        return self.add("pe", fn, reads, writes)

    def act(self, fn, reads=(), writes=()):
        return self.add("act", fn, reads, writes)

    def dve(self, fn, reads=(), writes=()):
        return self.add("dve", fn, reads, writes)

    def pool(self, fn, reads=(), writes=()):
        return self.add("pool", fn, reads, writes)

    def dma(self, eng, fn, reads=(), writes=(), key=None, war=()):
        assert key is not None
        return self.add(eng, fn, reads, writes, dma_key=key, war=war)

    def flush(self):
        nc = self.nc
        ops = self.ops
        if not ops:
            return
        needed = set()
        for op in ops:
            for d in op["deps"]:
                dop = ops[d]
                if dop["dma_key"] is None and dop["eng"] == "pe" and op["eng"] == "pe" and op["dma_key"] is None:
                    continue
                needed.add(d)
        per_eng = {e: [op for op in ops if op["eng"] == e] for e in self.ENGS}
        for e in self.ENGS:
            comp = [op for op in per_eng[e] if op["dma_key"] is None]
            if comp:
                needed.add(comp[-1]["idx"])
        for op in ops:
            if op["dma_key"] is not None:
                ki = self.key_idx[op["dma_key"]]
                self.sem_cnt[ki] += 16
                op["sig"] = (self.dma_sems[ki], self.sem_cnt[ki], ("k", ki))
            elif op["idx"] in needed:
                self.eng_cnt[op["eng"]] += 1
                op["sig"] = (self.eng_sem[op["eng"]], self.eng_cnt[op["eng"]], ("e", op["eng"]))
            else:
                op["sig"] = None
        final_eng = dict(self.eng_cnt)
        final_keys = [(self.dma_sems[ki], self.sem_cnt[ki]) for ki in range(len(self.dma_sems)) if self.sem_cnt[ki] > 0]
        with nc.Block() as block:
            def make_body(e):
                def body(engine):
                    known = {}
                    for op in per_eng[e]:
                        waits = {}
                        for d in op["deps"]:
                            dop = ops[d]
                            if dop["sig"] is None:
                                continue
                            if dop["dma_key"] is None and dop["eng"] == "pe" and e == "pe" and op["dma_key"] is None:
                                continue
                            sem, val, sk = dop["sig"]
                            if dop["dma_key"] is not None:
                                val = op["dwait"][dop["dma_key"]]
                            if known.get(sk, 0) >= val:
                                continue
                            if sk not in waits or waits[sk][1] < val:
                                waits[sk] = (sem, val)
                        for sk, (sem, val) in waits.items():
                            engine.wait_ge(sem, val)
                            known[sk] = val
                        ins = op["fn"](engine)
                        if op["sig"] is not None:
                            ins.then_inc(op["sig"][0], 16 if op["dma_key"] is not None else 1)
                    for e2 in self.ENGS:
                        if final_eng[e2] > 0:
                            engine.wait_ge(self.eng_sem[e2], final_eng[e2])
                    for sem, val in final_keys:
                        engine.wait_ge(sem, val)
                return body
            block.tensor(make_body("pe"))
            block.scalar(make_body("act"))
            block.vector(make_body("dve"))
            block.gpsimd(make_body("pool"))
            block.sync(make_body("sp"))
        self.total_ops += len(ops)
        self._reset()


class Ring:
    def __init__(self, name, tiles):
        self.name = name
        self.tiles = tiles
        self.i = 0

    def next(self):
        k = self.i % len(self.tiles)
        self.i += 1
        return self.tiles[k], (self.name, k)


def build_nc(debug=False, stop_after=None, part=0):
    nc = bass.Bass("TRN2", target_bir_lowering=False)
    dbg_kind = "ExternalOutput" if debug else "Internal"

    def din(name, shape, dt=F32):
        return nc.dram_tensor(name, shape, dt, kind="ExternalInput").ap()

    def dout(name, shape, dt=F32):
        return nc.dram_tensor(name, shape, dt, kind="ExternalOutput").ap()

    def dscr(name, shape, dt=BF16):
        return nc.dram_tensor(name, shape, dt, kind=dbg_kind).ap()

    x_all = din("x_all", [NT, D])
    cond2 = din("cond2", [32, 128])
    st_s = din("st_s", [2, 8, 128, 128])
    st_c = din("st_c", [2, 8, 64, 128])
    st_n = din("st_n", [2, 8, 64])
    st_m = din("st_m", [1, 16])
    w_mod = din("w_mod", [D, 6 * D])
    b_mod = din("b_mod", [96, 128])
    gvec = din("gvec", [64, 128])
    w_in = din("w_in", [D, IN_W])
    lb_logits = din("lb_logits", [32, 128])
    hg_g = din("hg_g", [1, 128])
    b_if = din("b_if", [1, 32])
    ml_g = din("ml_g", [1, 1024])
    w_up_hg = din("w_up_hg", [1024, D])
    w_up_ml = din("w_up_ml", [1024, D])
    w_out = din("w_out", [D, D])
    w_ffn_in = din("w_ffn_in", [D, 2 * DFF])
    w_ffn_out = din("w_ffn_out", [DFF, D])

    y_all = dout("y_all", [NT, D])
    o_s = dout("o_s", [4, 2, 8, 128, 128])
    o_c = dout("o_c", [4, 2, 8, 64, 128])
    o_n = dout("o_n", [4, 2, 8, 64])
    o_m = dout("o_m", [4, 16])

    QF = dscr("QF", [2, 2, NGRP, 128, 8, 256])
    KF = dscr("KF", [2, 2, NGRP, 128, 8, 256])
    MQ = dscr("MQ", [512, NT])
    MKF = dscr("MKF", [512, NT])
    hkind = {0: dbg_kind, 1: "ExternalOutput", 2: "ExternalInput"}[part]
    SGA = nc.dram_tensor("SGA", [D, NT], BF16, kind=hkind).ap()
    SGB = nc.dram_tensor("SGB", [D, NT], BF16, kind=hkind).ap()
    VV = dscr("VV", [NT, 1024])
    HGATE = dscr("HGATE", [NT, 1024])
    MKT = dscr("MKT", [NT, 512])
    MVV = dscr("MVV", [NT, 1024])
    MOO = dscr("MOO", [NT, 1024])
    GATES = dscr("GATES", [NT, 32], F32)
    OFW = dscr("OFW", [NT, 2048], F32)
    OHG = nc.dram_tensor("OHG", [128, 8, NT], BF16, kind=hkind).ap()
    OML = nc.dram_tensor("OML", [128, 8, NT], BF16, kind=hkind).ap()
    X1 = dscr("X1", [NT, D], F32)

    with contextlib.ExitStack() as outer:
        def sbt(st, name, shape, dt=F32):
            return st.enter_context(nc.sbuf_tensor(name, shape, dt))

        def pst(st, name, shape, dt=F32):
            return st.enter_context(nc.psum_tensor(name, shape, dt))

        P = Prog(nc, outer)

        identf = sbt(outer, "identf", [128, 128])
        identb = sbt(outer, "identb", [128, 128], BF16)
        up01 = sbt(outer, "up01", [64, 64])
        low01 = sbt(outer, "low01", [64, 64])
        upneg = sbt(outer, "upneg", [64, 64])
        lowneg = sbt(outer, "lowneg", [64, 64])
        ones64 = sbt(outer, "ones64", [64, 64])
        onesb = sbt(outer, "onesb", [64, 2], BF16)
        epst = sbt(outer, "epst", [128, 1])
        modv = sbt(outer, "modv", [128, 6, 16, 2])
        lbv = sbt(outer, "lbv", [128, 2, 16])
        dec_all = sbt(outer, "dec_all", [128, 2, 8, NCHUNK])
        hgg_bc = sbt(outer, "hgg_bc", [64, 128])
        mlg_bc = sbt(outer, "mlg_bc", [64, 1024])
        bif_bc = sbt(outer, "bif_bc", [128, 32])

        P.pool(lambda e: e.memset(identf[:], 1.0), writes=["identf"])
        P.pool(lambda e: e.affine_select(out=identf[:], in_=identf[:], pattern=[[-1, 128]], compare_op=ALU.is_equal,
                                        fill=0.0, base=0, channel_multiplier=1), reads=["identf"], writes=["identf"])
        P.dve(lambda e: e.tensor_copy(out=identb[:], in_=identf[:]), reads=["identf"], writes=["identb"])
        for t, nm, val in ((up01, "up01", 1.0), (low01, "low01", 1.0), (upneg, "upneg", 0.0), (lowneg, "lowneg", 0.0),
                           (ones64, "ones64", 1.0), (onesb, "onesb", 1.0), (epst, "epst", EPS)):
            P.pool(lambda e, t=t, val=val: e.memset(t[:], val), writes=[nm])
        P.pool(lambda e: e.affine_select(out=up01[:], in_=up01[:], pattern=[[1, 64]], compare_op=ALU.is_ge, fill=0.0,
                                        base=0, channel_multiplier=-1), reads=["up01"], writes=["up01"])
        P.pool(lambda e: e.affine_select(out=upneg[:], in_=upneg[:], pattern=[[1, 64]], compare_op=ALU.is_ge, fill=NEG,
                                        base=0, channel_multiplier=-1), reads=["upneg"], writes=["upneg"])
        P.pool(lambda e: e.affine_select(out=low01[:], in_=low01[:], pattern=[[-1, 64]], compare_op=ALU.is_ge, fill=0.0,
                                        base=0, channel_multiplier=1), reads=["low01"], writes=["low01"])
        P.pool(lambda e: e.affine_select(out=lowneg[:], in_=lowneg[:], pattern=[[-1, 64]], compare_op=ALU.is_ge, fill=NEG,
                                        base=0, channel_multiplier=1), reads=["lowneg"], writes=["lowneg"])
        P.dma("sp", lambda e: e.dma_start(out=hgg_bc[:], in_=bass.AP(hg_g.tensor, hg_g.offset, [[0, 64], [1, 128]])),
              writes=["hgg_bc"], key="c0")
        P.dma("sp", lambda e: e.dma_start(out=mlg_bc[:], in_=bass.AP(ml_g.tensor, ml_g.offset, [[0, 64], [1, 1024]])),
              writes=["mlg_bc"], key="c0")
        P.dma("sp", lambda e: e.dma_start(out=bif_bc[:], in_=bass.AP(b_if.tensor, b_if.offset, [[0, 128], [1, 32]])),
              writes=["bif_bc"], key="c0")

        with contextlib.ExitStack() as ph:
            rows = sbt(ph, "rows", [128, 256])
            colsT = sbt(ph, "colsT", [128, 256])
            scT = sbt(ph, "scT", [128, 32], BF16)
            modT = sbt(ph, "modT", [128, 96, 2])
            wsl = [sbt(ph, "w0s%d" % i, [128, KT, 512], BF16) for i in range(3)]
            wring = Ring("wsl", wsl)
            tp0 = pst(ph, "tp0", [128, 512])
            modps = pst(ph, "modps", [128, 192])
            rows2 = sbt(ph, "rows2", [128, 128])
            P.dma("sp", lambda e: e.dma_start(out=rows[0:32, 0:128], in_=cond2), writes=["rows"], key="p0a")
            P.dma("sp", lambda e: e.dma_start(out=rows[32:128, 0:128], in_=b_mod), writes=["rows"], key="p0a")
            P.dma("sp", lambda e: e.dma_start(out=rows2[0:64, :], in_=gvec), writes=["rows2"], key="p0b")
            P.dma("sp", lambda e: e.dma_start(out=rows2[64:96, :], in_=lb_logits), writes=["rows2"], key="p0b")
            P.pe(lambda e: e.transpose(tp0[:, 0:128], rows[:, 0:128], identf[:]), reads=["rows", "identf"], writes=["tp0"])
            P.pe(lambda e: e.transpose(tp0[:, 128:224], rows2[0:96, :], identf[0:96, 0:96]), reads=["rows2", "identf"], writes=["tp0"])
            P.dve(lambda e: e.tensor_copy(out=colsT[:, 0:224], in_=tp0[:, 0:224]), reads=["tp0"], writes=["colsT"])
            P.act(lambda e: e.activation(out=scT[:], in_=tp0[:, 0:32], func=AF.Silu), reads=["tp0"], writes=["scT"])
            lg = colsT[:, 192:224].rearrange("p (d l h) -> p d l h", d=2, l=2)
            lb3 = lbv[:, 0, :].rearrange("p (d h) -> p d h", d=2)
            P.dve(lambda e: e.tensor_tensor(out=lb3, in0=lg[:, :, 0, :], in1=lg[:, :, 1, :], op=ALU.subtract),
                  reads=["colsT"], writes=["lbv"])
            P.act(lambda e: e.activation(out=lbv[:, 0, :], in_=lbv[:, 0, :], func=AF.Sigmoid), reads=["lbv"], writes=["lbv"])
            P.dve(lambda e: e.tensor_scalar(out=lbv[:, 1, :], in0=lbv[:, 0, :], scalar1=-1.0, scalar2=1.0, op0=ALU.mult,
                                            op1=ALU.add), reads=["lbv"], writes=["lbv"])
            wm = w_mod.rearrange("(kt p) n -> p kt n", p=128)
            for cb in range(24):
                wt, wres = wring.next()
                P.dma("pool", lambda e, wt=wt, cb=cb: e.dma_start(out=wt[:], in_=wm[:, :, cb * 512:(cb + 1) * 512]),
                      writes=[wres], key=wres)
                for j in range(4):
                    ct = cb * 4 + j
                    for kt in range(KT):
                        rhs = bass.AP(scT[:].tensor, scT[:, kt:kt + 1].offset, [list(scT[:].ap[0]), [16, 2]])
                        P.pe(lambda e, wt=wt, j=j, kt=kt, ct=ct, rhs=rhs: e.matmul(
                            modps[:, ct * 2:ct * 2 + 2], lhsT=wt[:, kt, j * 128:(j + 1) * 128], rhs=rhs,
                            start=(kt == 0), stop=(kt == KT - 1)), reads=[wres, "scT"], writes=["modps"])
            bmT = ins_bc(colsT[:, 32:128], 2, 2)
            P.dve(lambda e: e.tensor_tensor(out=modT[:], in0=modps[:].rearrange("p (c two) -> p c two", two=2), in1=bmT,
                                            op=ALU.add), reads=["modps", "colsT"], writes=["modT"])
            def gv(i):
                return ins_bc(colsT[:, 128 + 16 * i:128 + 16 * (i + 1)], 2, 2)
            def mch(q):
                return modT[:, 16 * q:16 * (q + 1), :]
            P.dve(lambda e: e.scalar_tensor_tensor(out=modv[:, 0], in0=mch(1), scalar=1.0, in1=gv(0), op0=ALU.add,
                                                   op1=ALU.mult), reads=["modT", "colsT"], writes=["modv0"])
            P.dve(lambda e: e.tensor_copy(out=modv[:, 1], in_=mch(0)), reads=["modT"], writes=["modv1"])
            P.dve(lambda e: e.tensor_tensor(out=modv[:, 2], in0=mch(2), in1=gv(1), op=ALU.mult), reads=["modT", "colsT"],
                  writes=["modv2"])
            P.dve(lambda e: e.scalar_tensor_tensor(out=modv[:, 3], in0=mch(4), scalar=1.0, in1=gv(2), op0=ALU.add,
                                                   op1=ALU.mult), reads=["modT", "colsT"], writes=["modv3"])
            P.dve(lambda e: e.tensor_copy(out=modv[:, 4], in_=mch(3)), reads=["modT"], writes=["modv4"])
            P.dve(lambda e: e.tensor_tensor(out=modv[:, 5], in0=mch(5), in1=gv(3), op=ALU.mult), reads=["modT", "colsT"],
                  writes=["modv5"])
            P.flush()
        if stop_after == "0":
            dbg = dout("dbg_modv", [128, 192])
            dbg2 = dout("dbg_lbv", [128, 32])
            P.dma("sp", lambda e: e.dma_start(out=dbg, in_=modv[:].rearrange("p a b c -> p (a b c)")), key="dbg")
            P.dma("sp", lambda e: e.dma_start(out=dbg2, in_=lbv[:].rearrange("p a b -> p (a b)")), key="dbg")
            P.flush()
            return nc

        P.discard = (part == 2)
        with contextlib.ExitStack() as ph:
            hT = sbt(ph, "hT", [128, KT, NT], BF16)
            wsl = [sbt(ph, "was%d" % i, [128, KT, 512], BF16) for i in range(3)]
            wring = Ring("wsl", wsl)
            mmps = Ring("mmps", [pst(ph, "mmps%d" % i, [128, 512]) for i in range(5)])
            win = w_in.rearrange("(kt p) n -> p kt n", p=128)
            with contextlib.ExitStack() as a1:
                xr = Ring("xt", [sbt(a1, "xt%d" % i, [128, D]) for i in range(2)])
                xnr = Ring("xn", [sbt(a1, "xn%d" % i, [128, D]) for i in range(2)])
                junk = sbt(a1, "junk", [128, D], BF16)
                ssr = Ring("ss", [sbt(a1, "ss%d" % i, [128, 2]) for i in range(2)])
                tpr = Ring("tpa", [pst(a1, "tpa%d" % i, [128, 512]) for i in range(2)])
                for tt in range(NTT):
                    c = 0 if tt < 8 else 1
                    xt, xres = xr.next()
                    xn, xnres = xnr.next()
                    ss, ssres = ssr.next()
                    P.dma("sp", lambda e, xt=xt, tt=tt: e.dma_start(out=xt[:], in_=x_all[tt * 128:(tt + 1) * 128, :]),
                          writes=[xres], key=xres)
                    P.act(lambda e, xt=xt, ss=ss: e.activation(out=junk[:], in_=xt[:], func=AF.Square, accum_out=ss[:, 0:1]),
                          reads=[xres], writes=["junk", ssres])
                    P.act(lambda e, ss=ss: e.activation(out=ss[:, 1:2], in_=ss[:, 0:1], func=AF.Ln, scale=1.0 / D, bias=epst[:]),
                          reads=[ssres, "epst"], writes=[ssres])
                    P.act(lambda e, ss=ss: e.activation(out=ss[:, 1:2], in_=ss[:, 1:2], func=AF.Exp, scale=-0.5),
                          reads=[ssres], writes=[ssres])
                    P.dve(lambda e, xt=xt, xn=xn, ss=ss: e.tensor_scalar(out=xn[:], in0=xt[:], scalar1=ss[:, 1:2], scalar2=None,
                                                                         op0=ALU.mult), reads=[xres, ssres], writes=[xnres])

                    for q4 in range(4):
                        tp, tpres = tpr.next()
                        for j in range(4):
                            kt = q4 * 4 + j
                            P.pe(lambda e, tp=tp, j=j, kt=kt, xn=xn: e.transpose(tp[:, j * 128:(j + 1) * 128],
                                                                               xn[:, kt * 128:(kt + 1) * 128], identf[:]),
                                 reads=[xnres, "identf"], writes=[tpres])
                        for j in range(4):
                            kt = q4 * 4 + j
                            dst = hT[:, kt, tt * 128:(tt + 1) * 128]
                            if True:
                                P.act(lambda e, tp=tp, j=j, kt=kt, dst=dst, c=c: e.activation(
                                    out=dst, in_=tp[:, j * 128:(j + 1) * 128], func=AF.Identity,
                                    scale=modv[:, 0, kt, c:c + 1], bias=modv[:, 1, kt, c:c + 1]),
                                    reads=[tpres, "modv0", "modv1"], writes=[("hT", tt, kt)])
                            else:
                                P.dve(lambda e, tp=tp, j=j, kt=kt, dst=dst, c=c: e.tensor_scalar(
                                    out=dst, in0=tp[:, j * 128:(j + 1) * 128], scalar1=modv[:, 0, kt, c:c + 1],
                                    scalar2=modv[:, 1, kt, c:c + 1], op0=ALU.mult, op1=ALU.add),
                                    reads=[tpres, "modv0", "modv1"], writes=[("hT", tt, kt)])
                if stop_after == "A1":
                    dbg = dout("dbg_hT", [128, KT * NT], BF16)
                    for kt_ in range(KT):
                        P.dma("sp", lambda e, kt_=kt_: e.dma_start(out=dbg[:, kt_ * NT:(kt_ + 1) * NT], in_=hT[:, kt_, :]),
                              reads=[("hT", tt, kt_) for tt in range(NTT)], key="dbg")
                    P.flush()
                    return nc

                P.flush()

            def load_w(cols):
                wt, wres = wring.next()
                off = 0
                first = True
                for c0, n in cols:
                    P.dma("pool", lambda e, wt=wt, c0=c0, n=n, off=off: e.dma_start(out=wt[:, :, off:off + n],
                                                                                  in_=win[:, :, c0:c0 + n]),
                          writes=[wres] if first else [], key=wres, war=() if first else ())
                    first = False
                    off += n
                return wt, wres

            def fm_mm(wt, wres, j, tb):
                ps, pres = mmps.next()
                for kt in range(KT):
                    P.pe(lambda e, ps=ps, wt=wt, j=j, kt=kt, tb=tb: e.matmul(
                        ps[:], lhsT=wt[:, kt, j * 128:(j + 1) * 128], rhs=hT[:, kt, tb * 512:(tb + 1) * 512],
                        start=(kt == 0), stop=(kt == KT - 1)),
                        reads=[wres] + [("hT", tb * 4 + i, kt) for i in range(4)], writes=[pres])
                return ps, pres

            def tm_mm(wt, wres, tt, n):
                ps, pres = mmps.next()
                for kt in range(KT):
                    P.pe(lambda e, ps=ps, wt=wt, kt=kt, tt=tt, n=n: e.matmul(
                        ps[:, 0:n], lhsT=hT[:, kt, tt * 128:(tt + 1) * 128], rhs=wt[:, kt, 0:n],
                        start=(kt == 0), stop=(kt == KT - 1)),
                        reads=[wres, ("hT", tt, kt)], writes=[pres])
                return ps, pres

            stg = Ring("stg", [sbt(ph, "stg%d" % i, [128, 512], BF16) for i in range(6)])

            def grp_dst(T, d, kind, h, tb):
                return T[d, kind, 2 * tb:2 * tb + 2, :, h, :].rearrange("g p t -> p g t")

            qTh = sbt(ph, "qTh", [128, NT])
            mask01 = sbt(ph, "mask01", [128, 512])
            P.pool(lambda e: e.memset(mask01[:], 1.0), writes=["mask01"])
            P.pool(lambda e: e.memset(mask01[:].rearrange("p (c j) -> p c j", j=64)[:, :, 0:1], 0.0), reads=["mask01"],
                   writes=["mask01"])
            NTMP = 2
            tmpn = ["sg", "f", "lf", "b", "t", "dm", "e1", "e2"]
            tmps = [{n: sbt(ph, "tmp_%s%d" % (n, i), [128, 512]) for n in tmpn} for i in range(NTMP)]
            smalls = [sbt(ph, "small%d" % i, [128, 3, 8]) for i in range(NTMP)]
            blk = 0
            for h in range(8):
                wt, wres = load_w([(O_HQ + h * 128, 128), (O_HFF + h * 128, 128), (O_HFB + h * 128, 128)])
                for tb in range(NTB):
                    ps, pres = fm_mm(wt, wres, 0, tb)
                    P.act(lambda e, ps=ps, tb=tb: e.activation(out=qTh[:, tb * 512:(tb + 1) * 512], in_=ps[:], func=AF.Silu),
                          reads=[pres], writes=[("qTh", tb)])
                for d in range(2):
                    col = d * 8 + h
                    for tb in range(NTB):
                        T = tmps[blk % NTMP]
                        sm = smalls[blk % NTMP]
                        R = lambda n, b=blk % NTMP: ("tmp", n, b)
                        blk += 1
                        ps, pres = fm_mm(wt, wres, 1 + d, tb)
                        P.act(lambda e, ps=ps, T=T: e.activation(out=T["sg"][:], in_=ps[:], func=AF.Sigmoid),
                              reads=[pres], writes=[R("sg")])
                        P.dve(lambda e, T=T, col=col: e.scalar_tensor_tensor(out=T["f"][:], in0=T["sg"][:], scalar=lbv[:, 1, col:col + 1],
                                                                             in1=bc(lbv[:, 0, col:col + 1], 1, 512), op0=ALU.mult,
                                                                             op1=ALU.add),
                              reads=[R("sg"), "lbv"], writes=[R("f")])
                        P.act(lambda e, T=T: e.activation(out=T["lf"][:], in_=T["f"][:], func=AF.Ln), reads=[R("f")],
                              writes=[R("lf")])
                        P.pool(lambda e, T=T: e.tensor_scalar(out=T["sg"][:], in0=T["f"][:], scalar1=-1.0, scalar2=1.0,
                                                              op0=ALU.mult, op1=ALU.add), reads=[R("f")], writes=[R("sg")])
                        P.dve(lambda e, T=T: e.tensor_tensor_scan(out=T["b"][:], data0=mask01[:], data1=T["lf"][:], initial=0.0,
                                                                  op0=ALU.mult, op1=ALU.add), reads=["mask01", R("lf")],
                              writes=[R("b")])
                        v3 = lambda t: t[:].rearrange("p (c j) -> p c j", j=64)
                        if d == 0:
                            src, rsrc, jref, js1, js2 = T["b"], R("b"), 32, 0, 63
                        else:
                            P.dve(lambda e, T=T: e.tensor_tensor(out=T["t"][:], in0=T["lf"][:], in1=T["b"][:], op=ALU.subtract),
                                  reads=[R("lf"), R("b")], writes=[R("t")])
                            src, rsrc, jref, js1, js2 = T["t"], R("t"), 31, 63, 0
                        P.dve(lambda e, T=T, src=src, jref=jref: e.tensor_tensor(
                            out=v3(T["dm"]), in0=v3(src), in1=bc(v3(src)[:, :, jref:jref + 1], 2, 64), op=ALU.subtract),
                            reads=[rsrc], writes=[R("dm")])
                        P.act(lambda e, T=T: e.activation(out=T["e1"][:], in_=T["dm"][:], func=AF.Exp), reads=[R("dm")],
                              writes=[R("e1")])
                        P.act(lambda e, T=T: e.activation(out=T["e2"][:], in_=T["dm"][:], func=AF.Exp, scale=-1.0),
                              reads=[R("dm")], writes=[R("e2")])
                        P.dve(lambda e, T=T, sm=sm, js1=js1: e.tensor_tensor(out=sm[:, 0, :], in0=v3(T["e2"])[:, :, js1],
                                                                           in1=v3(T["f"])[:, :, js1], op=ALU.mult),
                              reads=[R("e2"), R("f")], writes=[R("sm0")])
                        P.dve(lambda e, T=T, sm=sm, js2=js2: e.tensor_copy(out=sm[:, 1, :], in_=v3(T["e1"])[:, :, js2]),
                              reads=[R("e1")], writes=[R("sm1")])
                        P.dve(lambda e, sm=sm, d=d, h=h, tb=tb: e.tensor_tensor(out=dec_all[:, d, h, tb * 8:(tb + 1) * 8],
                                                                               in0=sm[:, 0, :], in1=sm[:, 1, :], op=ALU.mult),
                              reads=[R("sm0"), R("sm1")], writes=[("dec", d, h, tb)])
                        s1, r1 = stg.next()
                        P.dve(lambda e, T=T, s1=s1, tb=tb: e.tensor_tensor(out=s1[:], in0=qTh[:, tb * 512:(tb + 1) * 512],
                                                                          in1=T["e1"][:], op=ALU.mult),
                              reads=[("qTh", tb), R("e1")], writes=[r1])
                        P.dma("sp", lambda e, s1=s1, d=d, h=h, tb=tb: e.dma_start(
                            out=grp_dst(QF, d, 0, h, tb), in_=s1[:].rearrange("p (g t) -> p g t", g=2)), reads=[r1], key=r1)
                        P.pool(lambda e, T=T, sm=sm, tb=tb: e.tensor_tensor(
                            out=v3(T["lf"]), in0=qTh[:, tb * 512:(tb + 1) * 512].rearrange("p (c j) -> p c j", j=64),
                            in1=bc(sm[:, 0, :].rearrange("p (c o) -> p c o", o=1), 2, 64), op=ALU.mult),
                            reads=[("qTh", tb), R("sm0")], writes=[R("lf")])
                        s2, r2 = stg.next()
                        P.dve(lambda e, T=T, s2=s2: e.tensor_tensor(out=s2[:], in0=T["lf"][:], in1=T["e1"][:], op=ALU.mult),
                              reads=[R("lf"), R("e1")], writes=[r2])
                        P.dma("sp", lambda e, s2=s2, d=d, h=h, tb=tb: e.dma_start(
                            out=grp_dst(QF, d, 1, h, tb), in_=s2[:].rearrange("p (g t) -> p g t", g=2)), reads=[r2], key=r2)
                        s3, r3 = stg.next()
                        P.dve(lambda e, T=T, s3=s3: e.tensor_tensor(out=s3[:], in0=T["sg"][:], in1=T["e2"][:], op=ALU.mult),
                              reads=[R("sg"), R("e2")], writes=[r3])
                        P.dma("sp", lambda e, s3=s3, d=d, h=h, tb=tb: e.dma_start(
                            out=grp_dst(KF, d, 0, h, tb), in_=s3[:].rearrange("p (g t) -> p g t", g=2)), reads=[r3], key=r3)
                        P.pool(lambda e, T=T, sm=sm: e.tensor_tensor(
                            out=v3(T["b"]), in0=v3(T["sg"]), in1=bc(sm[:, 1, :].rearrange("p (c o) -> p c o", o=1), 2, 64),
                            op=ALU.mult), reads=[R("sg"), R("sm1")], writes=[R("b")])
                        s4, r4 = stg.next()
                        P.dve(lambda e, T=T, s4=s4: e.tensor_tensor(out=s4[:], in0=T["b"][:], in1=T["e2"][:], op=ALU.mult),
                              reads=[R("b"), R("e2")], writes=[r4])
                        P.dma("sp", lambda e, s4=s4, d=d, h=h, tb=tb: e.dma_start(
                            out=grp_dst(KF, d, 1, h, tb), in_=s4[:].rearrange("p (g t) -> p g t", g=2)), reads=[r4], key=r4)

            def fm_simple(c0, ntiles, dst, func, scale=1.0):
                for g0 in range(0, ntiles, 4):
                    wt, wres = load_w([(c0 + g0 * 128, 512)])
                    for j in range(4):
                        for tb in range(NTB):
                            ps, pres = fm_mm(wt, wres, j, tb)
                            s, r = stg.next()
                            P.act(lambda e, ps=ps, s=s: e.activation(out=s[:], in_=ps[:], func=func, scale=scale),
                                  reads=[pres], writes=[r])
                            row = (g0 + j) * 128
                            P.dma("sp", lambda e, s=s, row=row, tb=tb: e.dma_start(
                                out=dst[row:row + 128, tb * 512:(tb + 1) * 512], in_=s[:]), reads=[r], key=r)

            fm_simple(O_MQ, 4, MQ, AF.Identity)
            fm_simple(O_MK, 4, MKF, AF.Identity, scale=0.125)

            def tm_simple(c0, ncols, dst, func, scale=1.0):
                for g0 in range(0, ncols, 512):
                    wt, wres = load_w([(c0 + g0, 512)])
                    for tt in range(NTT):
                        ps, pres = tm_mm(wt, wres, tt, 512)
                        s, r = stg.next()
                        if func is None:
                            P.dve(lambda e, ps=ps, s=s: e.tensor_copy(out=s[:], in_=ps[:]), reads=[pres], writes=[r])
                        else:
                            P.act(lambda e, ps=ps, s=s: e.activation(out=s[:], in_=ps[:], func=func, scale=scale),
                                  reads=[pres], writes=[r])
                        P.dma("sp", lambda e, s=s, tt=tt, g0=g0: e.dma_start(
                            out=dst[tt * 128:(tt + 1) * 128, g0:g0 + 512], in_=s[:]), reads=[r], key=r)

            tm_simple(O_HI, 1024, VV, None)
            tm_simple(O_MV, 1024, MVV, None)
            tm_simple(O_MK, 512, MKT, AF.Identity, scale=0.125)
            tm_simple(O_HG, 1024, HGATE, AF.Silu)
            wt, wres = load_w([(O_MI, 32)])
            gst = Ring("gst", [sbt(ph, "gst%d" % i, [128, 32]) for i in range(2)])
            for tt in range(NTT):
                ps, pres = tm_mm(wt, wres, tt, 32)
                g, gr = gst.next()
                P.dve(lambda e, ps=ps, g=g: e.tensor_tensor(out=g[:], in0=ps[:, 0:32], in1=bif_bc[:], op=ALU.add),
                      reads=[pres, "bif_bc"], writes=[gr])
                P.act(lambda e, g=g: e.activation(out=g[:, 16:32], in_=g[:, 16:32], func=AF.Exp, scale=-1.0), reads=[gr],
                      writes=[gr])
                P.act(lambda e, g=g: e.activation(out=g[:, 16:32], in_=g[:, 16:32], func=AF.Ln, bias=1.0, scale=1.0),
                      reads=[gr], writes=[gr])
                P.dve(lambda e, g=g: e.tensor_scalar(out=g[:, 16:32], in0=g[:, 16:32], scalar1=-1.0, scalar2=None,
                                                     op0=ALU.mult), reads=[gr], writes=[gr])
                P.dma("sp", lambda e, g=g, tt=tt: e.dma_start(out=GATES[tt * 128:(tt + 1) * 128, :], in_=g[:]), reads=[gr],
                      key=gr)
            tm_simple(O_MO, 1024, MOO, AF.Sigmoid)
            fm_simple(O_GA, 16, SGA, AF.Sigmoid)
            fm_simple(O_GB, 16, SGB, AF.Sigmoid)
            P.flush()
        if stop_after == "A":
            return nc

        with contextlib.ExitStack() as ph:
            S = sbt(ph, "S", [128, 8, 128]); Sbf = sbt(ph, "Sbf", [128, 8, 128], BF16)
            C = sbt(ph, "C", [64, 8, 128]); Cbf = sbt(ph, "Cbf", [64, 8, 128], BF16)
            nst = sbt(ph, "nst", [64, 8]); nbf = sbt(ph, "nbf", [64, 8], BF16); mst = sbt(ph, "mst", [64, 8])
            nrow = sbt(ph, "nrow", [8, 64])
            NS = 2
            gt = []
            for i in range(NS):
                gt.append(dict(
                    qt=sbt(ph, "g_qt%d" % i, [128, 8, 256], BF16), qh=sbt(ph, "g_qh%d" % i, [128, 8, 256], BF16),
                    kt=sbt(ph, "g_kt%d" % i, [128, 8, 256], BF16), kh=sbt(ph, "g_kh%d" % i, [128, 8, 256], BF16),
                    vv=sbt(ph, "g_vv%d" % i, [64, 4, 1024], BF16), mvv=sbt(ph, "g_mvv%d" % i, [64, 4, 1024], BF16),
                    mkt=sbt(ph, "g_mkt%d" % i, [64, 4, 512], BF16), gat=sbt(ph, "g_gat%d" % i, [64, 4, 32]),
                    mq=sbt(ph, "g_mq%d" % i, [64, 8, 256], BF16), mkf=sbt(ph, "g_mkf%d" % i, [64, 8, 256], BF16)))
            cext = [dict(ofw=sbt(ph, "c_ofw%d" % i, [64, 2048]), hgate=sbt(ph, "c_hg%d" % i, [64, 1024], BF16),
                         moo=sbt(ph, "c_mo%d" % i, [64, 1024], BF16)) for i in range(1)]
            osbr = Ring("osb", [sbt(ph, "osb%d" % i, [64, 2048]) for i in range(2)])
            att_sb = sbt(ph, "att_sb", [64, 8, 64], BF16)
            khat_sb = sbt(ph, "khat_sb", [64, 1024], BF16)
            diag = sbt(ph, "diag", [64, 8, 64])
            masked = sbt(ph, "masked", [64, 8, 64])
            t1 = masked; Et = masked
            t2 = sbt(ph, "t2", [64, 8, 64]); Sint = t2
            PT = sbt(ph, "PT", [64, 8, 64], BF16); qTs = sbt(ph, "qTs", [64, 8, 64], BF16); wk = sbt(ph, "wk", [64, 8, 64], BF16)
            tC = sbt(ph, "tC", [64, 8, 128])
            sm = sbt(ph, "smB", [64, 16, 8])
            gsb = sbt(ph, "gsb", [64, 16])
            osum = sbt(ph, "osum", [64, 2048]); sqb = sbt(ph, "sqb", [64, 1024]); ohn = sbt(ph, "ohn", [64, 1024])
            ohb = sbt(ph, "ohb", [64, 2, 1024], BF16)
            stT = [[sbt(ph, "stT%d_%d" % (b, i), [128, 8, 256], BF16) for i in range(1)] for b in range(2)]
            psA = pst(ph, "psA", [128, 512]); psB = pst(ph, "psB", [128, 512]); psO = pst(ph, "psO", [128, 1024])
            psD = pst(ph, "psD", [128, 1024]); psU = pst(ph, "psU", [128, 512]); psS = pst(ph, "psS", [128, 512])
            psBb = psB[:].bitcast(BF16)
            SMN = ["u", "cmax", "cm", "M", "wexp", "mt", "ex", "aden", "rd", "am", "ml", "mnew", "d12a", "d12b", "so", "sl"]
            smv = {n: sm[:, i, :] for i, n in enumerate(SMN)}
            d12 = sm[:, 12:14, :]
            sosl = sm[:, 14:16, :]
            id64 = identf[0:64, 0:64]
            MQr = MQ.rearrange("(h d) t -> d h t", d=64)
            MKFr = MKF.rearrange("(h d) t -> d h t", d=64)
            gset_i = [0]
            cset_i = [0]
            stT_i = [0]

            def load_group(gi, d):
                k = gset_i[0] % NS
                gset_i[0] += 1
                G = gt[k]
                key = ("gset", k)
                tl = slice(gi * 256, (gi + 1) * 256)
                names = list(G.keys())
                srcs = dict(qt=QF[d, 0, gi], qh=QF[d, 1, gi], kt=KF[d, 0, gi], kh=KF[d, 1, gi],
                            vv=VV[tl, :].rearrange("(c p) n -> p c n", p=64), mvv=MVV[tl, :].rearrange("(c p) n -> p c n", p=64),
                            mkt=MKT[tl, :].rearrange("(c p) n -> p c n", p=64), gat=GATES[tl, :].rearrange("(c p) n -> p c n", p=64),
                            mq=MQr[:, :, tl], mkf=MKFr[:, :, tl])
                allres = [("g", n, k) for n in names]
                for i, n in enumerate(names):
                    P.dma("sp", lambda e, dst=G[n], src=srcs[n]: e.dma_start(out=dst[:], in_=src),
                          writes=[("g", n, k)], key=key, war=allres if i == 0 else ())
                return G, k

            def chunk_step(G, k, ck, d, last_of_pass, seq, bwd_out):
                ci = ck % 4
                cs = slice(ci * 64, (ci + 1) * 64)
                gr = lambda n: ("g", n, k)
                attmask = up01 if d == 0 else low01
                tri = up01 if d == 0 else low01
                cummask = lowneg if d == 0 else upneg
                emask = upneg if d == 0 else lowneg
                rn = lambda n: "up01" if n is up01 else "low01" if n is low01 else "upneg" if n is upneg else "lowneg"
                osb, osr = osbr.next()
                tok0 = ck * 64
                if bwd_out:
                    X = cext[0]
                    xk = ("cext", 0)
                    P.dma("sp", lambda e: e.dma_start(out=X["ofw"][:], in_=OFW[tok0:tok0 + 64, :]), reads=[("OFW", ck)], writes=[(xk, "ofw")],
                          key=xk, war=[(xk, "hgate"), (xk, "moo")])
                    P.dma("sp", lambda e: e.dma_start(out=X["hgate"][:], in_=HGATE[tok0:tok0 + 64, :]), writes=[(xk, "hgate")], key=xk)
                    P.dma("sp", lambda e: e.dma_start(out=X["moo"][:], in_=MOO[tok0:tok0 + 64, :]), writes=[(xk, "moo")], key=xk)
                chain_h, chain_m = [], []
                P.defer = chain_h
                for h in range(8):
                    P.pe(lambda e, h=h: e.matmul(psA[0:64, h * 64:(h + 1) * 64], lhsT=G["kt"][:, h, cs], rhs=G["qt"][:, h, cs],
                                                 start=True, stop=True), reads=[gr("kt"), gr("qt")], writes=["psA"])
                P.dve(lambda e: e.tensor_tensor(out=att_sb[:], in0=psA[0:64, :].rearrange("p (h l) -> p h l", h=8),
                                                in1=ins_bc(attmask[:], 1, 8), op=ALU.mult), reads=["psA", rn(attmask)],
                      writes=["att_sb"])
                for h in range(8):
                    P.pe(lambda e, h=h: e.transpose(psBb[0:64, h * 128:(h + 1) * 128], G["kh"][:, h, cs], identb[:]),
                         reads=[gr("kh"), "identb"], writes=["psB"])
                P.act(lambda e: e.copy(out=khat_sb[:], in_=psBb[0:64, :]), reads=["psB"], writes=["khat_sb"])
                for h in range(8):
                    hs = slice(h * 128, (h + 1) * 128)
                    P.pe(lambda e, h=h, hs=hs: e.matmul(psO[0:64, hs], lhsT=att_sb[:, h, :], rhs=G["vv"][:, ci, hs], start=True,
                                                        stop=False), reads=["att_sb", gr("vv")], writes=["psO"])
                    P.pe(lambda e, h=h, hs=hs: e.matmul(psO[0:64, hs], lhsT=G["qh"][:, h, cs], rhs=Sbf[:, h, :], start=False,
                                                        stop=True), reads=[gr("qh"), "Sbf"], writes=["psO"])
                P.act(lambda e: e.copy(out=osb[:, 0:1024], in_=psO[0:64, :]), reads=["psO"], writes=[(osr, 0)])
                for h in range(8):
                    hs = slice(h * 128, (h + 1) * 128)
                    P.pe(lambda e, hs=hs: e.matmul(psD[:, hs], lhsT=khat_sb[:, hs], rhs=G["vv"][:, ci, hs], start=True, stop=True),
                         reads=["khat_sb", gr("vv")], writes=["psD"])
                decv = bc(dec_all[:, d, :, ck:ck + 1], 2, 128)
                P.dve(lambda e: e.tensor_tensor(out=S[:], in0=S[:], in1=decv, op=ALU.mult), reads=["S"] + [("dec", d, h, ck // 8) for h in range(8)],
                      writes=["S"])
                P.dve(lambda e: e.tensor_tensor(out=S[:], in0=S[:], in1=psD[:].rearrange("p (h v) -> p h v", h=8), op=ALU.add),
                      reads=["S", "psD"], writes=["S"])
                P.act(lambda e: e.copy(out=Sbf[:], in_=S[:]), reads=["S"], writes=["Sbf"])
                P.defer = chain_m
                li = G["gat"][:, ci, d * 8:(d + 1) * 8]
                lfm = G["gat"][:, ci, 16 + d * 8:16 + (d + 1) * 8]
                P.pe(lambda e: e.matmul(psS[0:64, 0:8], lhsT=tri[:], rhs=lfm, start=True, stop=True), reads=[rn(tri), gr("gat")],
                     writes=["psS"])
                P.pe(lambda e: e.matmul(psS[0:64, 8:16], lhsT=ones64[:], rhs=lfm, start=True, stop=True), reads=["ones64", gr("gat")],
                     writes=["psS"])
                P.dve(lambda e: e.tensor_tensor(out=smv["u"], in0=li, in1=psS[0:64, 0:8], op=ALU.subtract), reads=[gr("gat"), "psS"],
                      writes=["u"])
                P.dve(lambda e: e.tensor_copy(out=gsb[:], in_=psS[0:64, 0:16]), reads=["psS"], writes=["gsb"])
                v1 = lambda a: a.rearrange("p (h o) -> p h o", o=1)
                P.dve(lambda e: e.tensor_tensor(out=diag[:], in0=ins_bc(id64, 1, 8), in1=bc(v1(smv["u"]), 2, 64), op=ALU.mult),
                      reads=["identf", "u"], writes=["diag"])
                P.pe(lambda e: e.matmul(psU[0:64, :], lhsT=ones64[:], rhs=diag[:].rearrange("p h s -> p (h s)"), start=True, stop=True),
                     reads=["ones64", "diag"], writes=["psU"])
                psU3 = psU[0:64, :].rearrange("p (h s) -> p h s", h=8)
                P.dve(lambda e: e.tensor_reduce(out=smv["cmax"], in_=psU3, axis=AX.X, op=ALU.max), reads=["psU"], writes=["cmax"])
                P.dve(lambda e: e.tensor_tensor(out=masked[:], in0=psU3, in1=ins_bc(cummask[:], 1, 8), op=ALU.add),
                      reads=["psU", rn(cummask)], writes=["masked"])
                P.dve(lambda e: e.tensor_reduce(out=smv["cm"], in_=masked[:], axis=AX.X, op=ALU.max), reads=["masked"], writes=["cm"])
                P.dve(lambda e: e.tensor_tensor(out=smv["M"], in0=smv["cm"], in1=mst[:], op=ALU.max), reads=["cm", "mst"], writes=["M"])
                P.dve(lambda e: e.tensor_tensor(out=diag[:], in0=ins_bc(id64, 1, 8), in1=bc(v1(smv["M"]), 2, 64), op=ALU.mult),
                      reads=["identf", "M", "diag"], writes=["diag"])
                P.pe(lambda e: e.matmul(psU[0:64, :], lhsT=ones64[:], rhs=diag[:].rearrange("p h s -> p (h s)"), start=True, stop=True),
                     reads=["ones64", "diag"], writes=["psU"])
                P.dve(lambda e: e.tensor_tensor(out=t1[:], in0=bc(v1(smv["u"]), 2, 64), in1=psU3, op=ALU.subtract), reads=["u", "psU", "masked"],
                      writes=["masked"])
                P.dve(lambda e: e.tensor_tensor(out=t1[:], in0=t1[:], in1=ins_bc(emask[:], 1, 8), op=ALU.add), reads=["masked", rn(emask)],
                      writes=["masked"])
                P.act(lambda e: e.activation(out=Et[:], in_=t1[:], func=AF.Exp), reads=["masked"], writes=["masked"])
                P.dve(lambda e: e.tensor_tensor(out=t2[:], in0=bc(v1(mst[:]), 2, 64), in1=psU3, op=ALU.subtract), reads=["mst", "psU"],
                      writes=["t2"])
                P.act(lambda e: e.activation(out=Sint[:], in_=t2[:], func=AF.Exp), reads=["t2"], writes=["t2"])
                for h in range(8):
                    P.pe(lambda e, h=h: e.matmul(psA[0:64, h * 64:(h + 1) * 64], lhsT=G["mkf"][:, h, cs], rhs=G["mq"][:, h, cs],
                                                 start=True, stop=True), reads=[gr("mkf"), gr("mq")], writes=["psA"])
                P.dve(lambda e: e.tensor_tensor(out=PT[:], in0=psA[0:64, :].rearrange("p (h l) -> p h l", h=8), in1=Et[:], op=ALU.mult),
                      reads=["psA", "masked"], writes=["PT"])
                P.dve(lambda e: e.tensor_tensor(out=qTs[:], in0=G["mq"][:, :, cs], in1=Sint[:], op=ALU.mult), reads=[gr("mq"), "t2"],
                      writes=["qTs"])
                P.dve(lambda e: e.tensor_tensor(out=smv["wexp"], in0=smv["u"], in1=smv["cmax"], op=ALU.subtract), reads=["u", "cmax"],
                      writes=["wexp"])
                P.act(lambda e: e.activation(out=smv["wexp"], in_=smv["wexp"], func=AF.Exp), reads=["wexp"], writes=["wexp"])
                P.dve(lambda e: e.tensor_tensor(out=wk[:], in0=G["mkt"][:, ci, :].rearrange("p (h x) -> p h x", h=8),
                                                in1=bc(v1(smv["wexp"]), 2, 64), op=ALU.mult), reads=[gr("mkt"), "wexp"], writes=["wk"])
                for h in range(8):
                    hs = slice(h * 128, (h + 1) * 128)
                    P.pe(lambda e, h=h, hs=hs: e.matmul(psO[0:64, hs], lhsT=PT[:, h, :], rhs=G["mvv"][:, ci, hs], start=True, stop=False),
                         reads=["PT", gr("mvv")], writes=["psO"])
                    P.pe(lambda e, h=h, hs=hs: e.matmul(psO[0:64, hs], lhsT=qTs[:, h, :], rhs=Cbf[:, h, :], start=False, stop=True),
                         reads=["qTs", "Cbf"], writes=["psO"])
                    P.pe(lambda e, h=h: e.matmul(psS[0:64, 16 + h:17 + h], lhsT=PT[:, h, :], rhs=onesb[:, 0:1], start=True, stop=False),
                         reads=["PT", "onesb"], writes=["psS"])
                    P.pe(lambda e, h=h: e.matmul(psS[0:64, 16 + h:17 + h], lhsT=qTs[:, h, :], rhs=nbf[:, h:h + 1], start=False, stop=True),
                         reads=["qTs", "nbf"], writes=["psS"])
                for h in range(8):
                    hs = slice(h * 128, (h + 1) * 128)
                    P.pe(lambda e, h=h, hs=hs: e.matmul(psD[0:64, hs], lhsT=wk[:, h, :], rhs=G["mvv"][:, ci, hs], start=True, stop=True),
                         reads=["wk", gr("mvv")], writes=["psD"])
                    P.pe(lambda e, h=h: e.matmul(psS[0:64, 24 + h:25 + h], lhsT=wk[:, h, :], rhs=onesb[:, 0:1], start=True, stop=True),
                         reads=["wk", "onesb"], writes=["psS"])
                P.dve(lambda e: e.tensor_tensor(out=smv["mt"], in0=gsb[:, 0:8], in1=smv["M"], op=ALU.add), reads=["gsb", "M"], writes=["mt"])
                P.act(lambda e: e.activation(out=smv["ex"], in_=smv["mt"], func=AF.Exp, scale=-1.0), reads=["mt"], writes=["ex"])
                P.act(lambda e: e.activation(out=smv["aden"], in_=psS[0:64, 16:24], func=AF.Abs), reads=["psS"], writes=["aden"])
                P.dve(lambda e: e.tensor_tensor(out=smv["rd"], in0=smv["aden"], in1=smv["ex"], op=ALU.max), reads=["aden", "ex"], writes=["rd"])
                P.dve(lambda e: e.reciprocal(out=smv["rd"], in_=smv["rd"]), reads=["rd"], writes=["rd"])
                P.dve(lambda e: e.tensor_tensor(out=osb[:, 1024:2048].rearrange("p (h v) -> p h v", h=8),
                                                in0=psO[0:64, :].rearrange("p (h v) -> p h v", h=8), in1=bc(v1(smv["rd"]), 2, 128),
                                                op=ALU.mult), reads=["psO", "rd"], writes=[(osr, 1)])
                P.dve(lambda e: e.tensor_tensor(out=smv["am"], in0=gsb[:, 8:16], in1=mst[:], op=ALU.add), reads=["gsb", "mst"], writes=["am"])
                P.dve(lambda e: e.tensor_tensor(out=smv["ml"], in0=gsb[:, 8:16], in1=smv["cmax"], op=ALU.add), reads=["gsb", "cmax"],
                      writes=["ml"])
                P.dve(lambda e: e.tensor_tensor(out=smv["mnew"], in0=smv["am"], in1=smv["ml"], op=ALU.max), reads=["am", "ml"],
                      writes=["mnew"])
                P.dve(lambda e: e.tensor_tensor(out=smv["d12a"], in0=smv["am"], in1=smv["mnew"], op=ALU.subtract), reads=["am", "mnew"],
                      writes=["d12a"])
                P.dve(lambda e: e.tensor_tensor(out=smv["d12b"], in0=smv["ml"], in1=smv["mnew"], op=ALU.subtract), reads=["ml", "mnew"],
                      writes=["d12b"])
                P.act(lambda e: e.activation(out=sosl, in_=d12, func=AF.Exp), reads=["d12a", "d12b"], writes=["sosl"])
                P.dve(lambda e: e.tensor_copy(out=mst[:], in_=smv["mnew"]), reads=["mnew"], writes=["mst"])
                P.dve(lambda e: e.tensor_tensor(out=tC[:], in0=psD[0:64, :].rearrange("p (h v) -> p h v", h=8),
                                                in1=bc(v1(smv["sl"]), 2, 128), op=ALU.mult), reads=["psD", "sosl"], writes=["tC"])
                P.dve(lambda e: e.tensor_tensor(out=C[:], in0=C[:], in1=bc(v1(smv["so"]), 2, 128), op=ALU.mult), reads=["C", "sosl"],
                      writes=["C"])
                P.dve(lambda e: e.tensor_tensor(out=C[:], in0=C[:], in1=tC[:], op=ALU.add), reads=["C", "tC"], writes=["C"])
                P.act(lambda e: e.copy(out=Cbf[:], in_=C[:]), reads=["C"], writes=["Cbf"])
                P.dve(lambda e: e.tensor_tensor(out=nst[:], in0=nst[:], in1=smv["so"], op=ALU.mult), reads=["nst", "sosl"], writes=["nst"])
                P.dve(lambda e: e.tensor_tensor(out=smv["aden"], in0=psS[0:64, 24:32], in1=smv["sl"], op=ALU.mult),
                      reads=["psS", "sosl", "aden"], writes=["aden"])
                P.dve(lambda e: e.tensor_tensor(out=nst[:], in0=nst[:], in1=smv["aden"], op=ALU.add), reads=["nst", "aden"], writes=["nst"])
                P.act(lambda e: e.copy(out=nbf[:], in_=nst[:]), reads=["nst"], writes=["nbf"])
                P.defer = None
                P.replay_interleaved(chain_m, chain_h, ("psA", "psO", "psD", "psB", "psS", "psU"))
                tok0 = ck * 64
                if not bwd_out:
                    P.dma("sp", lambda e: e.dma_start(out=OFW[tok0:tok0 + 64, :], in_=osb[:]), reads=[(osr, 0), (osr, 1)],
                          writes=[("OFW", ck)], key=osr)
                    return
                P.dve(lambda e: e.tensor_tensor(out=osum[:], in0=osb[:], in1=X["ofw"][:], op=ALU.add), reads=[(osr, 0), (osr, 1), (xk, "ofw")],
                      writes=["osum"])
                o3 = lambda a: a.rearrange("p (h v) -> p h v", h=8)
                P.pool(lambda e: e.tensor_tensor(out=sqb[:], in0=osum[:, 0:1024], in1=osum[:, 0:1024], op=ALU.mult), reads=["osum"],
                       writes=["sqb"])
                P.dve(lambda e: e.tensor_reduce(out=smv["am"], in_=o3(sqb[:]), axis=AX.X, op=ALU.add), reads=["sqb"], writes=["am"])
                P.act(lambda e: e.activation(out=smv["am"], in_=smv["am"], func=AF.Ln, scale=1.0 / 128, bias=epst[0:64, :]), reads=["am", "epst"],
                      writes=["am"])
                P.act(lambda e: e.activation(out=smv["am"], in_=smv["am"], func=AF.Exp, scale=-0.5), reads=["am"], writes=["am"])
                P.dve(lambda e: e.tensor_tensor(out=o3(ohn[:]), in0=o3(osum[:, 0:1024]), in1=bc(v1(smv["am"]), 2, 128), op=ALU.mult),
                      reads=["osum", "am"], writes=["ohn"])
                P.dve(lambda e: e.tensor_tensor(out=o3(ohn[:]), in0=o3(ohn[:]), in1=ins_bc(hgg_bc[:], 1, 8), op=ALU.mult),
                      reads=["ohn", "hgg_bc"], writes=["ohn"])
                P.dve(lambda e: e.tensor_tensor(out=ohb[:, 0, :], in0=ohn[:], in1=X["hgate"][:], op=ALU.mult), reads=["ohn", (xk, "hgate")],
                      writes=["ohb0"])
                P.dve(lambda e: e.tensor_reduce(out=smv["ml"], in_=o3(osum[:, 1024:2048]), axis=AX.X, op=ALU.add), reads=["osum"],
                      writes=["ml"])
                P.dve(lambda e: e.tensor_scalar(out=smv["ml"], in0=smv["ml"], scalar1=1.0 / 128, scalar2=None, op0=ALU.mult), reads=["ml"],
                      writes=["ml"])
                P.dve(lambda e: e.tensor_tensor(out=o3(ohn[:]), in0=o3(osum[:, 1024:2048]), in1=bc(v1(smv["ml"]), 2, 128), op=ALU.subtract),
                      reads=["osum", "ml", "ohn"], writes=["ohn"])
                P.pool(lambda e: e.tensor_tensor(out=sqb[:], in0=ohn[:], in1=ohn[:], op=ALU.mult), reads=["ohn", "sqb"], writes=["sqb"])
                P.dve(lambda e: e.tensor_reduce(out=smv["mnew"], in_=o3(sqb[:]), axis=AX.X, op=ALU.add), reads=["sqb"], writes=["mnew"])
                P.act(lambda e: e.activation(out=smv["mnew"], in_=smv["mnew"], func=AF.Ln, scale=1.0 / 128, bias=epst[0:64, :]),
                      reads=["mnew", "epst"], writes=["mnew"])
                P.act(lambda e: e.activation(out=smv["mnew"], in_=smv["mnew"], func=AF.Exp, scale=-0.5), reads=["mnew"], writes=["mnew"])
                P.dve(lambda e: e.tensor_tensor(out=o3(ohn[:]), in0=o3(ohn[:]), in1=bc(v1(smv["mnew"]), 2, 128), op=ALU.mult),
                      reads=["ohn", "mnew"], writes=["ohn"])
                P.dve(lambda e: e.tensor_tensor(out=ohn[:], in0=ohn[:], in1=mlg_bc[:], op=ALU.mult), reads=["ohn", "mlg_bc"], writes=["ohn"])
                P.dve(lambda e: e.tensor_tensor(out=ohb[:, 1, :], in0=ohn[:], in1=X["moo"][:], op=ALU.mult), reads=["ohn", (xk, "moo")],
                      writes=["ohb1"])
                si = 0
                for b, dst in ((0, OHG), (1, OML)):
                    psT = psBb[:, 0:512].rearrange("p (h t) -> p h t", h=8)
                    for h in range(8):
                        P.pe(lambda e, h=h, b=b: e.transpose(psBb[:, h * 64:(h + 1) * 64], ohb[:, b, h * 128:(h + 1) * 128], identb[0:64, 0:64]),
                             reads=["ohb%d" % b, "identb"], writes=["psB"])
                    P.act(lambda e, b=b, psT=psT: e.copy(out=stT[b][si][:, :, cs], in_=psT), reads=["psB"], writes=[("stT", b, si)])
                    if last_of_pass:
                        g = ck // 4
                        P.dma("sp", lambda e, b=b, dst=dst, g=g: e.dma_start(out=dst[:, :, g * 256:(g + 1) * 256], in_=stT[b][si][:]),
                              reads=[("stT", b, si)], key=("stT", b, si))
                if last_of_pass:
                    stT_i[0] += 1

            visits = []
            for (c0_, n_, _p, _l) in SEQS:
                for d_ in range(2):
                    gl_ = list(range(c0_ // 4, (c0_ + n_) // 4))
                    visits += [(g_, d_) for g_ in (gl_ if d_ == 0 else gl_[::-1])]
            visit_pos = [0]
            loaded = {}
            for (c0, nchk, is_prompt, li_) in SEQS:
                for d in range(2):
                    if is_prompt:
                        P.pool(lambda e: e.memset(S[:], 0.0), writes=["S"])
                        P.pool(lambda e: e.memset(Sbf[:], 0.0), writes=["Sbf"])
                        P.pool(lambda e: e.memset(C[:], 0.0), writes=["C"])
                        P.pool(lambda e: e.memset(Cbf[:], 0.0), writes=["Cbf"])
                        P.pool(lambda e: e.memset(nst[:], 0.0), writes=["nst"])
                        P.pool(lambda e: e.memset(nbf[:], 0.0), writes=["nbf"])
                        P.pool(lambda e: e.memset(mst[:], 0.0), writes=["mst"])
                    else:
                        P.dma("sp", lambda e, d=d: e.dma_start(out=S[:], in_=st_s[d].rearrange("h k v -> k h v")), writes=["S"], key="st0")
                        P.dma("sp", lambda e, d=d: e.dma_start(out=C[:], in_=st_c[d].rearrange("h k v -> k h v")), writes=["C"], key="st0")
                        P.dma("sp", lambda e, d=d: e.dma_start(out=nrow[:], in_=st_n[d]), writes=["nrow"], key="st0")
                        P.dma("sp", lambda e, d=d: e.dma_start(out=mst[:], in_=bass.AP(st_m.tensor, st_m.offset + d * 8, [[0, 64], [1, 8]])),
                              writes=["mst"], key="st0")
                        P.pe(lambda e: e.transpose(psS[0:64, 32:40], nrow[:], identf[0:8, 0:8]), reads=["nrow", "identf"], writes=["psS"])
                        P.dve(lambda e: e.tensor_copy(out=nst[:], in_=psS[0:64, 32:40]), reads=["psS"], writes=["nst"])
                        P.pool(lambda e: e.tensor_copy(out=Sbf[:], in_=S[:]), reads=["S"], writes=["Sbf"])
                        P.pool(lambda e: e.tensor_copy(out=Cbf[:], in_=C[:]), reads=["C"], writes=["Cbf"])
                        P.pool(lambda e: e.tensor_copy(out=nbf[:], in_=nst[:]), reads=["nst"], writes=["nbf"])
                    order = list(range(c0, c0 + nchk)) if d == 0 else list(range(c0 + nchk - 1, c0 - 1, -1))
                    cur_g = None
                    for i, ck in enumerate(order):
                        if ck // 4 != cur_g:
                            cur_g = ck // 4
                            vi = visit_pos[0]
                            visit_pos[0] += 1
                            if vi not in loaded:
                                loaded[vi] = load_group(*visits[vi])
                            G, k = loaded[vi]
                            if vi + 1 < len(visits) and (vi + 1) not in loaded:
                                loaded[vi + 1] = load_group(*visits[vi + 1])
                        last_in_grp = (i == len(order) - 1) or (order[i + 1] // 4 != cur_g)
                        chunk_step(G, k, ck, d, last_in_grp, li_, d == 1)
                    if is_prompt:
                        b = li_
                        P.dma("sp", lambda e, b=b, d=d: e.dma_start(out=o_s[b, d].rearrange("h k v -> k h v"), in_=S[:]), reads=["S"],
                              key="fin")
                        P.dma("sp", lambda e, b=b, d=d: e.dma_start(out=o_c[b, d].rearrange("h k v -> k h v"), in_=C[:]), reads=["C"],
                              key="fin")
                        P.pe(lambda e: e.transpose(psS[0:8, 64:128], nst[:], id64), reads=["nst", "identf"], writes=["psS"])
                        P.dve(lambda e: e.tensor_copy(out=nrow[:], in_=psS[0:8, 64:128]), reads=["psS"], writes=["nrow"])
                        P.dma("sp", lambda e, b=b, d=d: e.dma_start(out=o_n[b, d], in_=nrow[:]), reads=["nrow"], key="fin")
                        P.dma("sp", lambda e, b=b, d=d: e.dma_start(out=o_m[b:b + 1, d * 8:(d + 1) * 8], in_=mst[0:1, :]), reads=["mst"],
                              key="fin")
            P.flush()
        if stop_after == "B" or part == 1:
            return nc
        P.discard = False

        with contextlib.ExitStack() as ph:
            actA = sbt(ph, "actA", [128, 16, 512], BF16)
            big = sbt(ph, "big", [128, 44, 512], BF16)
            h2T = sbt(ph, "h2T", [128, 16, 512], BF16)
            hbr = Ring("hb", [sbt(ph, "hb%d" % i, [128, D], BF16) for i in range(2)])
            wsl = [sbt(ph, "wcs%d" % i, [128, KT, 512], BF16) for i in range(2)]
            wring = Ring("wsl", wsl)
            sgr = Ring("sg", [sbt(ph, "sgt%d" % i, [128, 2, 512], BF16) for i in range(2)])
            xr = Ring("xc", [sbt(ph, "xc%d" % i, [128, D]) for i in range(2)])
            yblk = sbt(ph, "yblk", [128, 4, D])
            Gbc = sbt(ph, "Gbc", [128, D])
            tmpb = Ring("tmpb", [sbt(ph, "tmpb%d" % i, [128, 128]) for i in range(2)])
            junkc = sbt(ph, "junkc", [128, 512], BF16)
            junk2 = sbt(ph, "junk2", [128, D], BF16)
            m12 = Ring("m12", [sbt(ph, "m12_%d" % i, [128, 2, 512]) for i in range(2)])
            ssp = sbt(ph, "ssp", [128, 4, 4])
            ssv = sbt(ph, "ssv", [128, 4, 4])
            mmr = Ring("mmc", [pst(ph, "mmc%d" % i, [128, 512]) for i in range(3)])
            tpcr = Ring("tpc", [pst(ph, "tpc%d" % i, [128, 512]) for i in range(1)])
            big4 = pst(ph, "big4", [128, 4, 512])
            wuh = w_up_hg.rearrange("(kt p) n -> p kt n", p=128)
            wum = w_up_ml.rearrange("(kt p) n -> p kt n", p=128)
            wo = w_out.rearrange("(kt p) n -> p kt n", p=128)
            wfi = w_ffn_in.rearrange("(kt p) n -> p kt n", p=128)
            wfo = w_ffn_out.rearrange("(kt p) n -> p kt n", p=128)

            def load_slot(pieces):
                wt, wres = wring.next()
                first = True
                for (src, k0, nk, c0, ncol) in pieces:
                    P.dma("pool", lambda e, wt=wt, src=src, k0=k0, nk=nk, c0=c0, ncol=ncol: e.dma_start(
                        out=wt[:, k0:k0 + nk, c0:c0 + ncol], in_=src), writes=[wres] if first else [], key=wres)
                    first = False
                return wt, wres

            def make_gbc(q, c):
                for q4 in range(4):
                    ps, pres = mmr.next()
                    for j in range(4):
                        kt = q4 * 4 + j
                        tb_, tres = tmpb.next()
                        P.dve(lambda e, tb_=tb_, kt=kt: e.tensor_copy(out=tb_[:], in_=bc(modv[:, q, kt, c:c + 1], 1, 128)),
                              reads=["modv%d" % q], writes=[tres])
                        P.pe(lambda e, ps=ps, tb_=tb_, j=j: e.matmul(ps[:, j * 128:(j + 1) * 128], lhsT=tb_[:], rhs=identf[:], start=True,
                                                                    stop=True), reads=[tres, "identf"], writes=[pres])
                    P.act(lambda e, ps=ps, q4=q4: e.copy(out=Gbc[:, q4 * 512:(q4 + 1) * 512], in_=ps[:]), reads=[pres], writes=["Gbc"])

            def rstd_from(ssap, outap, res_in, res_out):
                P.act(lambda e: e.activation(out=outap, in_=ssap, func=AF.Ln, scale=1.0 / D, bias=epst[:]), reads=[res_in, "epst"],
                      writes=[res_out])
                P.act(lambda e: e.activation(out=outap, in_=outap, func=AF.Exp, scale=-0.5), reads=[res_out], writes=[res_out])

            _cstop = 9
            _ntb = NTB
            for tb in range(_ntb):
                c = 0 if tb < 2 else 1
                tsl = slice(tb * 512, (tb + 1) * 512)
                P.dma("sp", lambda e, tsl=tsl: e.dma_start(out=actA[:, 0:8, :], in_=OHG[:, :, tsl]),
                      writes=[("actA", kt) for kt in range(8)], key="actA")
                P.dma("sp", lambda e, tsl=tsl: e.dma_start(out=actA[:, 8:16, :], in_=OML[:, :, tsl]),
                      writes=[("actA", kt) for kt in range(8, 16)], key="actA")
                for jb in range(4):
                    wt, wres = load_slot([(wuh[:, :, jb * 512:(jb + 1) * 512], 0, 8, 0, 512),
                                          (wum[:, :, jb * 512:(jb + 1) * 512], 8, 8, 0, 512)])
                    for jj in range(4):
                        j = jb * 4 + jj
                        ps1, pr1 = mmr.next()
                        ps2, pr2 = mmr.next()
                        for kt in range(8):
                            P.pe(lambda e, ps1=ps1, wt=wt, kt=kt, jj=jj: e.matmul(ps1[:], lhsT=wt[:, kt, jj * 128:(jj + 1) * 128],
                                                                                rhs=actA[:, kt, :], start=(kt == 0), stop=(kt == 7)),
                                 reads=[wres, ("actA", kt)], writes=[pr1])
                        for kt in range(8, 16):
                            P.pe(lambda e, ps2=ps2, wt=wt, kt=kt, jj=jj: e.matmul(ps2[:], lhsT=wt[:, kt, jj * 128:(jj + 1) * 128],
                                                                                rhs=actA[:, kt, :], start=(kt == 8), stop=(kt == 15)),
                                 reads=[wres, ("actA", kt)], writes=[pr2])
                        sg, sgres = sgr.next()
                        P.dma("sp", lambda e, sg=sg, j=j, tsl=tsl: e.dma_start(out=sg[:, 0, :], in_=SGA[j * 128:(j + 1) * 128, tsl]),
                              writes=[sgres], key=sgres)
                        P.dma("sp", lambda e, sg=sg, j=j, tsl=tsl: e.dma_start(out=sg[:, 1, :], in_=SGB[j * 128:(j + 1) * 128, tsl]),
                              key=sgres)
                        mm, mres = m12.next()
                        P.dve(lambda e, mm=mm, ps1=ps1, sg=sg: e.tensor_tensor(out=mm[:, 0, :], in0=ps1[:], in1=sg[:, 0, :], op=ALU.mult),
                              reads=[pr1, sgres], writes=[(mres, 0)])
                        P.dve(lambda e, mm=mm, ps2=ps2, sg=sg: e.tensor_tensor(out=mm[:, 1, :], in0=ps2[:], in1=sg[:, 1, :], op=ALU.mult),
                              reads=[pr2, sgres], writes=[(mres, 1)])
                        P.dve(lambda e, mm=mm, j=j: e.tensor_tensor(out=big[:, j, :], in0=mm[:, 0, :], in1=mm[:, 1, :], op=ALU.add),
                              reads=[(mres, 0), (mres, 1)], writes=[("big", j)])
                if _cstop < 1.2:
                    continue
                make_gbc(2, c)
                if _cstop < 1.5:
                    continue
                for cb in range(4):
                    wt, wres = load_slot([(wo[:, :, cb * 512:(cb + 1) * 512], 0, 16, 0, 512)])
                    for tt in range(4):
                        ps, pres = mmr.next()
                        for kt in range(KT):
                            P.pe(lambda e, ps=ps, wt=wt, kt=kt, tt=tt: e.matmul(ps[:], lhsT=big[:, kt, tt * 128:(tt + 1) * 128],
                                                                              rhs=wt[:, kt, :], start=(kt == 0), stop=(kt == KT - 1)),
                                 reads=[wres, ("big", kt)], writes=[pres])
                        P.dve(lambda e, ps=ps, tt=tt, cb=cb: e.tensor_copy(out=yblk[:, tt, cb * 512:(cb + 1) * 512], in_=ps[:]),
                              reads=[pres], writes=[("yblk", tt, cb)])
                if _cstop < 1.8:
                    continue
                for tt in range(4):
                    tok0 = tb * 512 + tt * 128
                    yres = [("yblk", tt, cb) for cb in range(4)]
                    P.act(lambda e, tt=tt: e.activation(out=junk2[:], in_=yblk[:, tt, :], func=AF.Square, accum_out=ssv[:, tt, 0:1]),
                          reads=yres, writes=["junk2", ("ssv", tt, 0)])
                    rstd_from(ssv[:, tt, 0:1], ssv[:, tt, 1:2], ("ssv", tt, 0), ("ssv", tt, 1))
                    xt, xres = xr.next()
                    P.dma("sp", lambda e, xt=xt, tok0=tok0: e.dma_start(out=xt[:], in_=x_all[tok0:tok0 + 128, :]), writes=[xres], key=xres)
                    P.dve(lambda e, tt=tt: e.tensor_scalar(out=yblk[:, tt, :], in0=yblk[:, tt, :], scalar1=ssv[:, tt, 1:2], scalar2=None,
                                                           op0=ALU.mult), reads=yres + [("ssv", tt, 1)], writes=yres)
                    P.dve(lambda e, tt=tt: e.tensor_tensor(out=yblk[:, tt, :], in0=yblk[:, tt, :], in1=Gbc[:], op=ALU.mult),
                          reads=yres + ["Gbc"], writes=yres)
                    P.dve(lambda e, xt=xt, tt=tt: e.tensor_tensor(out=xt[:], in0=xt[:], in1=yblk[:, tt, :], op=ALU.add),
                          reads=yres + [xres], writes=[xres])
                    P.dma("sp", lambda e, xt=xt, tok0=tok0: e.dma_start(out=X1[tok0:tok0 + 128, :], in_=xt[:]), reads=[xres],
                          writes=[("X1", tb, tt)], key=xres)
                make_gbc(3, c)
                if _cstop < 1.91:
                    P.flush()
                    continue
                for tt in range(4):
                    tok0 = tb * 512 + tt * 128
                    yres = [("yblk", tt, cb) for cb in range(4)]
                    xt, xres = xr.next()
                    P.dma("sp", lambda e, xt=xt, tok0=tok0: e.dma_start(out=xt[:], in_=X1[tok0:tok0 + 128, :]), writes=[xres], key=xres)
                    P.act(lambda e, xt=xt, tt=tt: e.activation(out=junk2[:], in_=xt[:], func=AF.Square, accum_out=ssv[:, tt, 2:3]),
                          reads=[xres], writes=["junk2", ("ssv", tt, 2)])
                    rstd_from(ssv[:, tt, 2:3], ssv[:, tt, 3:4], ("ssv", tt, 2), ("ssv", tt, 3))
                    P.dve(lambda e, xt=xt, tt=tt: e.tensor_scalar(out=yblk[:, tt, :], in0=xt[:], scalar1=ssv[:, tt, 3:4], scalar2=None,
                                                                  op0=ALU.mult), reads=[xres, ("ssv", tt, 3)], writes=yres)
                    P.dve(lambda e, tt=tt: e.tensor_tensor(out=yblk[:, tt, :], in0=yblk[:, tt, :], in1=Gbc[:], op=ALU.mult),
                          reads=yres + ["Gbc"], writes=yres)
                if _cstop < 1.92:
                    P.flush()
                    continue
                make_gbc(4, c)
                if _cstop < 1.93:
                    P.flush()
                    continue
                for tt in range(4):
                    yres = [("yblk", tt, cb) for cb in range(4)]
                    hb, hbres = hbr.next()
                    P.dve(lambda e, tt=tt, hb=hb: e.tensor_tensor(out=hb[:], in0=yblk[:, tt, :], in1=Gbc[:], op=ALU.add),
                          reads=yres + ["Gbc"], writes=[hbres])
                    if _cstop < 1.94:
                        continue
                    for q4 in range(4):
                        ps, pres = tpcr.next()
                        psb16 = ps[:].bitcast(BF16)
                        for j in range(4):
                            kt = q4 * 4 + j
                            P.pe(lambda e, psb16=psb16, j=j, kt=kt, hb=hb: e.transpose(psb16[:, j * 128:(j + 1) * 128],
                                                                                      hb[:, kt * 128:(kt + 1) * 128], identb[:]),
                                 reads=[hbres, "identb"], writes=[pres])
                        k0 = q4 * 4
                        if _cstop < 1.95:
                            continue
                        P.act(lambda e, psb16=psb16, k0=k0, tt=tt: e.copy(
                            out=h2T[:, k0:k0 + 4, tt * 128:(tt + 1) * 128], in_=psb16[:, 0:512].rearrange("p (j t) -> p j t", j=4)),
                            reads=[pres], writes=[("h2T", k0 + j) for j in range(4)])
                if _cstop < 3:
                    continue
                for j2 in range(22):
                    wt, wres = load_slot([(wfi[:, :, j2 * 256:(j2 + 1) * 256], 0, 16, 0, 256),
                                          (wfi[:, :, DFF + j2 * 256:DFF + (j2 + 1) * 256], 0, 16, 256, 256)])
                    for jj in range(2):
                        j = j2 * 2 + jj
                        psa, pra = mmr.next()
                        psb, prb = mmr.next()
                        for kt in range(KT):
                            P.pe(lambda e, psa=psa, wt=wt, kt=kt, jj=jj: e.matmul(psa[:], lhsT=wt[:, kt, jj * 128:(jj + 1) * 128],
                                                                                rhs=h2T[:, kt, :], start=(kt == 0), stop=(kt == KT - 1)),
                                 reads=[wres, ("h2T", kt)], writes=[pra])
                        for kt in range(KT):
                            P.pe(lambda e, psb=psb, wt=wt, kt=kt, jj=jj: e.matmul(psb[:], lhsT=wt[:, kt, 256 + jj * 128:256 + (jj + 1) * 128],
                                                                                rhs=h2T[:, kt, :], start=(kt == 0), stop=(kt == KT - 1)),
                                 reads=[wres, ("h2T", kt)], writes=[prb])
                        mm, mres = m12.next()
                        P.act(lambda e, mm=mm, psa=psa: e.activation(out=mm[:, 0, :], in_=psa[:], func=AF.Silu), reads=[pra],
                              writes=[(mres, 0)])
                        P.dve(lambda e, mm=mm, psb=psb, j=j: e.tensor_tensor(out=big[:, j, :], in0=mm[:, 0, :], in1=psb[:], op=ALU.mult),
                              reads=[(mres, 0), prb], writes=[("big", j)])
                if _cstop < 4:
                    continue
                make_gbc(5, c)
                for cb in range(4):
                    for (k0, nk) in ((0, 16), (16, 16), (32, 12)):
                        wt, wres = load_slot([(wfo[:, k0:k0 + nk, cb * 512:(cb + 1) * 512], 0, nk, 0, 512)])
                        for tt in range(4):
                            for kk in range(nk):
                                P.pe(lambda e, wt=wt, kk=kk, k0=k0, tt=tt: e.matmul(
                                    big4[:, tt, :], lhsT=big[:, k0 + kk, tt * 128:(tt + 1) * 128], rhs=wt[:, kk, :],
                                    start=(k0 + kk == 0), stop=(k0 + kk == 43)), reads=[wres, ("big", k0 + kk)], writes=[("big4", tt)])
                    for tt in range(4):
                        P.dve(lambda e, tt=tt, cb=cb: e.tensor_copy(out=yblk[:, tt, cb * 512:(cb + 1) * 512], in_=big4[:, tt, :]),
                              reads=[("big4", tt)], writes=[("yblk", tt, cb)])
                for tt in range(4):
                    tok0 = tb * 512 + tt * 128
                    yres = [("yblk", tt, cb) for cb in range(4)]
                    P.act(lambda e, tt=tt: e.activation(out=junk2[:], in_=yblk[:, tt, :], func=AF.Square, accum_out=ssv[:, tt, 0:1]),
                          reads=yres, writes=["junk2", ("ssv", tt, 0)])
                    rstd_from(ssv[:, tt, 0:1], ssv[:, tt, 1:2], ("ssv", tt, 0), ("ssv", tt, 1))
                    xt, xres = xr.next()
                    P.dma("sp", lambda e, xt=xt, tok0=tok0: e.dma_start(out=xt[:], in_=X1[tok0:tok0 + 128, :]), reads=[("X1", tb, tt)],
                          writes=[xres], key=xres)
                    P.dve(lambda e, tt=tt: e.tensor_scalar(out=yblk[:, tt, :], in0=yblk[:, tt, :], scalar1=ssv[:, tt, 1:2], scalar2=None,
                                                           op0=ALU.mult), reads=yres + [("ssv", tt, 1)], writes=yres)
                    P.dve(lambda e, tt=tt: e.tensor_tensor(out=yblk[:, tt, :], in0=yblk[:, tt, :], in1=Gbc[:], op=ALU.mult),
                          reads=yres + ["Gbc"], writes=yres)
                    P.dve(lambda e, xt=xt, tt=tt: e.tensor_tensor(out=xt[:], in0=xt[:], in1=yblk[:, tt, :], op=ALU.add),
                          reads=yres + [xres], writes=[xres])
                    P.dma("sp", lambda e, xt=xt, tok0=tok0: e.dma_start(out=y_all[tok0:tok0 + 128, :], in_=xt[:]), reads=[xres], key=xres)
            P.flush()
        return nc


def shard_inputs(inputs):
    f = lambda a: np.ascontiguousarray(np.asarray(a, dtype=np.float32))
    xp = f(inputs["x_prompt"])
    xs = f(inputs["x_sample"])
    c = f(inputs["c"])
    c_ctx = f(inputs["c_ctx"])
    shared = {
        "w_mod": f(inputs["w_mod"])[0],
        "b_mod": f(inputs["b_mod"])[0].reshape(96, 128),
        "gvec": np.concatenate([f(inputs[k])[0].reshape(16, 128) for k in
                                ("norm_pre_mix", "norm_post_mix", "norm_pre_ffn", "norm_post_ffn")], axis=0),
        "w_in": f(inputs["w_in"])[0],
        "lb_logits": f(inputs["hgrn_lb_logits"]).reshape(2, 2, 8, 128).reshape(32, 128),
        "hg_g": f(inputs["hgrn_norm_g"])[0].reshape(1, 128),
        "b_if": np.concatenate([f(inputs["mlstm_b_i"])[0], f(inputs["mlstm_b_f"])[0]]).reshape(1, 32),
        "ml_g": f(inputs["mlstm_norm_g"])[0].reshape(1, 1024),
        "w_up_hg": f(inputs["w_up_hgrn"])[0],
        "w_up_ml": f(inputs["w_up_mlstm"])[0],
        "w_out": f(inputs["w_out"])[0],
        "w_ffn_in": f(inputs["w_ffn_in"])[0],
        "w_ffn_out": f(inputs["w_ffn_out"])[0],
    }
    maps = []
    for i in range(NCORES):
        m = dict(shared)
        m["x_all"] = np.concatenate([xp[4 * i:4 * i + 4].reshape(1024, D), xs[i]], axis=0)
        m["cond2"] = np.concatenate([c_ctx.reshape(16, 128), c[i].reshape(16, 128)], axis=0)
        m["st_s"] = f(inputs["state_hgrn_s"])[i, 0]
        m["st_c"] = f(inputs["state_mlstm_c"])[i, 0]
        m["st_n"] = f(inputs["state_mlstm_n"])[i, 0]
        m["st_m"] = f(inputs["state_mlstm_m"])[i, 0].reshape(1, 16)
        maps.append(m)
    return maps


def kernel(**inputs):
    maps = shard_inputs(inputs)
    nc = build_nc()
    rs = run_bass_kernel_spmd(nc, maps, core_ids=list(range(NCORES))).results
    rs2 = rs
    y = np.stack([r["y_all"] for r in rs2])
    y_prompt = y[:, :1024].reshape(32, 256, D)
    y_sample = y[:, 1024:]
    new_s = np.concatenate([r["o_s"] for r in rs], axis=0)[:, None]
    new_c = np.concatenate([r["o_c"] for r in rs], axis=0)[:, None]
    new_n = np.concatenate([r["o_n"] for r in rs], axis=0)[:, None]
    new_m = np.concatenate([r["o_m"].reshape(4, 2, 8) for r in rs], axis=0)[:, None]
    return (np.ascontiguousarray(y_prompt), np.ascontiguousarray(y_sample), np.ascontiguousarray(new_s),
            np.ascontiguousarray(new_c), np.ascontiguousarray(new_n), np.ascontiguousarray(new_m))
```
